# Optimizing a Trainium2 kernel written in Bass

```python
import jax, jax.numpy as jnp
from jax import lax
import numpy as np

D_MODEL = 2048
BATCH = 2
SEQ = 4096
DEPTH = 1

MEM_LEN = 256
EPS = 1e-6
NEG_INF = -1e30
D_FF = 5632
MIX_WIDTH = D_MODEL
GLA_WIDTH = MIX_WIDTH // 2
MOBA_WIDTH = MIX_WIDTH - GLA_WIDTH
GLA_HEADS = 4
GLA_DV = GLA_WIDTH // GLA_HEADS
GLA_DK = GLA_DV // 2
GLA_QK = GLA_HEADS * GLA_DK
GLA_GATE_RANK = 16
GLA_GATE_TAU = 16.0
GLA_CHUNK = 64
MOBA_DH = 128
MOBA_HEADS = MOBA_WIDTH // MOBA_DH
MOBA_BLOCK = 256
MOBA_TOPK = 3
MOBA_QCHUNK = 32
XATTN_HEADS = 4
XATTN_DH = 128
XATTN_WIDTH = XATTN_HEADS * XATTN_DH
IN_SPLITS = (GLA_QK, GLA_QK, GLA_WIDTH, GLA_WIDTH, GLA_GATE_RANK, MOBA_WIDTH, MOBA_WIDTH, MOBA_WIDTH)
W_IN_COLS = sum(IN_SPLITS)

kernel_name = "hymba_gla_moba_macaron_layer"


def rms_norm(x, g):
    xf = x.astype(jnp.float32)
    y = xf * lax.rsqrt(jnp.mean(xf * xf, axis=-1, keepdims=True) + EPS)
    return (y * g.astype(jnp.float32)).astype(x.dtype)


def swiglu(h, w_gate, w_up, w_down):
    return (jax.nn.silu(h @ w_gate) * (h @ w_up)) @ w_down


def split_offsets(sizes):
    offs, acc = [], 0
    for s in sizes[:-1]:
        acc += s
        offs.append(acc)
    return offs


def gla_group(q, k, v, r, gate_lr, w_gate2, b_gate2, g_out):
    B, T, H, dk = q.shape
    dv = v.shape[-1]
    C = GLA_CHUNK
    N = T // C
    log_a = jax.nn.log_sigmoid((gate_lr @ w_gate2 + b_gate2).astype(jnp.float32)) / GLA_GATE_TAU
    log_a = log_a.reshape(B, T, H, dk)

    def chunks(a):
        return a.astype(jnp.float32).reshape(B, N, C, H, -1).transpose(0, 3, 1, 2, 4)

    qc = chunks(q) * (dk ** -0.5)
    kc, vc, ac = chunks(k), chunks(v), chunks(log_a)
    bcum = jnp.cumsum(ac, axis=3)
    b_last = bcum[:, :, :, -1:, :]
    q_dec = qc * jnp.exp(bcum)
    k_inv = kc * jnp.exp(-bcum)
    k_tail = kc * jnp.exp(b_last - bcum)
    causal = jnp.tril(jnp.ones((C, C), dtype=bool))
    att = jnp.where(causal, jnp.einsum('bhnid,bhnjd->bhnij', q_dec, k_inv), 0.0)
    o_intra = jnp.einsum('bhnij,bhnjv->bhniv', att, vc)
    upd = jnp.einsum('bhncd,bhncv->bhndv', k_tail, vc)
    decay = jnp.exp(b_last[:, :, :, 0, :])

    def step(S, inp):
        u, dcy = inp
        return dcy[..., None] * S + u, S

    S0 = jnp.zeros((B, H, dk, dv), jnp.float32)
    _, S_prev = lax.scan(step, S0, (jnp.moveaxis(upd, 2, 0), jnp.moveaxis(decay, 2, 0)))
    S_prev = jnp.moveaxis(S_prev, 0, 2)
    o_inter = jnp.einsum('bhncd,bhndv->bhncv', q_dec, S_prev)
    o = (o_intra + o_inter).transpose(0, 2, 3, 1, 4).reshape(B, T, H, dv)
    o = rms_norm(o, g_out) * jax.nn.silu(r.astype(jnp.float32))
    return o.reshape(B, T, H * dv).astype(q.dtype)


def moba_group(q, k, v, g_q, g_k):
    B, T, H, dh = q.shape
    BS, QC = MOBA_BLOCK, MOBA_QCHUNK
    NB = -(-T // BS)
    Tp = NB * BS
    n_sel = min(MOBA_TOPK, NB)
    scale = dh ** -0.5
    qh = rms_norm(q, g_q).transpose(0, 2, 1, 3)
    kh = rms_norm(k, g_k).transpose(0, 2, 1, 3)
    vh = v.transpose(0, 2, 1, 3)
    pad = ((0, 0), (0, 0), (0, Tp - T), (0, 0))
    kb = jnp.pad(kh, pad).reshape(B, H, NB, BS, dh)
    vb = jnp.pad(vh, pad).reshape(B, H, NB, BS, dh)
    k_mean = jnp.mean(kb, axis=3)
    qblk = jnp.arange(T) // BS
    gate = jnp.einsum('bhtd,bhnd->bhtn', qh, k_mean).astype(jnp.float32)
    past = jnp.arange(NB)[None, :] < qblk[:, None]
    gate = jnp.where(past, gate, NEG_INF)
    _, idx = lax.top_k(gate, n_sel)
    valid = idx < qblk[None, None, :, None]

    NQ = T // QC

    def qchunks(a):
        return jnp.moveaxis(a.reshape((B, H, NQ, QC) + a.shape[3:]), 2, 0)

    bi = jnp.arange(B)[:, None, None, None]
    hi = jnp.arange(H)[None, :, None, None]

    def one_chunk(args):
        q_c, idx_c, valid_c, c = args
        k_g = kb[bi, hi, idx_c]
        v_g = vb[bi, hi, idx_c]
        s_past = jnp.einsum('bhqd,bhqjkd->bhqjk', q_c, k_g).astype(jnp.float32) * scale
        s_past = jnp.where(valid_c[..., None], s_past, NEG_INF).reshape(B, H, QC, n_sel * BS)
        blk = (c * QC) // BS
        k_own = lax.dynamic_index_in_dim(kb, blk, axis=2, keepdims=False)
        v_own = lax.dynamic_index_in_dim(vb, blk, axis=2, keepdims=False)
        q_pos = c * QC + jnp.arange(QC)
        k_pos = blk * BS + jnp.arange(BS)
        s_own = jnp.einsum('bhqd,bhkd->bhqk', q_c, k_own).astype(jnp.float32) * scale
        s_own = jnp.where(k_pos[None, :] <= q_pos[:, None], s_own, NEG_INF)
        p = jax.nn.softmax(jnp.concatenate([s_past, s_own], axis=-1), axis=-1).astype(v_g.dtype)
        p_past = p[..., :n_sel * BS].reshape(B, H, QC, n_sel, BS)
        return (jnp.einsum('bhqjk,bhqjkd->bhqd', p_past, v_g)
                + jnp.einsum('bhqk,bhkd->bhqd', p[..., n_sel * BS:], v_own))

    out = lax.map(one_chunk, (qchunks(qh), qchunks(idx), qchunks(valid), jnp.arange(NQ)))
    out = jnp.moveaxis(out, 0, 2).reshape(B, H, T, dh)
    return out.transpose(0, 2, 1, 3).reshape(B, T, H * dh)


def memory_cross_attn(h, mem_n, w_q, w_kv, w_o, g_q, g_k):
    B, T, _ = h.shape
    M = mem_n.shape[1]
    q = rms_norm((h @ w_q).reshape(B, T, XATTN_HEADS, XATTN_DH), g_q)
    kv = (mem_n @ w_kv).reshape(B, M, 2, XATTN_HEADS, XATTN_DH)
    k = rms_norm(kv[:, :, 0], g_k)
    v = kv[:, :, 1]
    s = jnp.einsum('bthd,bmhd->bhtm', q, k).astype(jnp.float32) * (XATTN_DH ** -0.5)
    p = jax.nn.softmax(s, axis=-1).astype(v.dtype)
    o = jnp.einsum('bhtm,bmhd->bthd', p, v).reshape(B, T, XATTN_WIDTH)
    return o @ w_o


def setup_inputs(seed: int = 0) -> dict:
    key = jax.random.key(seed)
    ks = jax.random.split(key, 32)
    L = DEPTH

    def w(k, shape, fan_in):
        return jax.random.normal(k, shape, jnp.float32) * (fan_in ** -0.5)

    def gain(k, shape):
        return 1.0 + 0.05 * jax.random.normal(k, shape, jnp.float32)

    return {
        "x": jax.random.normal(ks[0], (BATCH, SEQ, D_MODEL), jnp.float32),
        "mem": jax.random.normal(ks[1], (BATCH, MEM_LEN, D_MODEL), jnp.float32),
        "ffn1_norm": gain(ks[2], (L, D_MODEL)),
        "ffn1_w_gate": w(ks[3], (L, D_MODEL, D_FF), D_MODEL),
        "ffn1_w_up": w(ks[4], (L, D_MODEL, D_FF), D_MODEL),
        "ffn1_w_down": w(ks[5], (L, D_FF, D_MODEL), D_FF),
        "mix_norm": gain(ks[6], (L, D_MODEL)),
        "w_in": w(ks[7], (L, D_MODEL, W_IN_COLS), D_MODEL),
        "gla_w_gate2": w(ks[8], (L, GLA_GATE_RANK, GLA_QK), GLA_GATE_RANK),
        "gla_b_gate2": 0.1 * jax.random.normal(ks[9], (L, GLA_QK), jnp.float32),
        "gla_out_norm": gain(ks[10], (L, GLA_DV)),
        "moba_q_norm": gain(ks[11], (L, MOBA_DH)),
        "moba_k_norm": gain(ks[12], (L, MOBA_DH)),
        "w_out": w(ks[13], (L, MIX_WIDTH, D_MODEL), MIX_WIDTH),
        "xattn_norm": gain(ks[14], (L, D_MODEL)),
        "mem_norm": gain(ks[15], (L, D_MODEL)),
        "xattn_w_q": w(ks[16], (L, D_MODEL, XATTN_WIDTH), D_MODEL),
        "xattn_w_kv": w(ks[17], (L, D_MODEL, 2 * XATTN_WIDTH), D_MODEL),
        "xattn_w_o": w(ks[18], (L, XATTN_WIDTH, D_MODEL), XATTN_WIDTH),
        "xattn_q_norm": gain(ks[19], (L, XATTN_DH)),
        "xattn_k_norm": gain(ks[20], (L, XATTN_DH)),
        "ffn2_norm": gain(ks[21], (L, D_MODEL)),
        "ffn2_w_gate": w(ks[22], (L, D_MODEL, D_FF), D_MODEL),
        "ffn2_w_up": w(ks[23], (L, D_MODEL, D_FF), D_MODEL),
        "ffn2_w_down": w(ks[24], (L, D_FF, D_MODEL), D_FF),
    }


def reference(x, mem, ffn1_norm, ffn1_w_gate, ffn1_w_up, ffn1_w_down, mix_norm, w_in,
              gla_w_gate2, gla_b_gate2, gla_out_norm, moba_q_norm, moba_k_norm, w_out,
              xattn_norm, mem_norm, xattn_w_q, xattn_w_kv, xattn_w_o, xattn_q_norm,
              xattn_k_norm, ffn2_norm, ffn2_w_gate, ffn2_w_up, ffn2_w_down):
    B, T, _ = x.shape
    offs = split_offsets(IN_SPLITS)
    for l in range(DEPTH):
        x = x + 0.5 * swiglu(rms_norm(x, ffn1_norm[l]), ffn1_w_gate[l], ffn1_w_up[l], ffn1_w_down[l])
        h = rms_norm(x, mix_norm[l])
        u = h @ w_in[l]
        g_q, g_k, g_v, g_r, g_lr, m_q, m_k, m_v = jnp.split(u, offs, axis=-1)
        o_gla = gla_group(g_q.reshape(B, T, GLA_HEADS, GLA_DK), g_k.reshape(B, T, GLA_HEADS, GLA_DK),
                          g_v.reshape(B, T, GLA_HEADS, GLA_DV), g_r.reshape(B, T, GLA_HEADS, GLA_DV),
                          g_lr, gla_w_gate2[l], gla_b_gate2[l], gla_out_norm[l])
        o_moba = moba_group(m_q.reshape(B, T, MOBA_HEADS, MOBA_DH), m_k.reshape(B, T, MOBA_HEADS, MOBA_DH),
                            m_v.reshape(B, T, MOBA_HEADS, MOBA_DH), moba_q_norm[l], moba_k_norm[l])
        x = x + jnp.concatenate([o_gla, o_moba], axis=-1) @ w_out[l]
        x = x + memory_cross_attn(rms_norm(x, xattn_norm[l]), rms_norm(mem, mem_norm[l]),
                                  xattn_w_q[l], xattn_w_kv[l], xattn_w_o[l],
                                  xattn_q_norm[l], xattn_k_norm[l])
        x = x + 0.5 * swiglu(rms_norm(x, ffn2_norm[l]), ffn2_w_gate[l], ffn2_w_up[l], ffn2_w_down[l])
    return x
```

```python
import math
import numpy as np
import ml_dtypes
import concourse.bass as bass
import concourse.mybir as mybir
from concourse.bass_utils import run_bass_kernel_spmd

F32 = mybir.dt.float32
BF16 = mybir.dt.bfloat16
AF = mybir.ActivationFunctionType
ALU = mybir.AluOpType
AX = mybir.AxisListType

D = 2048
DFF = 5632
T = 1024
NC_ = 16
EPS = 1e-6
WIN = 6160
O_GQ, O_GK, O_GV, O_GR, O_GLR, O_MQ, O_MK, O_MV = 0, 512, 1024, 2048, 3072, 3088, 4112, 5136
FKV = 8192 + 8 * 8 * 130
FST = 1064
STOP_AFTER = 99
SKIP_FFN = False


class _Stop(Exception):
    pass


class Res:
    __slots__ = ("name", "w", "r", "ws")

    def __init__(self, name):
        self.name = name
        self.w = None
        self.r = []
        self.ws = []


class DSem:
    def __init__(self, sem, inc=16):
        self.sem = sem
        self.n = 0
        self.last = None
        self.inc = inc


class Op:
    pass


class Buf:
    def __init__(self, t, name):
        self.t = t
        self.res = Res(name)


class Prog:
    ENG = ("pe", "act", "dve", "pool", "sp")

    def __init__(self, nc):
        self.nc = nc
        self.ops = []
        self.dsems = []

    def dsem(self, name, inc=16):
        d = DSem(self.nc.alloc_semaphore(name), inc)
        self.dsems.append(d)
        return d

    def op(self, eng, fn, r=(), w=(), dsem=None, ndma=0, ww=()):
        o = Op()
        o.eng, o.fn, o.r, o.w, o.dsem, o.ndma = eng, fn, list(r), list(w), dsem, ndma
        o.ww = list(ww)
        o.deps = []
        o.signal = False
        o.bar = False
        self.ops.append(o)
        return o

    def barrier(self, exclude=()):
        o = Op()
        o.bar = True
        o.exclude = [id(d) for d in exclude]
        self.ops.append(o)

    def emit(self, block):
        nc = self.nc
        last = {}
        pend = {e: None for e in self.ENG}
        for o in self.ops:
            if o.bar:
                deps = [x for x in last.values()]
                deps += [d.last for d in self.dsems if d.last is not None and id(d) not in o.exclude]
                for e in self.ENG:
                    pend[e] = list(deps) if pend[e] is None else pend[e] + deps
                continue
            deps = []
            for r in o.r:
                if r.w is not None:
                    deps.append(r.w)
                deps.extend(r.ws)
            for w in o.w:
                if w.w is not None:
                    deps.append(w.w)
                deps.extend(w.r)
                deps.extend(w.ws)
            for w in o.ww:
                if w.w is not None:
                    deps.append(w.w)
                deps.extend(w.r)
            if pend[o.eng] is not None:
                deps.extend(pend[o.eng])
                pend[o.eng] = None
            seen = set()
            for d in deps:
                if d is o or id(d) in seen:
                    continue
                seen.add(id(d))
                if d.dsem is None and d.eng == "pe" and o.eng == "pe":
                    continue
                o.deps.append(d)
                d.signal = True
            for r in o.r:
                r.r.append(o)
            for w in o.w:
                w.w = o
                w.r = []
                w.ws = []
            for w in o.ww:
                w.ws.append(o)
            if o.dsem is not None:
                o.dsem.last = o
            else:
                last[o.eng] = o
        cnt = {e: 0 for e in self.ENG}
        esem = {e: nc.alloc_semaphore("es_" + e) for e in self.ENG}
        for o in self.ops:
            if o.bar:
                continue
            if o.dsem is not None:
                o.dsem.n += o.ndma
                o.ticket = o.dsem.inc * o.dsem.n
                o.sem = o.dsem.sem
            elif o.signal:
                cnt[o.eng] += 1
                o.ticket = cnt[o.eng]
                o.sem = esem[o.eng]
        by = {e: [o for o in self.ops if not o.bar and o.eng == e] for e in self.ENG}

        def run(name, e):
            waited = {}
            for o in by[name]:
                for d in o.deps:
                    k = id(d.sem)
                    if waited.get(k, 0) < d.ticket:
                        e.wait_ge(d.sem, d.ticket)
                        waited[k] = d.ticket
                ins = o.fn(e)
                if o.dsem is None and o.signal and ins is not None:
                    ins.then_inc(esem[name], 1)

        @block.tensor
        def _(e):
            run("pe", e)

        @block.scalar
        def _(e):
            run("act", e)

        @block.vector
        def _(e):
            run("dve", e)

        @block.gpsimd
        def _(e):
            run("pool", e)

        @block.sync
        def _(e):
            run("sp", e)


def build(stop_after=99):
    nc = bass.Bass("TRN2", target_bir_lowering=False)
    P = Prog(nc)

    def din(name, shape, dt=F32):
        return nc.dram_tensor(name, list(shape), dt, kind="ExternalInput").ap()

    x_d = din("x", [T, D])
    mem_d = din("mem", [256, D])
    y_d = nc.dram_tensor("y", [T, D], F32, kind="ExternalOutput").ap()
    WS = []

    class WT:
        pass

    def wshard(name, rows, cols):
        w = WT()
        if SKIP_FFN and name[0] == "f":
            rows, cols = 128, 128
        w.ap = din(name, [rows, cols])
        w.res = Res(name)
        w.sres = Res(name + "_s")
        WS.append(w)
        return w
    w_ffn = [[wshard("f1g", D, DFF), wshard("f1u", D, DFF), wshard("f1d", DFF, D)]]
    win_d = wshard("w_in", D, WIN)
    wout_d = wshard("w_out", D, D)
    wkv_d = wshard("xw_kv", D, 1024)
    wq_d = wshard("xw_q", D, 512)
    wo_d = wshard("xw_o", 512, D)
    w_ffn.append([wshard("f2g", D, DFF), wshard("f2u", D, DFF), wshard("f2d", DFF, D)])
    gains_d = din("gains", [128, 80])
    sg_d = din("sgains", [128, 4])
    gout_d = din("gout_bc", [128, 256])
    wg2_d = din("wg2", [16, 512])
    bg2_d = din("bg2", [1, 512])
    ident_d = din("ident", [128, 128])
    triN_d = din("triN", [128, 128])
    triS_d = din("triS", [128, 128])
    attm_d = din("attmask", [128, 128])
    triT_d = din("triT", [128, 128])
    pastb_d = din("pastbias", [128, 1024])
    past01_d = din("past01", [128, 1024])
    mj_d = din("mj", [128, 3])

    ksrc = [nc.dram_tensor("ksrc%d" % i, [128, 4096], BF16) for i in range(2)]
    kdst = [nc.dram_tensor("kdst%d" % i, [512, 4096], BF16) for i in range(2)]
    vsrc = [nc.dram_tensor("vsrc%d" % i, [128, 2080], BF16) for i in range(4)]
    vdst = [nc.dram_tensor("vdst%d" % i, [512, 2080], BF16) for i in range(4)]
    st_src = nc.dram_tensor("st_src", [128, FST], F32)
    st_dst = nc.dram_tensor("st_dst", [512, FST], F32)
    qn_dram = nc.dram_tensor("qn_dram", [128, 8192], BF16)

    def sb(name, shape, dt, off):
        t = nc.alloc_sbuf_tensor_at(name, list(shape), dt, offset=off)
        return Buf(t, name)

    def nbytes(shape, dt):
        n = 1
        for s in shape[1:]:
            n *= s
        return n * (2 if dt == BF16 else 4)

    class Arena:
        def __init__(self, off, size):
            self.off, self.end, self.base = off, off + size, off

        def alloc(self, name, shape, dt):
            b = sb(name, shape, dt, self.off)
            b.off = self.off
            self.off += (nbytes(shape, dt) + 31) // 32 * 32
            assert self.off <= self.end, (name, self.off, self.end)
            return b

        def reset(self):
            self.off = self.base

    BASE = 16512
    XT_OFF, HT_OFF, R_OFF, S_OFF = BASE, BASE + 65536, BASE + 98304, BASE + 196608
    xT = sb("xT", [128, NC_, T], F32, XT_OFF)
    xT_res = [[Res("xT%d_%d" % (c, h)) for h in range(2)] for c in range(NC_)]
    hT = sb("hT", [128, NC_, T], BF16, HT_OFF)
    hT_res = [[Res("hT%d_%d" % (c, h)) for h in range(2)] for c in range(NC_)]
    SA = Arena(S_OFF, 229344 - S_OFF)
    ident = SA.alloc("ident", [128, 128], F32)
    ones = SA.alloc("ones", [128, 128], F32)
    identb = SA.alloc("identb", [128, 128], BF16)
    onesb = SA.alloc("onesb", [128, 128], BF16)
    gains = SA.alloc("gains", [128, 80], F32)
    sgains = SA.alloc("sgains", [128, 4], F32)
    sqb = [SA.alloc("sq%d" % i, [128, 512], F32) for i in range(2)]
    rstd = SA.alloc("rstd", [128, 512], F32)
    rstds = [rstd, SA.alloc("rstd_b", [128, 512], F32)]
    sgb = [SA.alloc("sg%d" % i, [128, 512], F32) for i in range(2)]
    small = SA.alloc("small", [128, 64], F32)
    RA = Arena(R_OFF, S_OFF - R_OFF)

    pb = []
    for i in range(7):
        pb.append(Buf(nc.alloc_psum_tensor("pb%d" % i, [128, 512], F32), "pb%d" % i))
    pbf = Buf(nc.alloc_psum_tensor("pbf", [128, 1024], BF16), "pbf")

    cds = P.dsem("const")
    HS = [slice(0, 512), slice(512, 1024)]

    def dma(eng, out, in_, ds):
        return eng.dma_start(out=out, in_=in_).then_inc(ds.sem, 16)

    def ld_consts(e):
        dma(e, ident.t[:], ident_d, cds)
        dma(e, gains.t[:], gains_d, cds)
        return dma(e, sgains.t[:], sg_d, cds)
    P.op("sp", ld_consts, w=[ident.res, gains.res, sgains.res], dsem=cds, ndma=3)
    cdsb = P.dsem("constb")
    P.op("pool", lambda e: dma(e, identb.t[:], ident_d, cdsb), w=[identb.res], dsem=cdsb, ndma=1)
    P.op("dve", lambda e: e.memset(ones.t[:], 1.0), w=[ones.res])
    P.op("dve", lambda e: e.memset(onesb.t[:], 1.0), w=[onesb.res])

    def transpose_in(src_d, ntile, dst, dst_res_fn, stg, tcols):
        k = 0
        for i in range(ntile):
            st = stg[i % 2]
            P.op("sp", lambda e, i=i, st=st: dma(e, st.t[:], src_d[i * 128:(i + 1) * 128, :], st.ds),
                 w=[st.res], dsem=st.ds, ndma=1)
            for g4 in range(4):
                ps = pb[k % 4]
                k += 1

                def tr(e, st=st, ps=ps, g4=g4):
                    for q in range(4):
                        c = g4 * 4 + q
                        ins = e.transpose(out=ps.t[:, q * 128:(q + 1) * 128], in_=st.t[:, c * 128:(c + 1) * 128],
                                          identity=ident.t[:])
                    return ins
                P.op("pe", tr, r=[st.res, ident.res], w=[ps.res])
                dst_ap = dst.t[:, g4 * 4:(g4 + 1) * 4, i * 128:(i + 1) * 128]
                src_ap = ps.t[:].rearrange("p (q t) -> p q t", q=4)
                wres = dst_res_fn(g4, i)
                if k % 2 == 0:
                    P.op("dve", lambda e, a=dst_ap, b=src_ap: e.tensor_copy(out=a, in_=b), r=[ps.res], w=wres)
                else:
                    P.op("act", lambda e, a=dst_ap, b=src_ap: e.copy(out=a, in_=b), r=[ps.res], w=wres)

    def stat_rstd(ps, dim):
        n = 512
        P.op("act", lambda e: e.activation(out=rstd.t[:, :n], in_=ps.t[:, :n], func=AF.Ln, bias=EPS,
                                           scale=1.0 / dim), r=[ps.res], w=[rstd.res])
        P.op("act", lambda e: e.activation(out=rstd.t[:, :n], in_=rstd.t[:, :n], func=AF.Exp, scale=-0.5),
             r=[rstd.res], w=[rstd.res])

    sqh = [sb("sqh%d" % i, [128, 512], BF16, sqb[i // 2].off + (i % 2) * 1024) for i in range(4)]

    def rms_norm_T(gcol, ncols=T):
        for h in range(2):
            ps = pb[6 - h]
            for c in range(NC_):
                sq = sqh[c % 4]
                if c % 2 == 0:
                    P.op("dve", lambda e, sq=sq, c=c, h=h: e.tensor_tensor(out=sq.t[:], in0=xT.t[:, c, HS[h]],
                                                                          in1=xT.t[:, c, HS[h]], op=ALU.mult),
                         r=[xT_res[c][h]], w=[sq.res])
                else:
                    P.op("act", lambda e, sq=sq, c=c, h=h: e.activation(out=sq.t[:], in_=xT.t[:, c, HS[h]],
                                                                       func=AF.Square),
                         r=[xT_res[c][h]], w=[sq.res])
                P.op("pe", lambda e, sq=sq, c=c, ps=ps: e.matmul(ps.t[:], onesb.t[:], sq.t[:], start=(c == 0),
                                                                stop=(c == NC_ - 1)),
                     r=[sq.res, onesb.res], w=[ps.res])
        for h in range(2):
            ps, rs = pb[6 - h], rstds[h]
            P.op("act", lambda e, ps=ps, rs=rs: e.activation(out=rs.t[:], in_=ps.t[:], func=AF.Ln, bias=EPS,
                                                             scale=1.0 / D), r=[ps.res], w=[rs.res])
            P.op("act", lambda e, rs=rs: e.activation(out=rs.t[:], in_=rs.t[:], func=AF.Exp, scale=-0.5),
                 r=[rs.res], w=[rs.res])
        for h in range(2):
            rs = rstds[h]
            for c in range(NC_):
                P.op("dve", lambda e, c=c, h=h, rs=rs: e.scalar_tensor_tensor(
                    out=hT.t[:, c, HS[h]], in0=xT.t[:, c, HS[h]], scalar=gains.t[:, gcol + c:gcol + c + 1],
                    in1=rs.t[:], op0=ALU.mult, op1=ALU.mult),
                    r=[xT_res[c][h], rs.res, gains.res], w=[hT_res[c][h]])

    hn_k = [0]

    def head_norm(src_ap, src_res, gidx, out_ap, out_res, n=512, extra_r=()):
        hn_k[0] += 1
        sq = sqb[hn_k[0] % 2]
        ps = pb[5 + hn_k[0] % 2]
        rstd = rstds[hn_k[0] % 2]
        P.op("dve", lambda e: e.tensor_tensor(out=sq.t[:, :n], in0=src_ap, in1=src_ap, op=ALU.mult),
             r=[src_res], w=[sq.res])
        P.op("pe", lambda e: e.matmul(ps.t[:, :n], ones.t[:], sq.t[:, :n], start=True, stop=True),
             r=[sq.res, ones.res], w=[ps.res])
        P.op("act", lambda e: e.activation(out=rstd.t[:, :n], in_=ps.t[:, :n], func=AF.Ln, bias=EPS,
                                           scale=1.0 / 128), r=[ps.res], w=[rstd.res])
        P.op("act", lambda e: e.activation(out=rstd.t[:, :n], in_=rstd.t[:, :n], func=AF.Exp, scale=-0.5),
             r=[rstd.res], w=[rstd.res])
        P.op("dve", lambda e: e.scalar_tensor_tensor(out=out_ap, in0=src_ap, scalar=sgains.t[:, gidx:gidx + 1],
                                                     in1=rstd.t[:, :n], op0=ALU.mult, op1=ALU.mult),
             r=[src_res, rstd.res, sgains.res] + list(extra_r), ww=[out_res])

    def fill_cols(slot, W, c0, n, dst0=0, parts=4, nchunk=NC_):
        step = nchunk // parts

        def f(e):
            for k in range(parts):
                ins = dma(e, slot.t[:, k * step:(k + 1) * step, dst0:dst0 + n],
                          W.ap[k * step * 128:(k + 1) * step * 128, c0:c0 + n].rearrange("(c p) n -> p c n", p=128),
                          slot.ds)
            return ins
        P.op("pool", f, r=[W.res], w=[slot.res], dsem=slot.ds, ndma=parts)

    stg = [RA.alloc("stg%d" % i, [128, D], F32) for i in range(2)]
    for s in stg:
        s.ds = P.dsem(s.res.name)
    transpose_in(x_d, 8, xT, lambda g4, i: [xT_res[c][i // 4] for c in range(g4 * 4, g4 * 4 + 4)], stg, T)
    P.barrier()
    RA.reset()

    def ffn(wg, wu, wd, gcol, tag, tail=None):
        RA.reset()
        ring = [RA.alloc("ring%s%d" % (tag, i), [128, 8192], BF16) for i in range(5)]
        for s in ring:
            s.ds = P.dsem(s.res.name)
        act = [RA.alloc("act%s%d" % (tag, i), [128, 4, T], BF16) for i in range(2)]
        act_res = [[[Res("act%d_%d_%d" % (b, j, h)) for h in range(2)] for j in range(4)] for b in range(2)]
        rms_norm_T(gcol)
        NG = DFF // 512
        fills = []
        for g in range(NG):
            fills.append(("g", g))
            fills.append(("u", g))
            if g >= 1:
                fills.append(("d", g - 1))
        fills.append(("d", NG - 1))
        slot_of = {}
        issued = [0]

        def issue_upto(n):
            while issued[0] < min(n, len(fills)):
                kind, g = fills[issued[0]]
                slot = ring[issued[0] % 5]
                slot_of[(kind, g)] = slot
                if kind == "d":
                    def f(e, slot=slot, g=g):
                        for k in range(4):
                            ins = dma(e, slot.t[:, k * 2048:(k + 1) * 2048],
                                      wd.ap[(g * 4 + k) * 128:(g * 4 + k + 1) * 128, :], slot.ds)
                        return ins
                    P.op("pool", f, r=[wd.res], w=[slot.res], dsem=slot.ds, ndma=4)
                else:
                    W = wg if kind == "g" else wu

                    def f(e, slot=slot, g=g, W=W):
                        for k in range(4):
                            ins = dma(e, slot.t[:, k * 2048:(k + 1) * 2048].rearrange("p (c n) -> p c n", c=4),
                                      W.ap[k * 512:(k + 1) * 512, g * 512:(g + 1) * 512].rearrange(
                                          "(c p) n -> p c n", p=128), slot.ds)
                        return ins
                    P.op("pool", f, r=[W.res], w=[slot.res], dsem=slot.ds, ndma=4)
                issued[0] += 1
        gu_bank = [pb[0], pb[1], pb[2], pb[3]]
        dn_bank = [pb[4], pb[5]]
        kk = [0, 0]

        def gate_up(g):
            G = slot_of[("g", g)]
            U = slot_of[("u", g)]
            ab = g % 2
            for j in range(4):
                for h in range(2):
                    pg = gu_bank[(kk[0] % 2) * 2]
                    pu = gu_bank[(kk[0] % 2) * 2 + 1]
                    kk[0] += 1
                    sg = sgb[kk[0] % 2]

                    def mm(e, S, ps, j=j, h=h):
                        for c in range(NC_):
                            ins = e.matmul(ps.t[:], S.t[:, c * 512 + j * 128:c * 512 + (j + 1) * 128],
                                           hT.t[:, c, HS[h]], start=(c == 0), stop=(c == NC_ - 1))
                        return ins
                    hr = [hT_res[c][h] for c in range(NC_)]
                    P.op("pe", lambda e, G=G, pg=pg, mm=mm: mm(e, G, pg), r=hr + [G.res], w=[pg.res])
                    P.op("pe", lambda e, U=U, pu=pu, mm=mm: mm(e, U, pu), r=hr + [U.res], w=[pu.res])
                    P.op("act", lambda e, sg=sg, pg=pg: e.activation(out=sg.t[:], in_=pg.t[:], func=AF.Silu),
                         r=[pg.res], w=[sg.res])
                    P.op("dve", lambda e, sg=sg, pu=pu, ab=ab, j=j, h=h: e.tensor_tensor(
                        out=act[ab].t[:, j, HS[h]], in0=sg.t[:], in1=pu.t[:], op=ALU.mult),
                        r=[sg.res, pu.res], w=[act_res[ab][j][h]])

        def down(g, half_major=False):
            Dn = slot_of[("d", g)]
            ab = g % 2
            order = ([(dc, h) for h in range(2) for dc in range(NC_)] if half_major
                     else [(dc, h) for dc in range(NC_) for h in range(2)])
            for n_, (dc, h) in enumerate(order):
                if half_major and n_ == NC_:
                    tail(0, ring)
                if True:
                    ps = dn_bank[kk[1] % 2]
                    kk[1] += 1

                    def mm(e, ps=ps, dc=dc, h=h, Dn=Dn, ab=ab):
                        for j in range(4):
                            ins = e.matmul(ps.t[:], Dn.t[:, j * 2048 + dc * 128:j * 2048 + (dc + 1) * 128],
                                           act[ab].t[:, j, HS[h]], start=(j == 0), stop=(j == 3))
                        return ins
                    P.op("pe", mm, r=[Dn.res] + [act_res[ab][j][h] for j in range(4)], w=[ps.res])
                    P.op("dve", lambda e, ps=ps, dc=dc, h=h: e.scalar_tensor_tensor(
                        out=xT.t[:, dc, HS[h]], in0=ps.t[:], scalar=0.5, in1=xT.t[:, dc, HS[h]],
                        op0=ALU.mult, op1=ALU.add), r=[ps.res, xT_res[dc][h]], w=[xT_res[dc][h]])
        nf = 0
        issue_upto(5)
        for g in range(NG):
            gate_up(g)
            nf += 2
            issue_upto(nf + 5)
            if g >= 1:
                down(g - 1)
                nf += 1
                issue_upto(nf + 5)
        if tail is None:
            down(NG - 1)
            P.barrier()
        else:
            down(NG - 1, half_major=True)
            tail(1, ring)

    if stop_after >= 1 and not SKIP_FFN:
        ffn(*w_ffn[0], 0, "a")

    def chk(v):
        if stop_after < v:
            raise _Stop()

    def mixer():
        RA.reset()
        wsl = [RA.alloc("wsl%d" % i, [128, NC_, 256], BF16) for i in range(3)]
        for s in wsl:
            s.ds = P.dsem(s.res.name)
        stash = []
        for h in range(4):
            stash.append(dict(
                qd=RA.alloc("qd%d" % h, [128, T], BF16), ki=RA.alloc("ki%d" % h, [128, T], BF16),
                kt=RA.alloc("kt%d" % h, [128, 8, 128], BF16), v=RA.alloc("v%d" % h, [128, 8, 256], BF16),
                sr=RA.alloc("sr%d" % h, [128, 8, 256], BF16)))
        TA = Arena(RA.off, RA.end - RA.off)
        rms_norm_T(16)
        wi = [0]

        def next_slot():
            s = wsl[wi[0] % 3]
            wi[0] += 1
            return s
        hr = [[hT_res[c][h] for c in range(NC_)] for h in range(2)]
        hr_all = hr[0] + hr[1]

        def projT(slot, col, ps, h, n=128):
            def mm(e):
                for c in range(NC_):
                    ins = e.matmul(ps.t[:n, :], slot.t[:, c, col:col + n], hT.t[:, c, HS[h]],
                                   start=(c == 0), stop=(c == NC_ - 1))
                return ins
            P.op("pe", mm, r=hr[h] + [slot.res], w=[ps.res])

        def projTok(slot, col, n, ps, pcol, i):
            def mm(e):
                for c in range(NC_):
                    ins = e.matmul(ps.t[:, pcol:pcol + n], hT.t[:, c, i * 128:(i + 1) * 128],
                                   slot.t[:, c, col:col + n], start=(c == 0), stop=(c == NC_ - 1))
                return ins
            P.op("pe", mm, r=hr[i // 4] + [slot.res], w=[ps.res])

        ststage = TA.alloc("ststage", [128, FST], F32)
        TA_mark = TA.off
        raws = [TA.alloc("raw%d" % i, [128, 512], F32) for i in range(2)]
        knfs = [TA.alloc("knf%d" % i, [128, 512], F32) for i in range(2)]
        knb = [TA.alloc("knb%d" % i, [128, 512], BF16) for i in range(2)]
        vst = [TA.alloc("vst%d" % i, [128, 2, 130], BF16) for i in range(2)]
        kmown = SA.alloc("kmown", [128, 8, 4], F32)
        for b in knb + vst:
            b.ds = P.dsem(b.res.name)
        for b in vst:
            P.op("dve", lambda e, b=b: e.memset(b.t[:, :, 128:129], 1.0), w=[b.res])
            P.op("dve", lambda e, b=b: e.memset(b.t[:, :, 129:130], 0.0), w=[b.res])
        kvs = Res("kv_src")
        sts = Res("st_src")
        qnr = Res("qn_dram")
        kqc = [0]

        def m1_qk(which, ocol, gidx):
            its_ = [(hp, hh, h) for hp in range(4) for hh in range(2) for h in range(2)]
            slots, stt = {}, {}

            def stage_a(n):
                hp, hh, h = its_[n]
                if hp not in slots:
                    slots[hp] = next_slot()
                    fill_cols(slots[hp], win_d, ocol + hp * 256, 256)
                kq = kqc[0]
                kqc[0] += 1
                ps, raw = pb[kq % 2], raws[kq % 2]
                projT(slots[hp], hh * 128, ps, h)
                P.op("act", lambda e: e.copy(out=raw.t[:], in_=ps.t[:]), r=[ps.res], w=[raw.res])
                stt[n] = (raw, knfs[kq % 2], knb[kq % 2])

            def stage_b(n):
                hp, hh, h = its_[n]
                hd = hp * 2 + hh
                raw, knf, ob = stt[n]
                if which == "q":
                    head_norm(raw.t[:], raw.res, gidx, ob.t[:], ob.res)
                    P.op("sp", lambda e: dma(
                        e, qn_dram.ap()[:, hd * 1024 + h * 512:hd * 1024 + (h + 1) * 512], ob.t[:], ob.ds),
                        r=[ob.res], ww=[qnr], dsem=ob.ds, ndma=1)
                else:
                    head_norm(raw.t[:], raw.res, gidx, knf.t[:], knf.res)
                    P.op("dve", lambda e: e.tensor_reduce(
                        out=kmown.t[:, hd, h * 2:h * 2 + 2], in_=knf.t[:].rearrange("p (b k) -> p b k", b=2),
                        axis=AX.X, op=ALU.add), r=[knf.res], ww=[kmown.res])
                    P.op("act", lambda e: e.copy(out=ob.t[:], in_=knf.t[:]), r=[knf.res], w=[ob.res])
                    P.op("sp", lambda e: dma(
                        e, ksrc[hd // 4].ap()[:, (hd % 4) * 1024 + h * 512:(hd % 4) * 1024 + (h + 1) * 512],
                        ob.t[:], ob.ds), r=[ob.res], ww=[kvs], dsem=ob.ds, ndma=1)
            stage_a(0)
            for n in range(len(its_)):
                if n + 1 < len(its_):
                    stage_a(n + 1)
                stage_b(n)
        m1_qk("k", O_MK, 1)
        kq = 0
        for hp in range(4):
            slot = next_slot()
            fill_cols(slot, win_d, O_MV + hp * 256, 256)
            for i in range(8):
                ps = pb[kq % 2]
                kq += 1
                projTok(slot, 0, 256, ps, 0, i)
                vb = vst[kq % 2]
                P.op("act", lambda e, ps=ps, vb=vb: e.copy(out=vb.t[:, :, 0:128],
                                                            in_=ps.t[:, 0:256].rearrange("p (a c) -> p a c", a=2)),
                     r=[ps.res], w=[vb.res])
                P.op("sp", lambda e, vb=vb, hp=hp, i=i: dma(
                    e, vsrc[hp].ap().rearrange("p (h i c) -> p h i c", h=2, i=8)[:, :, i, :],
                    vb.t[:], vb.ds), r=[vb.res], ww=[kvs], dsem=vb.ds, ndma=1)
        P.op("dve", lambda e: e.memset(ststage.t[:, 1024:FST], 0.0), w=[ststage.res])
        P.op("dve", lambda e: e.tensor_scalar(out=kmown.t[:], in0=kmown.t[:], scalar1=1.0 / 256, scalar2=None,
                                              op0=ALU.mult), r=[kmown.res], w=[kmown.res])
        P.op("dve", lambda e: e.tensor_copy(out=ststage.t[:, 1028:1060].rearrange("p (h b) -> p h b", h=8),
                                            in_=kmown.t[:]), r=[kmown.res], w=[ststage.res])
        pre_glr = next_slot()
        fill_cols(pre_glr, win_d, O_GLR, 16)
        pre_sr0 = next_slot()
        fill_cols(pre_sr0, win_d, O_GR, 256)
        P.barrier()
        kvd = Res("kv_dst")
        RG = [[0, 1, 2, 3], [4, 5, 6, 7]]
        for src_, dst_ in list(zip(ksrc, kdst)) + list(zip(vsrc, vdst)):
            cs_ = P.dsem("cc_" + src_.name, inc=1)
            P.op("pool", lambda e, src_=src_, dst_=dst_, cs_=cs_: e.collective_compute(
                "AllGather", ALU.bypass, replica_groups=RG, ins=[src_.ap().opt()],
                outs=[dst_.ap().opt()]).then_inc(cs_.sem), r=[kvs], ww=[kvd], dsem=cs_, ndma=1)
        if stop_after < 1.3:
            return
        TA.off = TA_mark
        glrT = TA.alloc("glrT", [16, T], F32)
        spb = TA.alloc("sp", [128, 512], F32)
        e12 = TA.alloc("e12", [128, 512], F32)
        e2b = TA.alloc("e2b", [128, 512], F32)
        e3 = sb("e3", [128, 512], F32, sqb[1].off)
        wg2 = sb("wg2", [16, 512], F32, sgb[0].off)
        bg2 = sb("bg2", [1, 512], F32, sgb[1].off)
        dcy = SA.alloc("dcy", [128, 4, 8], F32)
        triN = TA.alloc("triN", [128, 128], F32)
        triS = TA.alloc("triS", [128, 128], F32)

        def ld2(e):
            dma(e, wg2.t[:], wg2_d, cds2)
            dma(e, bg2.t[:], bg2_d, cds2)
            dma(e, triN.t[:], triN_d, cds2)
            return dma(e, triS.t[:], triS_d, cds2)
        cds2 = P.dsem("const2")
        P.op("sp", ld2, w=[wg2.res, bg2.res, triN.res, triS.res], dsem=cds2, ndma=4)
        slot = pre_glr
        for h in range(2):
            ps = pb[h]
            projT(slot, 0, ps, h, n=16)
            P.op("act", lambda e, ps=ps, h=h: e.copy(out=glrT.t[:, HS[h]], in_=ps.t[:16, :]), r=[ps.res],
                 w=[glrT.res])
        LNS = math.log(128 ** -0.5)
        chk(1.31)
        for hd in range(4):
            st = stash[hd]
            if hd == 0:
                sr_ = pre_sr0
            else:
                sr_ = next_slot()
                fill_cols(sr_, win_d, O_GR + hd * 256, 256)
            sqk = next_slot()
            fill_cols(sqk, win_d, O_GQ + hd * 128, 128, dst0=0)
            fill_cols(sqk, win_d, O_GK + hd * 128, 128, dst0=128)
            sv = next_slot()
            fill_cols(sv, win_d, O_GV + hd * 256, 256)
            for t2 in range(4):
                pr = pb[5] if t2 % 2 == 0 else pb[6]
                for a in range(2):
                    projTok(sr_, 0, 256, pr, a * 256, t2 * 2 + a)
                P.op("act", lambda e, st=st, pr=pr, t2=t2: e.activation(
                    out=st["sr"].t[:, t2 * 2:t2 * 2 + 2, :],
                    in_=pr.t[:].rearrange("p (a c) -> p a c", a=2), func=AF.Silu),
                    r=[pr.res], ww=[st["sr"].res])
            for h in range(2):
                pla, pbc, ptl, pq, pk = pb[0], pb[1], pb[2], pb[3], pb[4]

                def mm_la(e, h=h, hd=hd):
                    for q in range(4):
                        i = h * 4 + q
                        e.matmul(pla.t[:, q * 128:(q + 1) * 128], glrT.t[:, i * 128:(i + 1) * 128],
                                 wg2.t[:, hd * 128:(hd + 1) * 128], start=True, stop=False)
                        ins = e.matmul(pla.t[:, q * 128:(q + 1) * 128], ones.t[0:1, :],
                                       bg2.t[:, hd * 128:(hd + 1) * 128], start=False, stop=True)
                    return ins
                P.op("pe", mm_la, r=[glrT.res, wg2.res, bg2.res, ones.res], w=[pla.res])
                P.op("act", lambda e: e.activation(out=spb.t[:], in_=pla.t[:], func=AF.Exp, scale=-1.0),
                     r=[pla.res], w=[spb.res])
                P.op("act", lambda e: e.activation(out=spb.t[:], in_=spb.t[:], func=AF.Ln, bias=1.0),
                     r=[spb.res], w=[spb.res])

                chk(1.32)

                def mm_bc(e):
                    for q in range(4):
                        ins = e.matmul(pbc.t[:, q * 128:(q + 1) * 128], spb.t[:, q * 128:(q + 1) * 128], triN.t[:],
                                       start=True, stop=True)
                    return ins
                P.op("pe", mm_bc, r=[spb.res, triN.res], w=[pbc.res])

                def mm_tl(e):
                    for q in range(4):
                        ins = e.matmul(ptl.t[:, q * 128:(q + 1) * 128], triS.t[:], spb.t[:, q * 128:(q + 1) * 128],
                                       start=True, stop=True)
                    return ins
                P.op("pe", mm_tl, r=[spb.res, triS.res], w=[ptl.res])
                chk(1.321)
                projT(sqk, 0, pq, h)
                projT(sqk, 128, pk, h)
                chk(1.322)
                P.op("act", lambda e, hd=hd, h=h: e.activation(
                    out=dcy.t[:, hd, h * 4:(h + 1) * 4],
                    in_=pbc.t[:].rearrange("p (q t) -> p q t", q=4)[:, :, 127], func=AF.Exp),
                    r=[pbc.res], ww=[dcy.res])
                chk(1.323)
                P.op("act", lambda e: e.activation(out=e12.t[:], in_=pbc.t[:], func=AF.Exp, bias=LNS),
                     r=[pbc.res], w=[e12.res])
                P.op("dve", lambda e, st=st, h=h: e.tensor_tensor(out=st["qd"].t[:, HS[h]], in0=pq.t[:],
                                                                  in1=e12.t[:], op=ALU.mult),
                     r=[pq.res, e12.res], ww=[st["qd"].res])
                chk(1.324)
                P.op("act", lambda e: e.activation(out=e2b.t[:], in_=pbc.t[:], func=AF.Exp, scale=-1.0),
                     r=[pbc.res], w=[e2b.res])
                P.op("dve", lambda e, st=st, h=h: e.tensor_tensor(out=st["ki"].t[:, HS[h]], in0=pk.t[:],
                                                                  in1=e2b.t[:], op=ALU.mult),
                     r=[pk.res, e2b.res], ww=[st["ki"].res])
                P.op("act", lambda e: e.activation(out=e3.t[:], in_=ptl.t[:], func=AF.Exp), r=[ptl.res],
                     w=[e3.res])
                chk(1.33)
                for q in range(4):
                    i = h * 4 + q
                    pkk, pvv = pb[5], pb[6]
                    projTok(sqk, 128, 128, pkk, 0, i)
                    projTok(sv, 0, 256, pvv, 0, i)
                    P.op("dve", lambda e, st=st, i=i, q=q, pkk=pkk: e.tensor_tensor(
                        out=st["kt"].t[:, i, :], in0=pkk.t[:, 0:128], in1=e3.t[:, q * 128:(q + 1) * 128],
                        op=ALU.mult), r=[pkk.res, e3.res], ww=[st["kt"].res])
                    P.op("act", lambda e, st=st, i=i, pvv=pvv: e.copy(out=st["v"].t[:, i, :], in_=pvv.t[:, 0:256]),
                         r=[pvv.res], ww=[st["v"].res])
                chk(1.332)
            chk(1.34)
            S_ap = ststage.t[:, hd * 256:(hd + 1) * 256]
            P.op("dve", lambda e, S_ap=S_ap: e.memset(S_ap, 0.0), w=[ststage.res])
            P.op("dve", lambda e, hd=hd: e.memset(ststage.t[:, 1024 + hd:1025 + hd], 1.0), w=[ststage.res])
            for i in range(8):
                pu = pb[i % 2]
                P.op("pe", lambda e, st=st, i=i, pu=pu: e.matmul(pu.t[:, 0:256], st["kt"].t[:, i, :],
                                                                st["v"].t[:, i, :], start=True, stop=True),
                     r=[st["kt"].res, st["v"].res], w=[pu.res])
                P.op("dve", lambda e, S_ap=S_ap, hd=hd, i=i, pu=pu: e.scalar_tensor_tensor(
                    out=S_ap, in0=S_ap, scalar=dcy.t[:, hd, i:i + 1], in1=pu.t[:, 0:256], op0=ALU.mult,
                    op1=ALU.add), r=[pu.res, dcy.res, ststage.res], w=[ststage.res])
                P.op("dve", lambda e, hd=hd, i=i: e.tensor_tensor(
                    out=ststage.t[:, 1024 + hd:1025 + hd], in0=ststage.t[:, 1024 + hd:1025 + hd],
                    in1=dcy.t[:, hd, i:i + 1], op=ALU.mult), r=[dcy.res, ststage.res], w=[ststage.res])
        if stop_after < 1.5:
            P.barrier()
            return
        stds = P.dsem("stds")
        P.op("sp", lambda e: dma(e, st_src.ap(), ststage.t[:], stds), r=[ststage.res], w=[sts], dsem=stds, ndma=1)
        ccs = P.dsem("cc", inc=1)
        ccs2 = P.dsem("cc2", inc=1)
        std = Res("st_dst")

        P.op("pool", lambda e: e.collective_compute(
            "AllGather", ALU.bypass, replica_groups=RG, ins=[st_src.ap().opt()],
            outs=[st_dst.ap().opt()]).then_inc(ccs2.sem), r=[sts], w=[std], dsem=ccs2, ndma=1)
        P.barrier(exclude=[ccs2])
        m1_qk("q", O_MQ, 0)
        P.barrier()
        if stop_after < 1.7:
            return
        TB = Arena(TA_mark, TA.end - TA_mark)
        ldb = TB.alloc("ldb", [128, FST], F32)
        ldb.ds = P.dsem("ldb")
        mjt = SA.alloc("mjt", [128, 3], F32)
        gout = TB.alloc("gout", [128, 256], F32)
        attmask = TB.alloc("attmask", [128, 128], F32)
        cds3 = P.dsem("const3")
        P.op("sp", lambda e: [dma(e, mjt.t[:], mj_d, cds3), dma(e, gout.t[:], gout_d, cds3),
                              dma(e, attmask.t[:], attm_d, cds3)][-1],
             w=[mjt.res, gout.res, attmask.res], dsem=cds3, ndma=3)
        dpr = SA.alloc("dpr", [128, 4], F32)
        ltmp = TB.alloc("ltmp", [128, 256], F32)
        P.op("dve", lambda e: e.memset(ststage.t[:, 0:1024], 0.0), w=[ststage.res])
        for j in range(3):
            P.op("sp", lambda e, j=j: dma(e, ldb.t[:], st_dst.ap()[j * 128:(j + 1) * 128, :], ldb.ds),
                 r=[std], w=[ldb.res], dsem=ldb.ds, ndma=1)
            P.op("dve", lambda e, j=j: e.tensor_scalar(out=dpr.t[:], in0=ldb.t[:, 1024:1028], scalar1=-1.0,
                                                       scalar2=mjt.t[:, j:j + 1], op0=ALU.add, op1=ALU.mult),
                 r=[ldb.res, mjt.res], w=[dpr.res])
            P.op("dve", lambda e: e.tensor_scalar(out=dpr.t[:], in0=dpr.t[:], scalar1=1.0, scalar2=None,
                                                  op0=ALU.add), r=[dpr.res], w=[dpr.res])
            for hd in range(4):
                P.op("dve", lambda e, j=j, hd=hd: e.tensor_scalar(
                    out=ltmp.t[:], in0=ldb.t[:, hd * 256:(hd + 1) * 256], scalar1=mjt.t[:, j:j + 1], scalar2=None,
                    op0=ALU.mult), r=[ldb.res, mjt.res], w=[ltmp.res])
                P.op("dve", lambda e, hd=hd: e.scalar_tensor_tensor(
                    out=ststage.t[:, hd * 256:(hd + 1) * 256], in0=ststage.t[:, hd * 256:(hd + 1) * 256],
                    scalar=dpr.t[:, hd:hd + 1], in1=ltmp.t[:], op0=ALU.mult, op1=ALU.add),
                    r=[ltmp.res, dpr.res, ststage.res], w=[ststage.res])
        mixv = nc.alloc_sbuf_tensor_at("mix", [128, 8, D], BF16, offset=HT_OFF)
        mix_res = Res("mix")
        HA = Arena(wsl[0].off, 3 * 8192)
        Sbf = [HA.alloc("Sbf%d" % h, [128, 256], BF16) for h in range(4)]
        attb = [HA.alloc("attb%d" % h, [128, 128], BF16) for h in range(4)]
        osq = [HA.alloc("osq%d" % h, [128, 256], F32) for h in range(4)]
        otmp = [HA.alloc("otmp%d" % h, [128, 256], F32) for h in range(4)]
        ocp = [HA.alloc("ocp%d" % h, [128, 256], F32) for h in range(4)]
        smh = [HA.alloc("smh%d" % h, [128, 8], F32) for h in range(4)]
        Sres = [Res("S%d" % h) for h in range(4)]
        pa_res = [pb[h].res for h in range(4)]
        po_res = [pb[h].res for h in range(4)]
        pu_res = [pb[4 + h % 3].res for h in range(4)]
        def hv(hd):
            return dict(st=stash[hd], S_ap=ststage.t[:, hd * 256:(hd + 1) * 256], pa=pb[hd].t[:, 0:128],
                        po=pb[hd].t[:, 256:512],
                        pu=pb[4 + hd % 3].t[:, (hd // 3) * 256:(hd // 3) * 256 + 256], sm=smh[hd])
        HV = [hv(hd) for hd in range(4)]
        for i in range(8):
            first_r = [ststage.res] if i == 0 else []
            for hd in range(4):
                v_ = HV[hd]
                st, S_ap, pa, pu = v_["st"], v_["S_ap"], v_["pa"], v_["pu"]
                P.op("act", lambda e, S_ap=S_ap, hd=hd: e.copy(out=Sbf[hd].t[:], in_=S_ap),
                     r=[Sres[hd]] + first_r, w=[Sbf[hd].res])
                P.op("pe", lambda e, st=st, i=i, pa=pa: e.matmul(
                    pa, st["ki"].t[:, i * 128:(i + 1) * 128], st["qd"].t[:, i * 128:(i + 1) * 128],
                    start=True, stop=True), r=[st["ki"].res, st["qd"].res], w=[pa_res[hd]])
            for hd in range(4):
                v_ = HV[hd]
                st, pu = v_["st"], v_["pu"]
                P.op("pe", lambda e, st=st, i=i, pu=pu: e.matmul(pu, st["kt"].t[:, i, :], st["v"].t[:, i, :],
                                                                start=True, stop=True),
                     r=[st["kt"].res, st["v"].res], w=[pu_res[hd]])
            for hd in range(4):
                pa = HV[hd]["pa"]
                P.op("dve", lambda e, pa=pa, hd=hd: e.tensor_tensor(out=attb[hd].t[:], in0=pa, in1=attmask.t[:],
                                                                   op=ALU.mult),
                     r=[pa_res[hd], attmask.res], w=[attb[hd].res])
            for hd in range(4):
                v_ = HV[hd]
                st, po, pu, S_ap = v_["st"], v_["po"], v_["pu"], v_["S_ap"]

                def mm_o(e, st=st, i=i, po=po, hd=hd):
                    e.matmul(po, st["qd"].t[:, i * 128:(i + 1) * 128], Sbf[hd].t[:], start=True, stop=False)
                    return e.matmul(po, attb[hd].t[:], st["v"].t[:, i, :], start=False, stop=True)
                P.op("pe", mm_o, r=[st["qd"].res, Sbf[hd].res, attb[hd].res, st["v"].res], w=[po_res[hd]])
                P.op("dve", lambda e, S_ap=S_ap, hd=hd, i=i, pu=pu: e.scalar_tensor_tensor(
                    out=S_ap, in0=S_ap, scalar=dcy.t[:, hd, i:i + 1], in1=pu, op0=ALU.mult,
                    op1=ALU.add), r=[pu_res[hd], dcy.res, Sres[hd], Sbf[hd].res] + first_r, w=[Sres[hd]])
            for hd in range(4):
                po = HV[hd]["po"]
                P.op("act", lambda e, po=po, hd=hd: e.copy(out=ocp[hd].t[:], in_=po), r=[po_res[hd]],
                     w=[ocp[hd].res])
            for hd in range(4):
                sm = HV[hd]["sm"]
                P.op("dve", lambda e, hd=hd: e.tensor_tensor(out=osq[hd].t[:], in0=ocp[hd].t[:], in1=ocp[hd].t[:],
                                                             op=ALU.mult), r=[ocp[hd].res], w=[osq[hd].res])
                P.op("dve", lambda e, hd=hd, sm=sm: e.tensor_reduce(out=sm.t[:, 0:1], in_=osq[hd].t[:], axis=AX.X,
                                                                   op=ALU.add), r=[osq[hd].res], w=[sm.res])
            for hd in range(4):
                sm = HV[hd]["sm"]
                P.op("act", lambda e, sm=sm: e.activation(out=sm.t[:, 1:2], in_=sm.t[:, 0:1], func=AF.Ln, bias=EPS,
                                                          scale=1.0 / 256), r=[sm.res], w=[sm.res])
                P.op("act", lambda e, sm=sm: e.activation(out=sm.t[:, 2:3], in_=sm.t[:, 1:2], func=AF.Exp,
                                                          scale=-0.5), r=[sm.res], w=[sm.res])
            for hd in range(4):
                v_ = HV[hd]
                st, sm = v_["st"], v_["sm"]
                P.op("dve", lambda e, hd=hd, sm=sm: e.scalar_tensor_tensor(
                    out=otmp[hd].t[:], in0=ocp[hd].t[:], scalar=sm.t[:, 2:3], in1=gout.t[:], op0=ALU.mult,
                    op1=ALU.mult), r=[ocp[hd].res, sm.res, gout.res], w=[otmp[hd].res])
                P.op("dve", lambda e, st=st, i=i, hd=hd: e.tensor_tensor(
                    out=mixv[:, i, hd * 256:(hd + 1) * 256], in0=otmp[hd].t[:], in1=st["sr"].t[:, i, :],
                    op=ALU.mult), r=[otmp[hd].res, st["sr"].res], ww=[mix_res])
        P.barrier()
        if stop_after < 1.85:
            return
        RA.reset()
        M = RA
        pastb = M.alloc("pastb", [128, 8, 8, 16], F32)
        past01 = M.alloc("past01", [128, 8, 8, 16], F32)
        sel = M.alloc("sel", [128, 8, 8, 16], F32)
        gm = M.alloc("gm", [128, 128], F32)
        kmT = M.alloc("kmT", [128, 8, 16], F32)
        kmTb = M.alloc("kmTb", [128, 8, 16], BF16)
        triT = M.alloc("triT", [128, 128], BF16)
        oacc = [M.alloc("oacc%d" % i, [128, 2, 130], F32) for i in range(2)]
        ptb = [M.alloc("ptb%d" % i, [128, 2, 256], BF16) for i in range(2)]
        qload = [M.alloc("qall%d" % i, [128, T], BF16) for i in range(2)]
        ld = []
        for i in range(2):
            d_ = dict(q=qload[i], ktp=M.alloc("ktp%d" % i, [128, 3, T], BF16), kto=M.alloc("kto%d" % i, [128, T], BF16),
                      vp=M.alloc("vp%d" % i, [128, 3, 8, 130], BF16), vo=M.alloc("vo%d" % i, [128, 8, 130], BF16),
                      res=Res("ld%d" % i), ds=P.dsem("ld%d" % i))
            ld.append(d_)

        def ld4(e):
            dma(e, pastb.t[:], pastb_d.rearrange("p (a b c) -> p a b c", a=8, b=8), cds4)
            dma(e, past01.t[:], past01_d.rearrange("p (a b c) -> p a b c", a=8, b=8), cds4)
            for r in range(3):
                dma(e, kmT.t[:, :, r * 4:(r + 1) * 4],
                    st_dst.ap()[r * 128:(r + 1) * 128, 1028:1060].rearrange("p (h b) -> p h b", h=8), cds4)
            return dma(e, kmT.t[:, :, 12:16], kmown.t[:], cds4)
        cds4 = P.dsem("const4")
        cds5 = P.dsem("const5")
        P.op("sp", ld4, r=[std, kmown.res], w=[pastb.res, past01.res, kmT.res], dsem=cds4, ndma=6)
        P.op("pool", lambda e: dma(e, triT.t[:], triT_d, cds5), w=[triT.res], dsem=cds5, ndma=1)
        P.op("dve", lambda e: e.tensor_copy(out=kmTb.t[:], in_=kmT.t[:]), r=[kmT.res], w=[kmTb.res])

        def load_head(hd):
            L = ld[hd % 2]

            def f(e):
                dma(e, L["q"].t[:], qn_dram.ap()[:, hd * 1024:(hd + 1) * 1024], L["ds"])
                kc = slice((hd % 4) * 1024, (hd % 4 + 1) * 1024)
                vc = slice((hd % 2) * 1040, (hd % 2 + 1) * 1040)
                dma(e, L["kto"].t[:], ksrc[hd // 4].ap()[:, kc], L["ds"])
                dma(e, L["vo"].t[:], vsrc[hd // 2].ap()[:, vc].rearrange(
                    "p (i c) -> p i c", i=8), L["ds"])
                for r in range(3):
                    dma(e, L["ktp"].t[:, r, :], kdst[hd // 4].ap()[r * 128:(r + 1) * 128, kc], L["ds"])
                    ins = dma(e, L["vp"].t[:, r, :, :],
                              vdst[hd // 2].ap()[r * 128:(r + 1) * 128, vc].rearrange(
                                  "p (i c) -> p i c", i=8), L["ds"])
                return ins
            P.op("sp", f, r=[kvd, kvs, qnr], w=[L["res"]], dsem=L["ds"], ndma=9)
        SC = 128 ** -0.5
        ptb = ptb + [M.alloc("ptb%d" % i, [128, 2, 256], BF16) for i in range(2, 5)]
        mx8 = M.alloc("mx8b", [128, 8, 8], F32)
        gm3 = gm.t[:].rearrange("p (i n) -> p i n", i=8)

        def gate_ops(hd):
            L = ld[hd % 2]
            pg = pb[6]

            def mmg(e):
                for i in range(8):
                    ins = e.matmul(pg.t[:, i * 16:(i + 1) * 16], L["q"].t[:, i * 128:(i + 1) * 128],
                                   kmTb.t[:, hd, :], start=True, stop=True)
                return ins
            P.op("pe", mmg, r=[L["res"], kmTb.res], w=[pg.res])
            P.op("dve", lambda e: e.tensor_tensor(
                out=gm3, in0=pg.t[:, 0:128].rearrange("p (i n) -> p i n", i=8), in1=pastb.t[:, :, hd, :],
                op=ALU.add), r=[pg.res, pastb.res], w=[gm.res])
            for i in range(8):
                P.op("dve", lambda e, i=i: e.max(out=mx8.t[:, i, :], in_=gm.t[:, i * 16:(i + 1) * 16]),
                     r=[gm.res], ww=[mx8.res])
            for i in range(8):
                P.op("dve", lambda e, i=i: e.scalar_tensor_tensor(
                    out=sel.t[:, i, hd, :], in0=gm.t[:, i * 16:(i + 1) * 16], scalar=mx8.t[:, i, 2:3],
                    in1=past01.t[:, i, hd, :], op0=ALU.is_ge, op1=ALU.mult),
                    r=[gm.res, mx8.res, past01.res], ww=[sel.res])

        tasks = []
        for hd in range(8):
            for jb in range(4):
                blocks = [("p", r, j, r * 4 + j) for r in range(3) for j in range(4)]
                blocks += [("o", 0, j, 12 + j) for j in range(jb)]
                blocks += [("d", 0, jb, -1)]
                for bi, (kind, r, j, col) in enumerate(blocks):
                    tasks.append(dict(hd=hd, jb=jb, kind=kind, r=r, j=j, col=col, first=(bi == 0),
                                      last=(bi == len(blocks) - 1)))
        SK, NS, NT, NU = 3, 3, 5, 3

        def emit_s(t):
            tk = tasks[t]
            L = ld[tk["hd"] % 2]
            psS, pt = pb[t % NS], ptb[t % NT]
            jb, kind, r, j = tk["jb"], tk["kind"], tk["r"], tk["j"]
            Q = L["q"].t[:, jb * 256:(jb + 1) * 256]
            KT = L["ktp"].t[:, r, j * 256:(j + 1) * 256] if kind == "p" else L["kto"].t[:, j * 256:(j + 1) * 256]

            def mm_s(e):
                for k in range(2):
                    ins = e.matmul(psS.t[:, k * 256:(k + 1) * 256], KT[:, k * 128:(k + 1) * 128], Q,
                                   start=True, stop=True)
                return ins
            P.op("pe", mm_s, r=[L["res"]], w=[psS.res])
            P.op("act", lambda e: e.activation(out=pt.t[:].rearrange("p a b -> p (a b)"), in_=psS.t[:],
                                               func=AF.Exp, scale=SC), r=[psS.res], w=[pt.res])
            if kind == "d":
                P.op("dve", lambda e: e.tensor_tensor(out=pt.t[:, 0, 0:128], in0=pt.t[:, 0, 0:128],
                                                      in1=triT.t[:], op=ALU.mult),
                     r=[pt.res, triT.res], w=[pt.res])
                P.op("dve", lambda e: e.tensor_tensor(out=pt.t[:, 1, 128:256], in0=pt.t[:, 1, 128:256],
                                                      in1=triT.t[:], op=ALU.mult),
                     r=[pt.res, triT.res], w=[pt.res])

        def emit_u(t):
            tk = tasks[t]
            hd, jb, kind, r, j, col = tk["hd"], tk["jb"], tk["kind"], tk["r"], tk["j"], tk["col"]
            L = ld[hd % 2]
            pt, psU = ptb[t % NT], pb[3 + t % NU]
            oa = oacc[jb % 2]
            if kind == "p":
                V = [L["vp"].t[:, r, j * 2 + k, :] for k in range(2)]
            else:
                V = [L["vo"].t[:, j * 2 + k, :] for k in range(2)]

            def mm_u(e):
                for a in range(2):
                    ks = [0] if (kind == "d" and a == 0) else [0, 1]
                    for n_, k in enumerate(ks):
                        ins = e.matmul(psU.t[:, a * 256:a * 256 + 130], pt.t[:, k, a * 128:(a + 1) * 128],
                                       V[k], start=(n_ == 0), stop=(n_ == len(ks) - 1))
                return ins
            P.op("pe", mm_u, r=[pt.res, L["res"]], w=[psU.res])
            for a in range(2):
                i = jb * 2 + a
                sc = 1.0 if kind == "d" else sel.t[:, i, hd, col:col + 1]
                if tk["first"]:
                    P.op("dve", lambda e, a=a, sc=sc: e.tensor_scalar(
                        out=oa.t[:, a, :], in0=psU.t[:, a * 256:a * 256 + 130], scalar1=sc, scalar2=None,
                        op0=ALU.mult), r=[psU.res, sel.res], w=[oa.res])
                else:
                    P.op("dve", lambda e, a=a, sc=sc: e.scalar_tensor_tensor(
                        out=oa.t[:, a, :], in0=psU.t[:, a * 256:a * 256 + 130], scalar=sc, in1=oa.t[:, a, :],
                        op0=ALU.mult, op1=ALU.add), r=[psU.res, sel.res, oa.res], w=[oa.res])
            if tk["last"]:
                for a in range(2):
                    i = jb * 2 + a
                    P.op("dve", lambda e, a=a: e.reciprocal(out=small.t[:, 8 + a:9 + a], in_=oa.t[:, a, 128:129]),
                         r=[oa.res], w=[small.res])
                    P.op("dve", lambda e, a=a, i=i: e.tensor_scalar(
                        out=mixv[:, i, 1024 + hd * 128:1024 + (hd + 1) * 128], in0=oa.t[:, a, 0:128],
                        scalar1=small.t[:, 8 + a:9 + a], scalar2=None, op0=ALU.mult),
                        r=[oa.res, small.res], ww=[mix_res])

        load_head(0)
        gate_ops(0)
        load_head(1)
        for t in range(len(tasks) + SK):
            if t < len(tasks):
                tk = tasks[t]
                if tk["jb"] == 2 and tk["first"] and tk["hd"] + 1 < 8:
                    gate_ops(tk["hd"] + 1)
                emit_s(t)
            u = t - SK
            if u >= 0:
                emit_u(u)
                tk = tasks[u]
                if tk["jb"] == 3 and tk["last"] and tk["hd"] + 2 < 8:
                    load_head(tk["hd"] + 2)
        P.barrier()
        if stop_after < 1.95:
            return
        RA.reset()
        mixT = RA.alloc("mixT", [128, NC_, T], BF16)
        mixT_res = Res("mixT")
        wsl2 = [RA.alloc("wso%d" % i, [128, NC_, 256], BF16) for i in range(3)]
        for s in wsl2:
            s.ds = P.dsem(s.res.name)
        k5 = 0
        for i in range(8):
            for g4 in range(4):
                def tr(e, i=i, g4=g4):
                    for q in range(4):
                        c = g4 * 4 + q
                        ins = e.transpose(out=pbf.t[:, q * 128:(q + 1) * 128], in_=mixv[:, i, c * 128:(c + 1) * 128],
                                          identity=identb.t[:])
                    return ins
                P.op("pe", tr, r=[mix_res, identb.res], w=[pbf.res])
                dst_ap = mixT.t[:, g4 * 4:(g4 + 1) * 4, i * 128:(i + 1) * 128]
                src_ap = pbf.t[:, 0:512].rearrange("p (q t) -> p q t", q=4)
                k5 += 1
                if k5 % 2:
                    P.op("dve", lambda e, a=dst_ap, b=src_ap: e.tensor_copy(out=a, in_=b), r=[pbf.res], ww=[mixT_res])
                else:
                    P.op("act", lambda e, a=dst_ap, b=src_ap: e.copy(out=a, in_=b), r=[pbf.res], ww=[mixT_res])
        for g in range(8):
            slot = wsl2[g % 3]
            fill_cols(slot, wout_d, g * 256, 256)
            for dd in range(2):
                dc = g * 2 + dd
                for h in range(2):
                    ps = pb[k5 % 2]
                    k5 += 1

                    def mm(e, slot=slot, dd=dd, h=h, ps=ps):
                        for c in range(NC_):
                            ins = e.matmul(ps.t[:], slot.t[:, c, dd * 128:(dd + 1) * 128], mixT.t[:, c, HS[h]],
                                           start=(c == 0), stop=(c == NC_ - 1))
                        return ins
                    P.op("pe", mm, r=[slot.res, mixT_res], w=[ps.res])
                    P.op("dve", lambda e, ps=ps, dc=dc, h=h: e.tensor_tensor(
                        out=xT.t[:, dc, HS[h]], in0=ps.t[:], in1=xT.t[:, dc, HS[h]], op=ALU.add),
                        r=[ps.res, xT_res[dc][h]], w=[xT_res[dc][h]])
        P.barrier()

    if stop_after >= 1.1:
        try:
            mixer()
        except _Stop:
            P.barrier()

    def xattn():
        RA.reset()
        A = RA
        memT = A.alloc("memT", [128, NC_, 256], F32)
        memn = A.alloc("memn", [128, NC_, 256], BF16)
        xk = A.alloc("xk", [128, 4, 256], BF16)
        xv = A.alloc("xv", [128, 2, 4, 128], BF16)
        xq = A.alloc("xq", [128, 4, T], BF16)
        xo = A.alloc("xo", [128, 4, T], BF16)
        wsl = [A.alloc("wsx%d" % i, [128, NC_, 256], BF16) for i in range(3)]
        wos = A.alloc("wos", [128, 4, D], BF16)
        rawsx = [A.alloc("rawx%d" % i, [128, 512], F32) for i in range(2)]
        rden = A.alloc("rden", [128, 512], F32)
        ptx = [A.alloc("ptx%d" % i, [128, 512], BF16) for i in range(4)]
        mstg = [sb("mstg0", [128, D], F32, xq.off), sb("mstg1", [128, D], F32, xo.off)]
        for s in wsl + [wos] + mstg:
            s.ds = P.dsem(s.res.name)
        transpose_in(mem_d, 2, memT, lambda g4, i: [memT.res], mstg, 256)
        ps = pb[6]
        for c in range(NC_):
            sq = sqb[c % 2]
            P.op("dve", lambda e, sq=sq, c=c: e.tensor_tensor(out=sq.t[:, :256], in0=memT.t[:, c, :],
                                                              in1=memT.t[:, c, :], op=ALU.mult),
                 r=[memT.res], w=[sq.res])
            P.op("pe", lambda e, sq=sq, c=c: e.matmul(ps.t[:, :256], ones.t[:], sq.t[:, :256], start=(c == 0),
                                                      stop=(c == NC_ - 1)), r=[sq.res, ones.res], w=[ps.res])
        P.op("act", lambda e: e.activation(out=rstd.t[:, :256], in_=ps.t[:, :256], func=AF.Ln, bias=EPS,
                                           scale=1.0 / D), r=[ps.res], w=[rstd.res])
        P.op("act", lambda e: e.activation(out=rstd.t[:, :256], in_=rstd.t[:, :256], func=AF.Exp, scale=-0.5),
             r=[rstd.res], w=[rstd.res])
        for c in range(NC_):
            P.op("dve", lambda e, c=c: e.scalar_tensor_tensor(
                out=memn.t[:, c, :], in0=memT.t[:, c, :], scalar=gains.t[:, 48 + c:49 + c], in1=rstd.t[:, :256],
                op0=ALU.mult, op1=ALU.mult), r=[memT.res, rstd.res, gains.res], ww=[memn.res])
        kx = 0
        for hp in range(2):
            slot = wsl[kx % 3]
            kx += 1
            fill_cols(slot, wkv_d, hp * 256, 256)
            for hh in range(2):
                hd = hp * 2 + hh
                pk = pb[hh]

                def mm(e, slot=slot, hh=hh, pk=pk):
                    for c in range(NC_):
                        ins = e.matmul(pk.t[:, :256], slot.t[:, c, hh * 128:(hh + 1) * 128], memn.t[:, c, :],
                                       start=(c == 0), stop=(c == NC_ - 1))
                    return ins
                P.op("pe", mm, r=[slot.res, memn.res], w=[pk.res])
                raw = rawsx[hh]
                P.op("act", lambda e, pk=pk, raw=raw: e.copy(out=raw.t[:, :256], in_=pk.t[:, :256]), r=[pk.res],
                     w=[raw.res])
                head_norm(raw.t[:, :256], raw.res, 3, xk.t[:, hd, :], xk.res, n=256)
        for hp in range(2):
            slot = wsl[kx % 3]
            kx += 1
            fill_cols(slot, wkv_d, 512 + hp * 256, 256)
            for mt in range(2):
                pv = pb[2 + mt]

                def mm(e, slot=slot, mt=mt, pv=pv):
                    for c in range(NC_):
                        ins = e.matmul(pv.t[:, :256], memn.t[:, c, mt * 128:(mt + 1) * 128], slot.t[:, c, :],
                                       start=(c == 0), stop=(c == NC_ - 1))
                    return ins
                P.op("pe", mm, r=[slot.res, memn.res], w=[pv.res])
                P.op("act", lambda e, pv=pv, mt=mt, hp=hp: e.copy(
                    out=xv.t[:, mt, hp * 2:hp * 2 + 2, :], in_=pv.t[:, :256].rearrange("p (a c) -> p a c", a=2)),
                    r=[pv.res], ww=[xv.res])
        P.barrier()
        rms_norm_T(32)
        hr = [[hT_res[c][h] for c in range(NC_)] for h in range(2)]
        qits = [(hp, hh, h) for hp in range(2) for hh in range(2) for h in range(2)]
        qslots, qst = {}, {}

        def q_a(n):
            hp, hh, h = qits[n]
            if hp not in qslots:
                qslots[hp] = wsl[(kx + hp) % 3]
                fill_cols(qslots[hp], wq_d, hp * 256, 256)
            slot = qslots[hp]
            pq, raw = pb[n % 2], rawsx[n % 2]

            def mm(e):
                for c in range(NC_):
                    ins = e.matmul(pq.t[:], slot.t[:, c, hh * 128:(hh + 1) * 128], hT.t[:, c, HS[h]],
                                   start=(c == 0), stop=(c == NC_ - 1))
                return ins
            P.op("pe", mm, r=[slot.res] + hr[h], w=[pq.res])
            P.op("act", lambda e: e.copy(out=raw.t[:], in_=pq.t[:]), r=[pq.res], w=[raw.res])
            qst[n] = raw

        def q_b(n):
            hp, hh, h = qits[n]
            raw = qst[n]
            head_norm(raw.t[:], raw.res, 2, xq.t[:, hp * 2 + hh, HS[h]], xq.res)
        q_a(0)
        for n in range(len(qits)):
            if n + 1 < len(qits):
                q_a(n + 1)
            q_b(n)
        def fwo(e):
            for k in range(4):
                ins = dma(e, wos.t[:, k, :], wo_d.ap[k * 128:(k + 1) * 128, :], wos.ds)
            return ins
        P.op("pool", fwo, r=[wo_d.res], w=[wos.res], dsem=wos.ds, ndma=4)
        SC = 128 ** -0.5
        its = [(hd, h) for hd in range(4) for h in range(2)]

        def x_s(it):
            hd, h = its[it]
            pS = [pb[0], pb[1]] if it % 2 == 0 else [pb[4], pb[5]]
            for mt in range(2):
                pt = ptx[(it % 2) * 2 + mt]
                P.op("pe", lambda e, mt=mt, p_=pS[mt]: e.matmul(
                    p_.t[:], xk.t[:, hd, mt * 128:(mt + 1) * 128], xq.t[:, hd, HS[h]], start=True, stop=True),
                    r=[xk.res, xq.res], w=[pS[mt].res])
                P.op("act", lambda e, p_=pS[mt], pt=pt: e.activation(out=pt.t[:], in_=p_.t[:], func=AF.Exp,
                                                                  scale=SC), r=[pS[mt].res], w=[pt.res])

        def x_o(it):
            hd, h = its[it]
            pts = [ptx[(it % 2) * 2 + mt] for mt in range(2)]
            pO, pD = pb[2], pb[3]

            def mm_o(e):
                for mt in range(2):
                    ins = e.matmul(pO.t[:], xv.t[:, mt, hd, :], pts[mt].t[:], start=(mt == 0), stop=(mt == 1))
                return ins
            P.op("pe", mm_o, r=[xv.res, pts[0].res, pts[1].res], w=[pO.res])

            def mm_d(e):
                for mt in range(2):
                    ins = e.matmul(pD.t[:], onesb.t[:], pts[mt].t[:], start=(mt == 0), stop=(mt == 1))
                return ins
            P.op("pe", mm_d, r=[onesb.res, pts[0].res, pts[1].res], w=[pD.res])
            P.op("dve", lambda e: e.reciprocal(out=rden.t[:], in_=pD.t[:]), r=[pD.res], w=[rden.res])
            P.op("dve", lambda e: e.tensor_tensor(out=xo.t[:, hd, HS[h]], in0=pO.t[:], in1=rden.t[:],
                                                  op=ALU.mult), r=[pO.res, rden.res], ww=[xo.res])
        x_s(0)
        for it in range(8):
            if it + 1 < 8:
                x_s(it + 1)
            x_o(it)
        for dc in range(NC_):
            for h in range(2):
                ps = pb[4 + (dc * 2 + h) % 2]

                def mm(e, ps=ps, dc=dc, h=h):
                    for k in range(4):
                        ins = e.matmul(ps.t[:], wos.t[:, k, dc * 128:(dc + 1) * 128], xo.t[:, k, HS[h]],
                                       start=(k == 0), stop=(k == 3))
                    return ins
                P.op("pe", mm, r=[wos.res, xo.res], w=[ps.res])
                P.op("dve", lambda e, ps=ps, dc=dc, h=h: e.tensor_tensor(
                    out=xT.t[:, dc, HS[h]], in0=ps.t[:], in1=xT.t[:, dc, HS[h]], op=ALU.add),
                    r=[ps.res, xT_res[dc][h]], w=[xT_res[dc][h]])
        P.barrier()

    if stop_after >= 3:
        xattn()

    outr = Res("y")
    out_state = {}

    def emit_out(tiles, ost):
        ko = out_state.get("ko", 0)
        for i in tiles:
            o_ = ost[i % 2]
            for g4 in range(4):
                ps = pb[ko % 4]
                ko += 1

                def tr(e, ps=ps, g4=g4, i=i):
                    for q in range(4):
                        c = g4 * 4 + q
                        ins = e.transpose(out=ps.t[:, q * 128:(q + 1) * 128], in_=xT.t[:, c, i * 128:(i + 1) * 128],
                                          identity=ident.t[:])
                    return ins
                P.op("pe", tr, r=[xT_res[c][i // 4] for c in range(g4 * 4, g4 * 4 + 4)] + [ident.res], w=[ps.res])
                if ko % 2:
                    P.op("dve", lambda e, o_=o_, ps=ps, g4=g4: e.tensor_copy(out=o_.t[:, g4 * 512:(g4 + 1) * 512],
                                                                            in_=ps.t[:]), r=[ps.res], w=[o_.res])
                else:
                    P.op("act", lambda e, o_=o_, ps=ps, g4=g4: e.copy(out=o_.t[:, g4 * 512:(g4 + 1) * 512],
                                                                      in_=ps.t[:]), r=[ps.res], w=[o_.res])
            P.op("sp", lambda e, o_=o_, i=i: dma(e, y_d[i * 128:(i + 1) * 128, :], o_.t[:], o_.ds),
                 r=[o_.res], ww=[outr], dsem=o_.ds, ndma=1)
        out_state["ko"] = ko

    def ffn2_tail(half, ring):
        if "ost" not in out_state:
            ost = [sb("ostA", [128, D], F32, ring[0].off), sb("ostB", [128, D], F32, ring[1].off)]
            for s_ in ost:
                s_.ds = P.dsem(s_.res.name)
            out_state["ost"] = ost
        emit_out(range(half * 4, half * 4 + 4), out_state["ost"])

    if stop_after >= 4 and not SKIP_FFN:
        ffn(*w_ffn[1], 64, "b", tail=ffn2_tail)
    else:
        RA.reset()
        ost = [RA.alloc("ost%d" % i, [128, D], F32) for i in range(2)]
        for s_ in ost:
            s_.ds = P.dsem(s_.res.name)
        emit_out(range(8), ost)
    P.barrier()
    P.op("sp", lambda e: None, r=[outr])

    with nc.Block() as block:
        P.emit(block)
    return nc


_NC_CACHE = {}
_DEBUG_MAPS = None


def _consts(rank_in_group):
    p = rank_in_group
    c = {}
    c["ident"] = np.eye(128, dtype=np.float32)
    j = np.arange(128)[:, None]
    i = np.arange(128)[None, :]
    c["triN"] = np.where(j <= i, -1.0 / 16.0, 0.0).astype(np.float32)
    c["triS"] = np.where(j > i, -1.0 / 16.0, 0.0).astype(np.float32)
    c["attmask"] = (j <= i).astype(np.float32)
    c["triT"] = (j <= i).astype(np.float32)
    past = np.zeros((8, 16), np.float32)
    for t in range(8):
        jb = t // 2
        for col in range(12):
            past[t, col] = 1.0 if col < 4 * p else 0.0
        for jj in range(4):
            past[t, 12 + jj] = 1.0 if jj < jb else 0.0
    p01 = np.broadcast_to(past[None, :, None, :], (128, 8, 8, 16)).reshape(128, 1024)
    c["past01"] = np.ascontiguousarray(p01, dtype=np.float32)
    c["pastbias"] = np.ascontiguousarray((p01 - 1.0) * 1e30, dtype=np.float32)
    mj = np.array([1.0 if jj < p else 0.0 for jj in range(3)], np.float32)
    c["mj"] = np.ascontiguousarray(np.broadcast_to(mj[None, :], (128, 3)), dtype=np.float32)
    return c


def kernel(x, mem, ffn1_norm, ffn1_w_gate, ffn1_w_up, ffn1_w_down, mix_norm, w_in,
           gla_w_gate2, gla_b_gate2, gla_out_norm, moba_q_norm, moba_k_norm, w_out,
           xattn_norm, mem_norm, xattn_w_q, xattn_w_kv, xattn_w_o, xattn_q_norm,
           xattn_k_norm, ffn2_norm, ffn2_w_gate, ffn2_w_up, ffn2_w_down):
    f = lambda a: np.ascontiguousarray(np.asarray(a, dtype=np.float32))
    x = f(x)
    mem = f(mem)

    def gl(g):
        return f(g).reshape(16, 128).T
    gains = np.ascontiguousarray(np.concatenate(
        [gl(ffn1_norm[0]), gl(mix_norm[0]), gl(xattn_norm[0]), gl(mem_norm[0]), gl(ffn2_norm[0])], axis=1))
    sg = np.ascontiguousarray(np.stack([f(moba_q_norm[0]), f(moba_k_norm[0]), f(xattn_q_norm[0]),
                                        f(xattn_k_norm[0])], axis=1))
    gout = np.ascontiguousarray(np.broadcast_to(f(gla_out_norm[0])[None, :], (128, 256)))
    shared = {
        "f1g": f(ffn1_w_gate[0]), "f1u": f(ffn1_w_up[0]), "f1d": f(ffn1_w_down[0]),
        "f2g": f(ffn2_w_gate[0]), "f2u": f(ffn2_w_up[0]), "f2d": f(ffn2_w_down[0]),
        "w_in": f(w_in[0]), "w_out": f(w_out[0]), "xw_q": f(xattn_w_q[0]), "xw_kv": f(xattn_w_kv[0]),
        "xw_o": f(xattn_w_o[0]), "gains": gains, "sgains": sg, "gout_bc": gout,
        "wg2": f(gla_w_gate2[0]), "bg2": f(gla_b_gate2[0]).reshape(1, 512),
    }
    if SKIP_FFN:
        for k in ("f1g", "f1u", "f1d", "f2g", "f2u", "f2d"):
            shared[k] = np.zeros((128, 128), np.float32)
    in_maps = []
    WNAMES = ("f1g", "f1u", "f1d", "f2g", "f2u", "f2d", "w_in", "w_out", "xw_q", "xw_kv", "xw_o")
    for c in range(8):
        b, p = c // 4, c % 4
        m = dict(shared)
        m["x"] = np.ascontiguousarray(x[b, p * T:(p + 1) * T, :])
        m["mem"] = mem[b]
        m.update(_consts(p))
        in_maps.append(m)
    if _DEBUG_MAPS is not None:
        _DEBUG_MAPS.append(in_maps)
        return None
    key = (STOP_AFTER, SKIP_FFN)
    if key not in _NC_CACHE:
        _NC_CACHE[key] = build(STOP_AFTER)
    nc = _NC_CACHE[key]
    res = run_bass_kernel_spmd(nc, in_maps, core_ids=list(range(8)))
    out = np.empty((2, 4096, D), np.float32)
    for c in range(8):
        b, p = c // 4, c % 4
        out[b, p * T:(p + 1) * T, :] = np.asarray(res.results[c]["y"], dtype=np.float32)
    return out
```

```python
import math
import numpy as np
import ml_dtypes
import concourse.bass as bass
import concourse.mybir as mybir
from concourse.bass_utils import run_bass_kernel_spmd

F32 = mybir.dt.float32
BF16 = mybir.dt.bfloat16
AF = mybir.ActivationFunctionType
ALU = mybir.AluOpType
AX = mybir.AxisListType

D = 2048
DFF = 5632
T = 1024
NC_ = 16
EPS = 1e-6
WIN = 6160
O_GQ, O_GK, O_GV, O_GR, O_GLR, O_MQ, O_MK, O_MV = 0, 512, 1024, 2048, 3072, 3088, 4112, 5136
FKV = 8192 + 8 * 8 * 130
FST = 1064
STOP_AFTER = 99
SKIP_FFN = False


class _Stop(Exception):
    pass


class Res:
    __slots__ = ("name", "w", "r", "ws")

    def __init__(self, name):
        self.name = name
        self.w = None
        self.r = []
        self.ws = []


class DSem:
    def __init__(self, sem, inc=16):
        self.sem = sem
        self.n = 0
        self.last = None
        self.inc = inc


class Op:
    pass


class Buf:
    def __init__(self, t, name):
        self.t = t
        self.res = Res(name)


class Prog:
    ENG = ("pe", "act", "dve", "pool", "sp")

    def __init__(self, nc):
        self.nc = nc
        self.ops = []
        self.dsems = []

    def dsem(self, name, inc=16):
        d = DSem(self.nc.alloc_semaphore(name), inc)
        self.dsems.append(d)
        return d

    def op(self, eng, fn, r=(), w=(), dsem=None, ndma=0, ww=()):
        o = Op()
        o.eng, o.fn, o.r, o.w, o.dsem, o.ndma = eng, fn, list(r), list(w), dsem, ndma
        o.ww = list(ww)
        o.deps = []
        o.signal = False
        o.bar = False
        self.ops.append(o)
        return o

    def barrier(self, exclude=()):
        o = Op()
        o.bar = True
        o.exclude = [id(d) for d in exclude]
        self.ops.append(o)

    def emit(self, block):
        nc = self.nc
        last = {}
        pend = {e: None for e in self.ENG}
        for o in self.ops:
            if o.bar:
                deps = [x for x in last.values()]
                deps += [d.last for d in self.dsems if d.last is not None and id(d) not in o.exclude]
                for e in self.ENG:
                    pend[e] = list(deps) if pend[e] is None else pend[e] + deps
                continue
            deps = []
            for r in o.r:
                if r.w is not None:
                    deps.append(r.w)
                deps.extend(r.ws)
            for w in o.w:
                if w.w is not None:
                    deps.append(w.w)
                deps.extend(w.r)
                deps.extend(w.ws)
            for w in o.ww:
                if w.w is not None:
                    deps.append(w.w)
                deps.extend(w.r)
            if pend[o.eng] is not None:
                deps.extend(pend[o.eng])
                pend[o.eng] = None
            seen = set()
            for d in deps:
                if d is o or id(d) in seen:
                    continue
                seen.add(id(d))
                if d.dsem is None and d.eng == "pe" and o.eng == "pe":
                    continue
                o.deps.append(d)
                d.signal = True
            for r in o.r:
                r.r.append(o)
            for w in o.w:
                w.w = o
                w.r = []
                w.ws = []
            for w in o.ww:
                w.ws.append(o)
            if o.dsem is not None:
                o.dsem.last = o
            else:
                last[o.eng] = o
        cnt = {e: 0 for e in self.ENG}
        esem = {e: nc.alloc_semaphore("es_" + e) for e in self.ENG}
        for o in self.ops:
            if o.bar:
                continue
            if o.dsem is not None:
                o.dsem.n += o.ndma
                o.ticket = o.dsem.inc * o.dsem.n
                o.sem = o.dsem.sem
            elif o.signal:
                cnt[o.eng] += 1
                o.ticket = cnt[o.eng]
                o.sem = esem[o.eng]
        by = {e: [o for o in self.ops if not o.bar and o.eng == e] for e in self.ENG}

        def run(name, e):
            waited = {}
            for o in by[name]:
                for d in o.deps:
                    k = id(d.sem)
                    if waited.get(k, 0) < d.ticket:
                        e.wait_ge(d.sem, d.ticket)
                        waited[k] = d.ticket
                ins = o.fn(e)
                if o.dsem is None and o.signal and ins is not None:
                    ins.then_inc(esem[name], 1)

        @block.tensor
        def _(e):
            run("pe", e)

        @block.scalar
        def _(e):
            run("act", e)

        @block.vector
        def _(e):
            run("dve", e)

        @block.gpsimd
        def _(e):
            run("pool", e)

        @block.sync
        def _(e):
            run("sp", e)


def build(stop_after=99):
    nc = bass.Bass("TRN2", target_bir_lowering=False)
    P = Prog(nc)

    def din(name, shape, dt=F32):
        return nc.dram_tensor(name, list(shape), dt, kind="ExternalInput").ap()

    x_d = din("x", [T, D])
    mem_d = din("mem", [256, D])
    y_d = nc.dram_tensor("y", [T, D], F32, kind="ExternalOutput").ap()
    WS = []

    class WT:
        pass

    def wshard(name, rows, cols):
        w = WT()
        if SKIP_FFN and name[0] == "f":
            rows, cols = 128, 128
        w.ap = din(name, [rows, cols])
        w.res = Res(name)
        w.sres = Res(name + "_s")
        WS.append(w)
        return w
    w_ffn = [[wshard("f1g", D, DFF), wshard("f1u", D, DFF), wshard("f1d", DFF, D)]]
    win_d = wshard("w_in", D, WIN)
    wout_d = wshard("w_out", D, D)
    wkv_d = wshard("xw_kv", D, 1024)
    wq_d = wshard("xw_q", D, 512)
    wo_d = wshard("xw_o", 512, D)
    w_ffn.append([wshard("f2g", D, DFF), wshard("f2u", D, DFF), wshard("f2d", DFF, D)])
    gains_d = din("gains", [128, 80])
    sg_d = din("sgains", [128, 4])
    gout_d = din("gout_bc", [128, 256])
    wg2_d = din("wg2", [16, 512])
    bg2_d = din("bg2", [1, 512])
    ident_d = din("ident", [128, 128])
    triN_d = din("triN", [128, 128])
    triS_d = din("triS", [128, 128])
    attm_d = din("attmask", [128, 128])
    triT_d = din("triT", [128, 128])
    pastb_d = din("pastbias", [128, 1024])
    past01_d = din("past01", [128, 1024])
    mj_d = din("mj", [128, 3])

    ksrc = [nc.dram_tensor("ksrc%d" % i, [128, 4096], BF16) for i in range(2)]
    kdst = [nc.dram_tensor("kdst%d" % i, [512, 4096], BF16) for i in range(2)]
    vsrc = [nc.dram_tensor("vsrc%d" % i, [128, 2080], BF16) for i in range(4)]
    vdst = [nc.dram_tensor("vdst%d" % i, [512, 2080], BF16) for i in range(4)]
    st_src = nc.dram_tensor("st_src", [128, FST], F32)
    st_dst = nc.dram_tensor("st_dst", [512, FST], F32)
    qn_dram = nc.dram_tensor("qn_dram", [128, 8192], BF16)

    def sb(name, shape, dt, off):
        t = nc.alloc_sbuf_tensor_at(name, list(shape), dt, offset=off)
        return Buf(t, name)

    def nbytes(shape, dt):
        n = 1
        for s in shape[1:]:
            n *= s
        return n * (2 if dt == BF16 else 4)

    class Arena:
        def __init__(self, off, size):
            self.off, self.end, self.base = off, off + size, off

        def alloc(self, name, shape, dt):
            b = sb(name, shape, dt, self.off)
            b.off = self.off
            self.off += (nbytes(shape, dt) + 31) // 32 * 32
            assert self.off <= self.end, (name, self.off, self.end)
            return b

        def reset(self):
            self.off = self.base

    BASE = 16512
    XT_OFF, HT_OFF, R_OFF, S_OFF = BASE, BASE + 65536, BASE + 98304, BASE + 196608
    xT = sb("xT", [128, NC_, T], F32, XT_OFF)
    xT_res = [[Res("xT%d_%d" % (c, h)) for h in range(2)] for c in range(NC_)]
    hT = sb("hT", [128, NC_, T], BF16, HT_OFF)
    hT_res = [[Res("hT%d_%d" % (c, h)) for h in range(2)] for c in range(NC_)]
    SA = Arena(S_OFF, 229344 - S_OFF)
    ident = SA.alloc("ident", [128, 128], F32)
    ones = SA.alloc("ones", [128, 128], F32)
    identb = SA.alloc("identb", [128, 128], BF16)
    onesb = SA.alloc("onesb", [128, 128], BF16)
    gains = SA.alloc("gains", [128, 80], F32)
    sgains = SA.alloc("sgains", [128, 4], F32)
    sqb = [SA.alloc("sq%d" % i, [128, 512], F32) for i in range(2)]
    rstd = SA.alloc("rstd", [128, 512], F32)
    rstds = [rstd, SA.alloc("rstd_b", [128, 512], F32)]
    sgb = [SA.alloc("sg%d" % i, [128, 512], F32) for i in range(2)]
    small = SA.alloc("small", [128, 64], F32)
    RA = Arena(R_OFF, S_OFF - R_OFF)

    pb = []
    for i in range(7):
        pb.append(Buf(nc.alloc_psum_tensor("pb%d" % i, [128, 512], F32), "pb%d" % i))
    pbf = Buf(nc.alloc_psum_tensor("pbf", [128, 1024], BF16), "pbf")

    cds = P.dsem("const")
    HS = [slice(0, 512), slice(512, 1024)]

    def dma(eng, out, in_, ds):
        return eng.dma_start(out=out, in_=in_).then_inc(ds.sem, 16)

    def ld_consts(e):
        dma(e, ident.t[:], ident_d, cds)
        dma(e, gains.t[:], gains_d, cds)
        return dma(e, sgains.t[:], sg_d, cds)
    P.op("sp", ld_consts, w=[ident.res, gains.res, sgains.res], dsem=cds, ndma=3)
    cdsb = P.dsem("constb")
    P.op("pool", lambda e: dma(e, identb.t[:], ident_d, cdsb), w=[identb.res], dsem=cdsb, ndma=1)
    P.op("dve", lambda e: e.memset(ones.t[:], 1.0), w=[ones.res])
    P.op("dve", lambda e: e.memset(onesb.t[:], 1.0), w=[onesb.res])

    def transpose_in(src_d, ntile, dst, dst_res_fn, stg, tcols):
        k = 0
        for i in range(ntile):
            st = stg[i % 2]
            P.op("sp", lambda e, i=i, st=st: dma(e, st.t[:], src_d[i * 128:(i + 1) * 128, :], st.ds),
                 w=[st.res], dsem=st.ds, ndma=1)
            for g4 in range(4):
                ps = pb[k % 4]
                k += 1

                def tr(e, st=st, ps=ps, g4=g4):
                    for q in range(4):
                        c = g4 * 4 + q
                        ins = e.transpose(out=ps.t[:, q * 128:(q + 1) * 128], in_=st.t[:, c * 128:(c + 1) * 128],
                                          identity=ident.t[:])
                    return ins
                P.op("pe", tr, r=[st.res, ident.res], w=[ps.res])
                dst_ap = dst.t[:, g4 * 4:(g4 + 1) * 4, i * 128:(i + 1) * 128]
                src_ap = ps.t[:].rearrange("p (q t) -> p q t", q=4)
                wres = dst_res_fn(g4, i)
                if k % 2 == 0:
                    P.op("dve", lambda e, a=dst_ap, b=src_ap: e.tensor_copy(out=a, in_=b), r=[ps.res], w=wres)
                else:
                    P.op("act", lambda e, a=dst_ap, b=src_ap: e.copy(out=a, in_=b), r=[ps.res], w=wres)

    def stat_rstd(ps, dim):
        n = 512
        P.op("act", lambda e: e.activation(out=rstd.t[:, :n], in_=ps.t[:, :n], func=AF.Ln, bias=EPS,
                                           scale=1.0 / dim), r=[ps.res], w=[rstd.res])
        P.op("act", lambda e: e.activation(out=rstd.t[:, :n], in_=rstd.t[:, :n], func=AF.Exp, scale=-0.5),
             r=[rstd.res], w=[rstd.res])

    sqh = [sb("sqh%d" % i, [128, 512], BF16, sqb[i // 2].off + (i % 2) * 1024) for i in range(4)]

    def rms_norm_T(gcol, ncols=T):
        for h in range(2):
            ps = pb[6 - h]
            for c in range(NC_):
                sq = sqh[c % 4]
                if c % 2 == 0:
                    P.op("dve", lambda e, sq=sq, c=c, h=h: e.tensor_tensor(out=sq.t[:], in0=xT.t[:, c, HS[h]],
                                                                          in1=xT.t[:, c, HS[h]], op=ALU.mult),
                         r=[xT_res[c][h]], w=[sq.res])
                else:
                    P.op("act", lambda e, sq=sq, c=c, h=h: e.activation(out=sq.t[:], in_=xT.t[:, c, HS[h]],
                                                                       func=AF.Square),
                         r=[xT_res[c][h]], w=[sq.res])
                P.op("pe", lambda e, sq=sq, c=c, ps=ps: e.matmul(ps.t[:], onesb.t[:], sq.t[:], start=(c == 0),
                                                                stop=(c == NC_ - 1)),
                     r=[sq.res, onesb.res], w=[ps.res])
        for h in range(2):
            ps, rs = pb[6 - h], rstds[h]
            P.op("act", lambda e, ps=ps, rs=rs: e.activation(out=rs.t[:], in_=ps.t[:], func=AF.Ln, bias=EPS,
                                                             scale=1.0 / D), r=[ps.res], w=[rs.res])
            P.op("act", lambda e, rs=rs: e.activation(out=rs.t[:], in_=rs.t[:], func=AF.Exp, scale=-0.5),
                 r=[rs.res], w=[rs.res])
        for h in range(2):
            rs = rstds[h]
            for c in range(NC_):
                P.op("dve", lambda e, c=c, h=h, rs=rs: e.scalar_tensor_tensor(
                    out=hT.t[:, c, HS[h]], in0=xT.t[:, c, HS[h]], scalar=gains.t[:, gcol + c:gcol + c + 1],
                    in1=rs.t[:], op0=ALU.mult, op1=ALU.mult),
                    r=[xT_res[c][h], rs.res, gains.res], w=[hT_res[c][h]])

    def rms_norm_half(gcol, h):
        ps, rs = pb[6 - h], rstds[h]
        for c in range(NC_):
            sq = sqh[c % 4]
            if c % 2 == 0:
                P.op("dve", lambda e, sq=sq, c=c: e.tensor_tensor(out=sq.t[:], in0=xT.t[:, c, HS[h]],
                                                                  in1=xT.t[:, c, HS[h]], op=ALU.mult),
                     r=[xT_res[c][h]], w=[sq.res])
            else:
                P.op("act", lambda e, sq=sq, c=c: e.activation(out=sq.t[:], in_=xT.t[:, c, HS[h]],
                                                               func=AF.Square),
                     r=[xT_res[c][h]], w=[sq.res])
            P.op("pe", lambda e, sq=sq, c=c: e.matmul(ps.t[:], onesb.t[:], sq.t[:], start=(c == 0),
                                                      stop=(c == NC_ - 1)),
                 r=[sq.res, onesb.res], w=[ps.res])
        P.op("act", lambda e: e.activation(out=rs.t[:], in_=ps.t[:], func=AF.Ln, bias=EPS, scale=1.0 / D),
             r=[ps.res], w=[rs.res])
        P.op("act", lambda e: e.activation(out=rs.t[:], in_=rs.t[:], func=AF.Exp, scale=-0.5),
             r=[rs.res], w=[rs.res])
        for c in range(NC_):
            P.op("dve", lambda e, c=c: e.scalar_tensor_tensor(
                out=hT.t[:, c, HS[h]], in0=xT.t[:, c, HS[h]], scalar=gains.t[:, gcol + c:gcol + c + 1],
                in1=rs.t[:], op0=ALU.mult, op1=ALU.mult),
                r=[xT_res[c][h], rs.res, gains.res], w=[hT_res[c][h]])

    hn_k = [0]

    def head_norm(src_ap, src_res, gidx, out_ap, out_res, n=512, extra_r=()):
        hn_k[0] += 1
        sq = sqb[hn_k[0] % 2]
        ps = pb[5 + hn_k[0] % 2]
        rstd = rstds[hn_k[0] % 2]
        P.op("dve", lambda e: e.tensor_tensor(out=sq.t[:, :n], in0=src_ap, in1=src_ap, op=ALU.mult),
             r=[src_res], w=[sq.res])
        P.op("pe", lambda e: e.matmul(ps.t[:, :n], ones.t[:], sq.t[:, :n], start=True, stop=True),
             r=[sq.res, ones.res], w=[ps.res])
        P.op("act", lambda e: e.activation(out=rstd.t[:, :n], in_=ps.t[:, :n], func=AF.Ln, bias=EPS,
                                           scale=1.0 / 128), r=[ps.res], w=[rstd.res])
        P.op("act", lambda e: e.activation(out=rstd.t[:, :n], in_=rstd.t[:, :n], func=AF.Exp, scale=-0.5),
             r=[rstd.res], w=[rstd.res])
        P.op("dve", lambda e: e.scalar_tensor_tensor(out=out_ap, in0=src_ap, scalar=sgains.t[:, gidx:gidx + 1],
                                                     in1=rstd.t[:, :n], op0=ALU.mult, op1=ALU.mult),
             r=[src_res, rstd.res, sgains.res] + list(extra_r), ww=[out_res])

    def fill_cols(slot, W, c0, n, dst0=0, parts=4, nchunk=NC_):
        step = nchunk // parts

        def f(e):
            for k in range(parts):
                ins = dma(e, slot.t[:, k * step:(k + 1) * step, dst0:dst0 + n],
                          W.ap[k * step * 128:(k + 1) * step * 128, c0:c0 + n].rearrange("(c p) n -> p c n", p=128),
                          slot.ds)
            return ins
        P.op("pool", f, r=[W.res], w=[slot.res], dsem=slot.ds, ndma=parts)

    stg = [RA.alloc("stg%d" % i, [128, D], F32) for i in range(2)]
    for s in stg:
        s.ds = P.dsem(s.res.name)
    transpose_in(x_d, 8, xT, lambda g4, i: [xT_res[c][i // 4] for c in range(g4 * 4, g4 * 4 + 4)], stg, T)
    P.barrier()
    RA.reset()

    def ffn(wg, wu, wd, gcol, tag, tail=None):
        RA.reset()
        ring = [RA.alloc("ring%s%d" % (tag, i), [128, 8192], BF16) for i in range(5)]
        for s in ring:
            s.ds = P.dsem(s.res.name)
        act = [RA.alloc("act%s%d" % (tag, i), [128, 4, T], BF16) for i in range(2)]
        act_res = [[[Res("act%d_%d_%d" % (b, j, h)) for h in range(2)] for j in range(4)] for b in range(2)]
        rms_norm_T(gcol)
        NG = DFF // 512
        fills = []
        for g in range(NG):
            fills.append(("g", g))
            fills.append(("u", g))
            if g >= 1:
                fills.append(("d", g - 1))
        fills.append(("d", NG - 1))
        slot_of = {}
        issued = [0]

        def issue_upto(n):
            while issued[0] < min(n, len(fills)):
                kind, g = fills[issued[0]]
                slot = ring[issued[0] % 5]
                slot_of[(kind, g)] = slot
                if kind == "d":
                    def f(e, slot=slot, g=g):
                        for k in range(4):
                            ins = dma(e, slot.t[:, k * 2048:(k + 1) * 2048],
                                      wd.ap[(g * 4 + k) * 128:(g * 4 + k + 1) * 128, :], slot.ds)
                        return ins
                    P.op("pool", f, r=[wd.res], w=[slot.res], dsem=slot.ds, ndma=4)
                else:
                    W = wg if kind == "g" else wu

                    def f(e, slot=slot, g=g, W=W):
                        for k in range(4):
                            ins = dma(e, slot.t[:, k * 2048:(k + 1) * 2048].rearrange("p (c n) -> p c n", c=4),
                                      W.ap[k * 512:(k + 1) * 512, g * 512:(g + 1) * 512].rearrange(
                                          "(c p) n -> p c n", p=128), slot.ds)
                        return ins
                    P.op("pool", f, r=[W.res], w=[slot.res], dsem=slot.ds, ndma=4)
                issued[0] += 1
        gu_bank = [pb[0], pb[1], pb[2], pb[3]]
        dn_bank = [pb[4], pb[5]]
        kk = [0, 0]

        def gate_up(g):
            G = slot_of[("g", g)]
            U = slot_of[("u", g)]
            ab = g % 2
            for j in range(4):
                for h in range(2):
                    pg = gu_bank[(kk[0] % 2) * 2]
                    pu = gu_bank[(kk[0] % 2) * 2 + 1]
                    kk[0] += 1
                    sg = sgb[kk[0] % 2]

                    def mm(e, S, ps, j=j, h=h):
                        for c in range(NC_):
                            ins = e.matmul(ps.t[:], S.t[:, c * 512 + j * 128:c * 512 + (j + 1) * 128],
                                           hT.t[:, c, HS[h]], start=(c == 0), stop=(c == NC_ - 1))
                        return ins
                    hr = [hT_res[c][h] for c in range(NC_)]
                    P.op("pe", lambda e, G=G, pg=pg, mm=mm: mm(e, G, pg), r=hr + [G.res], w=[pg.res])
                    P.op("pe", lambda e, U=U, pu=pu, mm=mm: mm(e, U, pu), r=hr + [U.res], w=[pu.res])
                    P.op("act", lambda e, sg=sg, pg=pg: e.activation(out=sg.t[:], in_=pg.t[:], func=AF.Silu),
                         r=[pg.res], w=[sg.res])
                    P.op("dve", lambda e, sg=sg, pu=pu, ab=ab, j=j, h=h: e.tensor_tensor(
                        out=act[ab].t[:, j, HS[h]], in0=sg.t[:], in1=pu.t[:], op=ALU.mult),
                        r=[sg.res, pu.res], w=[act_res[ab][j][h]])

        def down(g, half_major=False):
            Dn = slot_of[("d", g)]
            ab = g % 2
            order = ([(dc, h) for h in range(2) for dc in range(NC_)] if half_major
                     else [(dc, h) for dc in range(NC_) for h in range(2)])
            for n_, (dc, h) in enumerate(order):
                if half_major and n_ == NC_:
                    tail(0, ring)
                if True:
                    ps = dn_bank[kk[1] % 2]
                    kk[1] += 1

                    def mm(e, ps=ps, dc=dc, h=h, Dn=Dn, ab=ab):
                        for j in range(4):
                            ins = e.matmul(ps.t[:], Dn.t[:, j * 2048 + dc * 128:j * 2048 + (dc + 1) * 128],
                                           act[ab].t[:, j, HS[h]], start=(j == 0), stop=(j == 3))
                        return ins
                    P.op("pe", mm, r=[Dn.res] + [act_res[ab][j][h] for j in range(4)], w=[ps.res])
                    P.op("dve", lambda e, ps=ps, dc=dc, h=h: e.scalar_tensor_tensor(
                        out=xT.t[:, dc, HS[h]], in0=ps.t[:], scalar=0.5, in1=xT.t[:, dc, HS[h]],
                        op0=ALU.mult, op1=ALU.add), r=[ps.res, xT_res[dc][h]], w=[xT_res[dc][h]])
        nf = 0
        issue_upto(5)
        for g in range(NG):
            gate_up(g)
            nf += 2
            issue_upto(nf + 5)
            if g >= 1:
                down(g - 1)
                nf += 1
                issue_upto(nf + 5)
        if tail is None:
            down(NG - 1)
            P.barrier()
        else:
            down(NG - 1, half_major=True)
            tail(1, ring)

    mix_norm_done = [False]
    if stop_after >= 1 and not SKIP_FFN:
        if stop_after >= 1.1:
            ffn(*w_ffn[0], 0, "a", tail=lambda half, ring: rms_norm_half(16, half))
            P.barrier()
            mix_norm_done[0] = True
        else:
            ffn(*w_ffn[0], 0, "a")

    def chk(v):
        if stop_after < v:
            raise _Stop()

    def mixer():
        RA.reset()
        wsl = [RA.alloc("wsl%d" % i, [128, NC_, 256], BF16) for i in range(3)]
        for s in wsl:
            s.ds = P.dsem(s.res.name)
        stash = []
        for h in range(4):
            stash.append(dict(
                qd=RA.alloc("qd%d" % h, [128, T], BF16), ki=RA.alloc("ki%d" % h, [128, T], BF16),
                kt=RA.alloc("kt%d" % h, [128, 8, 128], BF16), v=RA.alloc("v%d" % h, [128, 8, 256], BF16),
                sr=RA.alloc("sr%d" % h, [128, 8, 256], BF16)))
        TA = Arena(RA.off, RA.end - RA.off)
        if not mix_norm_done[0]:
            rms_norm_T(16)
        wi = [0]

        def next_slot():
            s = wsl[wi[0] % 3]
            wi[0] += 1
            return s
        hr = [[hT_res[c][h] for c in range(NC_)] for h in range(2)]
        hr_all = hr[0] + hr[1]

        def projT(slot, col, ps, h, n=128):
            def mm(e):
                for c in range(NC_):
                    ins = e.matmul(ps.t[:n, :], slot.t[:, c, col:col + n], hT.t[:, c, HS[h]],
                                   start=(c == 0), stop=(c == NC_ - 1))
                return ins
            P.op("pe", mm, r=hr[h] + [slot.res], w=[ps.res])

        def projTok(slot, col, n, ps, pcol, i):
            def mm(e):
                for c in range(NC_):
                    ins = e.matmul(ps.t[:, pcol:pcol + n], hT.t[:, c, i * 128:(i + 1) * 128],
                                   slot.t[:, c, col:col + n], start=(c == 0), stop=(c == NC_ - 1))
                return ins
            P.op("pe", mm, r=hr[i // 4] + [slot.res], w=[ps.res])

        ststage = TA.alloc("ststage", [128, FST], F32)
        TA_mark = TA.off
        raws = [TA.alloc("raw%d" % i, [128, 512], F32) for i in range(2)]
        knfs = [TA.alloc("knf%d" % i, [128, 512], F32) for i in range(2)]
        knb = [TA.alloc("knb%d" % i, [128, 512], BF16) for i in range(2)]
        vst = [TA.alloc("vst%d" % i, [128, 2, 130], BF16) for i in range(2)]
        kmown = SA.alloc("kmown", [128, 8, 4], F32)
        for b in knb + vst:
            b.ds = P.dsem(b.res.name)
        for b in vst:
            P.op("dve", lambda e, b=b: e.memset(b.t[:, :, 128:129], 1.0), w=[b.res])
            P.op("dve", lambda e, b=b: e.memset(b.t[:, :, 129:130], 0.0), w=[b.res])
        kvs = Res("kv_src")
        sts = Res("st_src")
        qnr = Res("qn_dram")
        kqc = [0]

        def m1_qk(which, ocol, gidx):
            its_ = [(hp, hh, h) for hp in range(4) for hh in range(2) for h in range(2)]
            slots, stt = {}, {}

            def stage_a(n):
                hp, hh, h = its_[n]
                if hp not in slots:
                    slots[hp] = next_slot()
                    fill_cols(slots[hp], win_d, ocol + hp * 256, 256)
                kq = kqc[0]
                kqc[0] += 1
                ps, raw = pb[kq % 2], raws[kq % 2]
                projT(slots[hp], hh * 128, ps, h)
                P.op("act", lambda e: e.copy(out=raw.t[:], in_=ps.t[:]), r=[ps.res], w=[raw.res])
                stt[n] = (raw, knfs[kq % 2], knb[kq % 2])

            def stage_b(n):
                hp, hh, h = its_[n]
                hd = hp * 2 + hh
                raw, knf, ob = stt[n]
                if which == "q":
                    head_norm(raw.t[:], raw.res, gidx, ob.t[:], ob.res)
                    P.op("sp", lambda e: dma(
                        e, qn_dram.ap()[:, hd * 1024 + h * 512:hd * 1024 + (h + 1) * 512], ob.t[:], ob.ds),
                        r=[ob.res], ww=[qnr], dsem=ob.ds, ndma=1)
                else:
                    head_norm(raw.t[:], raw.res, gidx, knf.t[:], knf.res)
                    P.op("dve", lambda e: e.tensor_reduce(
                        out=kmown.t[:, hd, h * 2:h * 2 + 2], in_=knf.t[:].rearrange("p (b k) -> p b k", b=2),
                        axis=AX.X, op=ALU.add), r=[knf.res], ww=[kmown.res])
                    P.op("act", lambda e: e.copy(out=ob.t[:], in_=knf.t[:]), r=[knf.res], w=[ob.res])
                    P.op("sp", lambda e: dma(
                        e, ksrc[hd // 4].ap()[:, (hd % 4) * 1024 + h * 512:(hd % 4) * 1024 + (h + 1) * 512],
                        ob.t[:], ob.ds), r=[ob.res], ww=[kvs], dsem=ob.ds, ndma=1)
            stage_a(0)
            for n in range(len(its_)):
                if n + 1 < len(its_):
                    stage_a(n + 1)
                stage_b(n)
        m1_qk("k", O_MK, 1)
        kq = 0
        for hp in range(4):
            slot = next_slot()
            fill_cols(slot, win_d, O_MV + hp * 256, 256)
            for i in range(8):
                ps = pb[kq % 2]
                kq += 1
                projTok(slot, 0, 256, ps, 0, i)
                vb = vst[kq % 2]
                P.op("act", lambda e, ps=ps, vb=vb: e.copy(out=vb.t[:, :, 0:128],
                                                            in_=ps.t[:, 0:256].rearrange("p (a c) -> p a c", a=2)),
                     r=[ps.res], w=[vb.res])
                P.op("sp", lambda e, vb=vb, hp=hp, i=i: dma(
                    e, vsrc[hp].ap().rearrange("p (h i c) -> p h i c", h=2, i=8)[:, :, i, :],
                    vb.t[:], vb.ds), r=[vb.res], ww=[kvs], dsem=vb.ds, ndma=1)
        P.op("dve", lambda e: e.memset(ststage.t[:, 1024:FST], 0.0), w=[ststage.res])
        P.op("dve", lambda e: e.tensor_scalar(out=kmown.t[:], in0=kmown.t[:], scalar1=1.0 / 256, scalar2=None,
                                              op0=ALU.mult), r=[kmown.res], w=[kmown.res])
        P.op("dve", lambda e: e.tensor_copy(out=ststage.t[:, 1028:1060].rearrange("p (h b) -> p h b", h=8),
                                            in_=kmown.t[:]), r=[kmown.res], w=[ststage.res])
        pre_glr = next_slot()
        fill_cols(pre_glr, win_d, O_GLR, 16)
        pre_sr0 = next_slot()
        fill_cols(pre_sr0, win_d, O_GR, 256)
        P.barrier()
        kvd = Res("kv_dst")
        RG = [[0, 1, 2, 3], [4, 5, 6, 7]]
        for src_, dst_ in list(zip(ksrc, kdst)) + list(zip(vsrc, vdst)):
            cs_ = P.dsem("cc_" + src_.name, inc=1)
            P.op("pool", lambda e, src_=src_, dst_=dst_, cs_=cs_: e.collective_compute(
                "AllGather", ALU.bypass, replica_groups=RG, ins=[src_.ap().opt()],
                outs=[dst_.ap().opt()]).then_inc(cs_.sem), r=[kvs], ww=[kvd], dsem=cs_, ndma=1)
        if stop_after < 1.3:
            return
        TA.off = TA_mark
        glrT = TA.alloc("glrT", [16, T], F32)
        spb = TA.alloc("sp", [128, 512], F32)
        e12 = TA.alloc("e12", [128, 512], F32)
        e2b = TA.alloc("e2b", [128, 512], F32)
        e3 = sb("e3", [128, 512], F32, sqb[1].off)
        wg2 = sb("wg2", [16, 512], F32, sgb[0].off)
        bg2 = sb("bg2", [1, 512], F32, sgb[1].off)
        dcy = SA.alloc("dcy", [128, 4, 8], F32)
        triN = TA.alloc("triN", [128, 128], F32)
        triS = TA.alloc("triS", [128, 128], F32)

        def ld2(e):
            dma(e, wg2.t[:], wg2_d, cds2)
            dma(e, bg2.t[:], bg2_d, cds2)
            dma(e, triN.t[:], triN_d, cds2)
            return dma(e, triS.t[:], triS_d, cds2)
        cds2 = P.dsem("const2")
        P.op("sp", ld2, w=[wg2.res, bg2.res, triN.res, triS.res], dsem=cds2, ndma=4)
        slot = pre_glr
        for h in range(2):
            ps = pb[h]
            projT(slot, 0, ps, h, n=16)
            P.op("act", lambda e, ps=ps, h=h: e.copy(out=glrT.t[:, HS[h]], in_=ps.t[:16, :]), r=[ps.res],
                 w=[glrT.res])
        LNS = math.log(128 ** -0.5)
        chk(1.31)
        for hd in range(4):
            st = stash[hd]
            if hd == 0:
                sr_ = pre_sr0
            else:
                sr_ = next_slot()
                fill_cols(sr_, win_d, O_GR + hd * 256, 256)
            sqk = next_slot()
            fill_cols(sqk, win_d, O_GQ + hd * 128, 128, dst0=0)
            fill_cols(sqk, win_d, O_GK + hd * 128, 128, dst0=128)
            sv = next_slot()
            fill_cols(sv, win_d, O_GV + hd * 256, 256)
            for t2 in range(4):
                pr = pb[5] if t2 % 2 == 0 else pb[6]
                for a in range(2):
                    projTok(sr_, 0, 256, pr, a * 256, t2 * 2 + a)
                P.op("act", lambda e, st=st, pr=pr, t2=t2: e.activation(
                    out=st["sr"].t[:, t2 * 2:t2 * 2 + 2, :],
                    in_=pr.t[:].rearrange("p (a c) -> p a c", a=2), func=AF.Silu),
                    r=[pr.res], ww=[st["sr"].res])
            for h in range(2):
                pla, pbc, ptl, pq, pk = pb[0], pb[1], pb[2], pb[3], pb[4]

                def mm_la(e, h=h, hd=hd):
                    for q in range(4):
                        i = h * 4 + q
                        e.matmul(pla.t[:, q * 128:(q + 1) * 128], glrT.t[:, i * 128:(i + 1) * 128],
                                 wg2.t[:, hd * 128:(hd + 1) * 128], start=True, stop=False)
                        ins = e.matmul(pla.t[:, q * 128:(q + 1) * 128], ones.t[0:1, :],
                                       bg2.t[:, hd * 128:(hd + 1) * 128], start=False, stop=True)
                    return ins
                P.op("pe", mm_la, r=[glrT.res, wg2.res, bg2.res, ones.res], w=[pla.res])
                P.op("act", lambda e: e.activation(out=spb.t[:], in_=pla.t[:], func=AF.Exp, scale=-1.0),
                     r=[pla.res], w=[spb.res])
                P.op("act", lambda e: e.activation(out=spb.t[:], in_=spb.t[:], func=AF.Ln, bias=1.0),
                     r=[spb.res], w=[spb.res])

                chk(1.32)

                def mm_bc(e):
                    for q in range(4):
                        ins = e.matmul(pbc.t[:, q * 128:(q + 1) * 128], spb.t[:, q * 128:(q + 1) * 128], triN.t[:],
                                       start=True, stop=True)
                    return ins
                P.op("pe", mm_bc, r=[spb.res, triN.res], w=[pbc.res])

                def mm_tl(e):
                    for q in range(4):
                        ins = e.matmul(ptl.t[:, q * 128:(q + 1) * 128], triS.t[:], spb.t[:, q * 128:(q + 1) * 128],
                                       start=True, stop=True)
                    return ins
                P.op("pe", mm_tl, r=[spb.res, triS.res], w=[ptl.res])
                chk(1.321)
                projT(sqk, 0, pq, h)
                projT(sqk, 128, pk, h)
                chk(1.322)
                P.op("act", lambda e, hd=hd, h=h: e.activation(
                    out=dcy.t[:, hd, h * 4:(h + 1) * 4],
                    in_=pbc.t[:].rearrange("p (q t) -> p q t", q=4)[:, :, 127], func=AF.Exp),
                    r=[pbc.res], ww=[dcy.res])
                chk(1.323)
                P.op("act", lambda e: e.activation(out=e12.t[:], in_=pbc.t[:], func=AF.Exp, bias=LNS),
                     r=[pbc.res], w=[e12.res])
                P.op("dve", lambda e, st=st, h=h: e.tensor_tensor(out=st["qd"].t[:, HS[h]], in0=pq.t[:],
                                                                  in1=e12.t[:], op=ALU.mult),
                     r=[pq.res, e12.res], ww=[st["qd"].res])
                chk(1.324)
                P.op("act", lambda e: e.activation(out=e2b.t[:], in_=pbc.t[:], func=AF.Exp, scale=-1.0),
                     r=[pbc.res], w=[e2b.res])
                P.op("dve", lambda e, st=st, h=h: e.tensor_tensor(out=st["ki"].t[:, HS[h]], in0=pk.t[:],
                                                                  in1=e2b.t[:], op=ALU.mult),
                     r=[pk.res, e2b.res], ww=[st["ki"].res])
                P.op("act", lambda e: e.activation(out=e3.t[:], in_=ptl.t[:], func=AF.Exp), r=[ptl.res],
                     w=[e3.res])
                chk(1.33)
                for q in range(4):
                    i = h * 4 + q
                    pkk, pvv = pb[5], pb[6]
                    projTok(sqk, 128, 128, pkk, 0, i)
                    projTok(sv, 0, 256, pvv, 0, i)
                    P.op("dve", lambda e, st=st, i=i, q=q, pkk=pkk: e.tensor_tensor(
                        out=st["kt"].t[:, i, :], in0=pkk.t[:, 0:128], in1=e3.t[:, q * 128:(q + 1) * 128],
                        op=ALU.mult), r=[pkk.res, e3.res], ww=[st["kt"].res])
                    P.op("act", lambda e, st=st, i=i, pvv=pvv: e.copy(out=st["v"].t[:, i, :], in_=pvv.t[:, 0:256]),
                         r=[pvv.res], ww=[st["v"].res])
                chk(1.332)
            chk(1.34)
            S_ap = ststage.t[:, hd * 256:(hd + 1) * 256]
            P.op("dve", lambda e, S_ap=S_ap: e.memset(S_ap, 0.0), w=[ststage.res])
            P.op("dve", lambda e, hd=hd: e.memset(ststage.t[:, 1024 + hd:1025 + hd], 1.0), w=[ststage.res])
            for i in range(8):
                pu = pb[i % 2]
                P.op("pe", lambda e, st=st, i=i, pu=pu: e.matmul(pu.t[:, 0:256], st["kt"].t[:, i, :],
                                                                st["v"].t[:, i, :], start=True, stop=True),
                     r=[st["kt"].res, st["v"].res], w=[pu.res])
                P.op("dve", lambda e, S_ap=S_ap, hd=hd, i=i, pu=pu: e.scalar_tensor_tensor(
                    out=S_ap, in0=S_ap, scalar=dcy.t[:, hd, i:i + 1], in1=pu.t[:, 0:256], op0=ALU.mult,
                    op1=ALU.add), r=[pu.res, dcy.res, ststage.res], w=[ststage.res])
                P.op("dve", lambda e, hd=hd, i=i: e.tensor_tensor(
                    out=ststage.t[:, 1024 + hd:1025 + hd], in0=ststage.t[:, 1024 + hd:1025 + hd],
                    in1=dcy.t[:, hd, i:i + 1], op=ALU.mult), r=[dcy.res, ststage.res], w=[ststage.res])
        if stop_after < 1.5:
            P.barrier()
            return
        stds = P.dsem("stds")
        P.op("sp", lambda e: dma(e, st_src.ap(), ststage.t[:], stds), r=[ststage.res], w=[sts], dsem=stds, ndma=1)
        ccs = P.dsem("cc", inc=1)
        ccs2 = P.dsem("cc2", inc=1)
        std = Res("st_dst")

        P.op("pool", lambda e: e.collective_compute(
            "AllGather", ALU.bypass, replica_groups=RG, ins=[st_src.ap().opt()],
            outs=[st_dst.ap().opt()]).then_inc(ccs2.sem), r=[sts], w=[std], dsem=ccs2, ndma=1)
        P.barrier(exclude=[ccs2])
        m1_qk("q", O_MQ, 0)
        P.barrier()
        if stop_after < 1.7:
            return
        TB = Arena(TA_mark, TA.end - TA_mark)
        ldb = TB.alloc("ldb", [128, FST], F32)
        ldb.ds = P.dsem("ldb")
        mjt = SA.alloc("mjt", [128, 3], F32)
        gout = TB.alloc("gout", [128, 256], F32)
        attmask = TB.alloc("attmask", [128, 128], F32)
        cds3 = P.dsem("const3")
        P.op("sp", lambda e: [dma(e, mjt.t[:], mj_d, cds3), dma(e, gout.t[:], gout_d, cds3),
                              dma(e, attmask.t[:], attm_d, cds3)][-1],
             w=[mjt.res, gout.res, attmask.res], dsem=cds3, ndma=3)
        dpr = SA.alloc("dpr", [128, 4], F32)
        ltmp = TB.alloc("ltmp", [128, 256], F32)
        P.op("dve", lambda e: e.memset(ststage.t[:, 0:1024], 0.0), w=[ststage.res])
        for j in range(3):
            P.op("sp", lambda e, j=j: dma(e, ldb.t[:], st_dst.ap()[j * 128:(j + 1) * 128, :], ldb.ds),
                 r=[std], w=[ldb.res], dsem=ldb.ds, ndma=1)
            P.op("dve", lambda e, j=j: e.tensor_scalar(out=dpr.t[:], in0=ldb.t[:, 1024:1028], scalar1=-1.0,
                                                       scalar2=mjt.t[:, j:j + 1], op0=ALU.add, op1=ALU.mult),
                 r=[ldb.res, mjt.res], w=[dpr.res])
            P.op("dve", lambda e: e.tensor_scalar(out=dpr.t[:], in0=dpr.t[:], scalar1=1.0, scalar2=None,
                                                  op0=ALU.add), r=[dpr.res], w=[dpr.res])
            for hd in range(4):
                P.op("dve", lambda e, j=j, hd=hd: e.tensor_scalar(
                    out=ltmp.t[:], in0=ldb.t[:, hd * 256:(hd + 1) * 256], scalar1=mjt.t[:, j:j + 1], scalar2=None,
                    op0=ALU.mult), r=[ldb.res, mjt.res], w=[ltmp.res])
                P.op("dve", lambda e, hd=hd: e.scalar_tensor_tensor(
                    out=ststage.t[:, hd * 256:(hd + 1) * 256], in0=ststage.t[:, hd * 256:(hd + 1) * 256],
                    scalar=dpr.t[:, hd:hd + 1], in1=ltmp.t[:], op0=ALU.mult, op1=ALU.add),
                    r=[ltmp.res, dpr.res, ststage.res], w=[ststage.res])
        mixv = nc.alloc_sbuf_tensor_at("mix", [128, 8, D], BF16, offset=HT_OFF)
        mix_res = Res("mix")
        HA = Arena(wsl[0].off, 3 * 8192)
        Sbf = [HA.alloc("Sbf%d" % h, [128, 256], BF16) for h in range(4)]
        attb = [HA.alloc("attb%d" % h, [128, 128], BF16) for h in range(4)]
        osq = [HA.alloc("osq%d" % h, [128, 256], F32) for h in range(4)]
        otmp = [HA.alloc("otmp%d" % h, [128, 256], F32) for h in range(4)]
        ocp = [HA.alloc("ocp%d" % h, [128, 256], F32) for h in range(4)]
        smh = [HA.alloc("smh%d" % h, [128, 8], F32) for h in range(4)]
        Sres = [Res("S%d" % h) for h in range(4)]
        pa_res = [pb[h].res for h in range(4)]
        po_res = [pb[h].res for h in range(4)]
        pu_res = [pb[4 + h % 3].res for h in range(4)]
        def hv(hd):
            return dict(st=stash[hd], S_ap=ststage.t[:, hd * 256:(hd + 1) * 256], pa=pb[hd].t[:, 0:128],
                        po=pb[hd].t[:, 256:512],
                        pu=pb[4 + hd % 3].t[:, (hd // 3) * 256:(hd // 3) * 256 + 256], sm=smh[hd])
        HV = [hv(hd) for hd in range(4)]
        for i in range(8):
            first_r = [ststage.res] if i == 0 else []
            for hd in range(4):
                v_ = HV[hd]
                st, S_ap, pa, pu = v_["st"], v_["S_ap"], v_["pa"], v_["pu"]
                P.op("act", lambda e, S_ap=S_ap, hd=hd: e.copy(out=Sbf[hd].t[:], in_=S_ap),
                     r=[Sres[hd]] + first_r, w=[Sbf[hd].res])
                P.op("pe", lambda e, st=st, i=i, pa=pa: e.matmul(
                    pa, st["ki"].t[:, i * 128:(i + 1) * 128], st["qd"].t[:, i * 128:(i + 1) * 128],
                    start=True, stop=True), r=[st["ki"].res, st["qd"].res], w=[pa_res[hd]])
            for hd in range(4):
                v_ = HV[hd]
                st, pu = v_["st"], v_["pu"]
                P.op("pe", lambda e, st=st, i=i, pu=pu: e.matmul(pu, st["kt"].t[:, i, :], st["v"].t[:, i, :],
                                                                start=True, stop=True),
                     r=[st["kt"].res, st["v"].res], w=[pu_res[hd]])
            for hd in range(4):
                pa = HV[hd]["pa"]
                P.op("dve", lambda e, pa=pa, hd=hd: e.tensor_tensor(out=attb[hd].t[:], in0=pa, in1=attmask.t[:],
                                                                   op=ALU.mult),
                     r=[pa_res[hd], attmask.res], w=[attb[hd].res])
            for hd in range(4):
                v_ = HV[hd]
                st, po, pu, S_ap = v_["st"], v_["po"], v_["pu"], v_["S_ap"]

                def mm_o(e, st=st, i=i, po=po, hd=hd):
                    e.matmul(po, st["qd"].t[:, i * 128:(i + 1) * 128], Sbf[hd].t[:], start=True, stop=False)
                    return e.matmul(po, attb[hd].t[:], st["v"].t[:, i, :], start=False, stop=True)
                P.op("pe", mm_o, r=[st["qd"].res, Sbf[hd].res, attb[hd].res, st["v"].res], w=[po_res[hd]])
                P.op("dve", lambda e, S_ap=S_ap, hd=hd, i=i, pu=pu: e.scalar_tensor_tensor(
                    out=S_ap, in0=S_ap, scalar=dcy.t[:, hd, i:i + 1], in1=pu, op0=ALU.mult,
                    op1=ALU.add), r=[pu_res[hd], dcy.res, Sres[hd], Sbf[hd].res] + first_r, w=[Sres[hd]])
            for hd in range(4):
                po = HV[hd]["po"]
                P.op("act", lambda e, po=po, hd=hd: e.copy(out=ocp[hd].t[:], in_=po), r=[po_res[hd]],
                     w=[ocp[hd].res])
            for hd in range(4):
                sm = HV[hd]["sm"]
                P.op("dve", lambda e, hd=hd: e.tensor_tensor(out=osq[hd].t[:], in0=ocp[hd].t[:], in1=ocp[hd].t[:],
                                                             op=ALU.mult), r=[ocp[hd].res], w=[osq[hd].res])
                P.op("dve", lambda e, hd=hd, sm=sm: e.tensor_reduce(out=sm.t[:, 0:1], in_=osq[hd].t[:], axis=AX.X,
                                                                   op=ALU.add), r=[osq[hd].res], w=[sm.res])
            for hd in range(4):
                sm = HV[hd]["sm"]
                P.op("act", lambda e, sm=sm: e.activation(out=sm.t[:, 1:2], in_=sm.t[:, 0:1], func=AF.Ln, bias=EPS,
                                                          scale=1.0 / 256), r=[sm.res], w=[sm.res])
                P.op("act", lambda e, sm=sm: e.activation(out=sm.t[:, 2:3], in_=sm.t[:, 1:2], func=AF.Exp,
                                                          scale=-0.5), r=[sm.res], w=[sm.res])
            for hd in range(4):
                v_ = HV[hd]
                st, sm = v_["st"], v_["sm"]
                P.op("dve", lambda e, hd=hd, sm=sm: e.scalar_tensor_tensor(
                    out=otmp[hd].t[:], in0=ocp[hd].t[:], scalar=sm.t[:, 2:3], in1=gout.t[:], op0=ALU.mult,
                    op1=ALU.mult), r=[ocp[hd].res, sm.res, gout.res], w=[otmp[hd].res])
                P.op("dve", lambda e, st=st, i=i, hd=hd: e.tensor_tensor(
                    out=mixv[:, i, hd * 256:(hd + 1) * 256], in0=otmp[hd].t[:], in1=st["sr"].t[:, i, :],
                    op=ALU.mult), r=[otmp[hd].res, st["sr"].res], ww=[mix_res])
        P.barrier()
        if stop_after < 1.85:
            return
        RA.reset()
        M = RA
        pastb = M.alloc("pastb", [128, 8, 8, 16], F32)
        past01 = M.alloc("past01", [128, 8, 8, 16], F32)
        sel = M.alloc("sel", [128, 8, 8, 16], F32)
        gm = M.alloc("gm", [128, 128], F32)
        kmT = M.alloc("kmT", [128, 8, 16], F32)
        kmTb = M.alloc("kmTb", [128, 8, 16], BF16)
        triT = M.alloc("triT", [128, 128], BF16)
        oacc = [M.alloc("oacc%d" % i, [128, 2, 130], F32) for i in range(2)]
        ptb = [M.alloc("ptb%d" % i, [128, 2, 256], BF16) for i in range(2)]
        qload = [M.alloc("qall%d" % i, [128, T], BF16) for i in range(2)]
        ld = []
        for i in range(2):
            d_ = dict(q=qload[i], ktp=M.alloc("ktp%d" % i, [128, 3, T], BF16), kto=M.alloc("kto%d" % i, [128, T], BF16),
                      vp=M.alloc("vp%d" % i, [128, 3, 8, 130], BF16), vo=M.alloc("vo%d" % i, [128, 8, 130], BF16),
                      res=Res("ld%d" % i), ds=P.dsem("ld%d" % i))
            ld.append(d_)

        def ld4(e):
            dma(e, pastb.t[:], pastb_d.rearrange("p (a b c) -> p a b c", a=8, b=8), cds4)
            dma(e, past01.t[:], past01_d.rearrange("p (a b c) -> p a b c", a=8, b=8), cds4)
            for r in range(3):
                dma(e, kmT.t[:, :, r * 4:(r + 1) * 4],
                    st_dst.ap()[r * 128:(r + 1) * 128, 1028:1060].rearrange("p (h b) -> p h b", h=8), cds4)
            return dma(e, kmT.t[:, :, 12:16], kmown.t[:], cds4)
        cds4 = P.dsem("const4")
        cds5 = P.dsem("const5")
        P.op("sp", ld4, r=[std, kmown.res], w=[pastb.res, past01.res, kmT.res], dsem=cds4, ndma=6)
        P.op("pool", lambda e: dma(e, triT.t[:], triT_d, cds5), w=[triT.res], dsem=cds5, ndma=1)
        P.op("dve", lambda e: e.tensor_copy(out=kmTb.t[:], in_=kmT.t[:]), r=[kmT.res], w=[kmTb.res])

        def load_head(hd):
            L = ld[hd % 2]

            def f(e):
                dma(e, L["q"].t[:], qn_dram.ap()[:, hd * 1024:(hd + 1) * 1024], L["ds"])
                kc = slice((hd % 4) * 1024, (hd % 4 + 1) * 1024)
                vc = slice((hd % 2) * 1040, (hd % 2 + 1) * 1040)
                dma(e, L["kto"].t[:], ksrc[hd // 4].ap()[:, kc], L["ds"])
                dma(e, L["vo"].t[:], vsrc[hd // 2].ap()[:, vc].rearrange(
                    "p (i c) -> p i c", i=8), L["ds"])
                for r in range(3):
                    dma(e, L["ktp"].t[:, r, :], kdst[hd // 4].ap()[r * 128:(r + 1) * 128, kc], L["ds"])
                    ins = dma(e, L["vp"].t[:, r, :, :],
                              vdst[hd // 2].ap()[r * 128:(r + 1) * 128, vc].rearrange(
                                  "p (i c) -> p i c", i=8), L["ds"])
                return ins
            P.op("sp", f, r=[kvd, kvs, qnr], w=[L["res"]], dsem=L["ds"], ndma=9)
        SC = 128 ** -0.5
        ptb = ptb + [M.alloc("ptb%d" % i, [128, 2, 256], BF16) for i in range(2, 5)]
        mx8 = M.alloc("mx8b", [128, 8, 8], F32)
        gm3 = gm.t[:].rearrange("p (i n) -> p i n", i=8)

        def gate_ops(hd):
            L = ld[hd % 2]
            pg = pb[6]

            def mmg(e):
                for i in range(8):
                    ins = e.matmul(pg.t[:, i * 16:(i + 1) * 16], L["q"].t[:, i * 128:(i + 1) * 128],
                                   kmTb.t[:, hd, :], start=True, stop=True)
                return ins
            P.op("pe", mmg, r=[L["res"], kmTb.res], w=[pg.res])
            P.op("dve", lambda e: e.tensor_tensor(
                out=gm3, in0=pg.t[:, 0:128].rearrange("p (i n) -> p i n", i=8), in1=pastb.t[:, :, hd, :],
                op=ALU.add), r=[pg.res, pastb.res], w=[gm.res])
            for i in range(8):
                P.op("dve", lambda e, i=i: e.max(out=mx8.t[:, i, :], in_=gm.t[:, i * 16:(i + 1) * 16]),
                     r=[gm.res], ww=[mx8.res])
            for i in range(8):
                P.op("dve", lambda e, i=i: e.scalar_tensor_tensor(
                    out=sel.t[:, i, hd, :], in0=gm.t[:, i * 16:(i + 1) * 16], scalar=mx8.t[:, i, 2:3],
                    in1=past01.t[:, i, hd, :], op0=ALU.is_ge, op1=ALU.mult),
                    r=[gm.res, mx8.res, past01.res], ww=[sel.res])

        tasks = []
        for hd in range(8):
            for jb in range(4):
                blocks = [("p", r, j, r * 4 + j) for r in range(3) for j in range(4)]
                blocks += [("o", 0, j, 12 + j) for j in range(jb)]
                blocks += [("d", 0, jb, -1)]
                for bi, (kind, r, j, col) in enumerate(blocks):
                    tasks.append(dict(hd=hd, jb=jb, kind=kind, r=r, j=j, col=col, first=(bi == 0),
                                      last=(bi == len(blocks) - 1)))
        SK, NS, NT, NU = 3, 3, 5, 3

        def emit_s(t):
            tk = tasks[t]
            L = ld[tk["hd"] % 2]
            psS, pt = pb[t % NS], ptb[t % NT]
            jb, kind, r, j = tk["jb"], tk["kind"], tk["r"], tk["j"]
            Q = L["q"].t[:, jb * 256:(jb + 1) * 256]
            KT = L["ktp"].t[:, r, j * 256:(j + 1) * 256] if kind == "p" else L["kto"].t[:, j * 256:(j + 1) * 256]

            def mm_s(e):
                for k in range(2):
                    ins = e.matmul(psS.t[:, k * 256:(k + 1) * 256], KT[:, k * 128:(k + 1) * 128], Q,
                                   start=True, stop=True)
                return ins
            P.op("pe", mm_s, r=[L["res"]], w=[psS.res])
            P.op("act", lambda e: e.activation(out=pt.t[:].rearrange("p a b -> p (a b)"), in_=psS.t[:],
                                               func=AF.Exp, scale=SC), r=[psS.res], w=[pt.res])
            if kind == "d":
                P.op("dve", lambda e: e.tensor_tensor(out=pt.t[:, 0, 0:128], in0=pt.t[:, 0, 0:128],
                                                      in1=triT.t[:], op=ALU.mult),
                     r=[pt.res, triT.res], w=[pt.res])
                P.op("dve", lambda e: e.tensor_tensor(out=pt.t[:, 1, 128:256], in0=pt.t[:, 1, 128:256],
                                                      in1=triT.t[:], op=ALU.mult),
                     r=[pt.res, triT.res], w=[pt.res])

        def emit_u(t):
            tk = tasks[t]
            hd, jb, kind, r, j, col = tk["hd"], tk["jb"], tk["kind"], tk["r"], tk["j"], tk["col"]
            L = ld[hd % 2]
            pt, psU = ptb[t % NT], pb[3 + t % NU]
            oa = oacc[jb % 2]
            if kind == "p":
                V = [L["vp"].t[:, r, j * 2 + k, :] for k in range(2)]
            else:
                V = [L["vo"].t[:, j * 2 + k, :] for k in range(2)]

            def mm_u(e):
                for a in range(2):
                    ks = [0] if (kind == "d" and a == 0) else [0, 1]
                    for n_, k in enumerate(ks):
                        ins = e.matmul(psU.t[:, a * 256:a * 256 + 130], pt.t[:, k, a * 128:(a + 1) * 128],
                                       V[k], start=(n_ == 0), stop=(n_ == len(ks) - 1))
                return ins
            P.op("pe", mm_u, r=[pt.res, L["res"]], w=[psU.res])
            for a in range(2):
                i = jb * 2 + a
                sc = 1.0 if kind == "d" else sel.t[:, i, hd, col:col + 1]
                if tk["first"]:
                    P.op("dve", lambda e, a=a, sc=sc: e.tensor_scalar(
                        out=oa.t[:, a, :], in0=psU.t[:, a * 256:a * 256 + 130], scalar1=sc, scalar2=None,
                        op0=ALU.mult), r=[psU.res, sel.res], w=[oa.res])
                else:
                    P.op("dve", lambda e, a=a, sc=sc: e.scalar_tensor_tensor(
                        out=oa.t[:, a, :], in0=psU.t[:, a * 256:a * 256 + 130], scalar=sc, in1=oa.t[:, a, :],
                        op0=ALU.mult, op1=ALU.add), r=[psU.res, sel.res, oa.res], w=[oa.res])
            if tk["last"]:
                for a in range(2):
                    i = jb * 2 + a
                    P.op("dve", lambda e, a=a: e.reciprocal(out=small.t[:, 8 + a:9 + a], in_=oa.t[:, a, 128:129]),
                         r=[oa.res], w=[small.res])
                    P.op("dve", lambda e, a=a, i=i: e.tensor_scalar(
                        out=mixv[:, i, 1024 + hd * 128:1024 + (hd + 1) * 128], in0=oa.t[:, a, 0:128],
                        scalar1=small.t[:, 8 + a:9 + a], scalar2=None, op0=ALU.mult),
                        r=[oa.res, small.res], ww=[mix_res])

        load_head(0)
        gate_ops(0)
        load_head(1)
        for t in range(len(tasks) + SK):
            if t < len(tasks):
                tk = tasks[t]
                if tk["jb"] == 2 and tk["first"] and tk["hd"] + 1 < 8:
                    gate_ops(tk["hd"] + 1)
                emit_s(t)
            u = t - SK
            if u >= 0:
                emit_u(u)
                tk = tasks[u]
                if tk["jb"] == 3 and tk["last"] and tk["hd"] + 2 < 8:
                    load_head(tk["hd"] + 2)
        P.barrier()
        if stop_after < 1.95:
            return
        RA.reset()
        mixT = RA.alloc("mixT", [128, NC_, T], BF16)
        mixT_res = Res("mixT")
        wsl2 = [RA.alloc("wso%d" % i, [128, NC_, 256], BF16) for i in range(3)]
        for s in wsl2:
            s.ds = P.dsem(s.res.name)
        k5 = 0
        for i in range(8):
            for g4 in range(4):
                def tr(e, i=i, g4=g4):
                    for q in range(4):
                        c = g4 * 4 + q
                        ins = e.transpose(out=pbf.t[:, q * 128:(q + 1) * 128], in_=mixv[:, i, c * 128:(c + 1) * 128],
                                          identity=identb.t[:])
                    return ins
                P.op("pe", tr, r=[mix_res, identb.res], w=[pbf.res])
                dst_ap = mixT.t[:, g4 * 4:(g4 + 1) * 4, i * 128:(i + 1) * 128]
                src_ap = pbf.t[:, 0:512].rearrange("p (q t) -> p q t", q=4)
                k5 += 1
                if k5 % 2:
                    P.op("dve", lambda e, a=dst_ap, b=src_ap: e.tensor_copy(out=a, in_=b), r=[pbf.res], ww=[mixT_res])
                else:
                    P.op("act", lambda e, a=dst_ap, b=src_ap: e.copy(out=a, in_=b), r=[pbf.res], ww=[mixT_res])
        for g in range(8):
            slot = wsl2[g % 3]
            fill_cols(slot, wout_d, g * 256, 256)
            for dd in range(2):
                dc = g * 2 + dd
                for h in range(2):
                    ps = pb[k5 % 2]
                    k5 += 1

                    def mm(e, slot=slot, dd=dd, h=h, ps=ps):
                        for c in range(NC_):
                            ins = e.matmul(ps.t[:], slot.t[:, c, dd * 128:(dd + 1) * 128], mixT.t[:, c, HS[h]],
                                           start=(c == 0), stop=(c == NC_ - 1))
                        return ins
                    P.op("pe", mm, r=[slot.res, mixT_res], w=[ps.res])
                    P.op("dve", lambda e, ps=ps, dc=dc, h=h: e.tensor_tensor(
                        out=xT.t[:, dc, HS[h]], in0=ps.t[:], in1=xT.t[:, dc, HS[h]], op=ALU.add),
                        r=[ps.res, xT_res[dc][h]], w=[xT_res[dc][h]])
        P.barrier()

    if stop_after >= 1.1:
        try:
            mixer()
        except _Stop:
            P.barrier()

    def xattn():
        RA.reset()
        A = RA
        memT = A.alloc("memT", [128, NC_, 256], F32)
        memn = A.alloc("memn", [128, NC_, 256], BF16)
        xk = A.alloc("xk", [128, 4, 256], BF16)
        xv = A.alloc("xv", [128, 2, 4, 128], BF16)
        xq = A.alloc("xq", [128, 4, T], BF16)
        xo = A.alloc("xo", [128, 4, T], BF16)
        wsl = [A.alloc("wsx%d" % i, [128, NC_, 256], BF16) for i in range(3)]
        wos = A.alloc("wos", [128, 4, D], BF16)
        rawsx = [A.alloc("rawx%d" % i, [128, 512], F32) for i in range(2)]
        rden = A.alloc("rden", [128, 512], F32)
        ptx = [A.alloc("ptx%d" % i, [128, 512], BF16) for i in range(4)]
        mstg = [sb("mstg0", [128, D], F32, xq.off), sb("mstg1", [128, D], F32, xo.off)]
        for s in wsl + [wos] + mstg:
            s.ds = P.dsem(s.res.name)
        transpose_in(mem_d, 2, memT, lambda g4, i: [memT.res], mstg, 256)
        ps = pb[6]
        for c in range(NC_):
            sq = sqb[c % 2]
            P.op("dve", lambda e, sq=sq, c=c: e.tensor_tensor(out=sq.t[:, :256], in0=memT.t[:, c, :],
                                                              in1=memT.t[:, c, :], op=ALU.mult),
                 r=[memT.res], w=[sq.res])
            P.op("pe", lambda e, sq=sq, c=c: e.matmul(ps.t[:, :256], ones.t[:], sq.t[:, :256], start=(c == 0),
                                                      stop=(c == NC_ - 1)), r=[sq.res, ones.res], w=[ps.res])
        P.op("act", lambda e: e.activation(out=rstd.t[:, :256], in_=ps.t[:, :256], func=AF.Ln, bias=EPS,
                                           scale=1.0 / D), r=[ps.res], w=[rstd.res])
        P.op("act", lambda e: e.activation(out=rstd.t[:, :256], in_=rstd.t[:, :256], func=AF.Exp, scale=-0.5),
             r=[rstd.res], w=[rstd.res])
        for c in range(NC_):
            P.op("dve", lambda e, c=c: e.scalar_tensor_tensor(
                out=memn.t[:, c, :], in0=memT.t[:, c, :], scalar=gains.t[:, 48 + c:49 + c], in1=rstd.t[:, :256],
                op0=ALU.mult, op1=ALU.mult), r=[memT.res, rstd.res, gains.res], ww=[memn.res])
        kx = 0
        for hp in range(2):
            slot = wsl[kx % 3]
            kx += 1
            fill_cols(slot, wkv_d, hp * 256, 256)
            for hh in range(2):
                hd = hp * 2 + hh
                pk = pb[hh]

                def mm(e, slot=slot, hh=hh, pk=pk):
                    for c in range(NC_):
                        ins = e.matmul(pk.t[:, :256], slot.t[:, c, hh * 128:(hh + 1) * 128], memn.t[:, c, :],
                                       start=(c == 0), stop=(c == NC_ - 1))
                    return ins
                P.op("pe", mm, r=[slot.res, memn.res], w=[pk.res])
                raw = rawsx[hh]
                P.op("act", lambda e, pk=pk, raw=raw: e.copy(out=raw.t[:, :256], in_=pk.t[:, :256]), r=[pk.res],
                     w=[raw.res])
                head_norm(raw.t[:, :256], raw.res, 3, xk.t[:, hd, :], xk.res, n=256)
        for hp in range(2):
            slot = wsl[kx % 3]
            kx += 1
            fill_cols(slot, wkv_d, 512 + hp * 256, 256)
            for mt in range(2):
                pv = pb[2 + mt]

                def mm(e, slot=slot, mt=mt, pv=pv):
                    for c in range(NC_):
                        ins = e.matmul(pv.t[:, :256], memn.t[:, c, mt * 128:(mt + 1) * 128], slot.t[:, c, :],
                                       start=(c == 0), stop=(c == NC_ - 1))
                    return ins
                P.op("pe", mm, r=[slot.res, memn.res], w=[pv.res])
                P.op("act", lambda e, pv=pv, mt=mt, hp=hp: e.copy(
                    out=xv.t[:, mt, hp * 2:hp * 2 + 2, :], in_=pv.t[:, :256].rearrange("p (a c) -> p a c", a=2)),
                    r=[pv.res], ww=[xv.res])
        P.barrier()
        rms_norm_T(32)
        hr = [[hT_res[c][h] for c in range(NC_)] for h in range(2)]
        qits = [(hp, hh, h) for hp in range(2) for hh in range(2) for h in range(2)]
        qslots, qst = {}, {}

        def q_a(n):
            hp, hh, h = qits[n]
            if hp not in qslots:
                qslots[hp] = wsl[(kx + hp) % 3]
                fill_cols(qslots[hp], wq_d, hp * 256, 256)
            slot = qslots[hp]
            pq, raw = pb[n % 2], rawsx[n % 2]

            def mm(e):
                for c in range(NC_):
                    ins = e.matmul(pq.t[:], slot.t[:, c, hh * 128:(hh + 1) * 128], hT.t[:, c, HS[h]],
                                   start=(c == 0), stop=(c == NC_ - 1))
                return ins
            P.op("pe", mm, r=[slot.res] + hr[h], w=[pq.res])
            P.op("act", lambda e: e.copy(out=raw.t[:], in_=pq.t[:]), r=[pq.res], w=[raw.res])
            qst[n] = raw

        def q_b(n):
            hp, hh, h = qits[n]
            raw = qst[n]
            head_norm(raw.t[:], raw.res, 2, xq.t[:, hp * 2 + hh, HS[h]], xq.res)
        q_a(0)
        for n in range(len(qits)):
            if n + 1 < len(qits):
                q_a(n + 1)
            q_b(n)
        def fwo(e):
            for k in range(4):
                ins = dma(e, wos.t[:, k, :], wo_d.ap[k * 128:(k + 1) * 128, :], wos.ds)
            return ins
        P.op("pool", fwo, r=[wo_d.res], w=[wos.res], dsem=wos.ds, ndma=4)
        SC = 128 ** -0.5
        its = [(hd, h) for hd in range(4) for h in range(2)]

        def x_s(it):
            hd, h = its[it]
            pS = [pb[0], pb[1]] if it % 2 == 0 else [pb[4], pb[5]]
            for mt in range(2):
                pt = ptx[(it % 2) * 2 + mt]
                P.op("pe", lambda e, mt=mt, p_=pS[mt]: e.matmul(
                    p_.t[:], xk.t[:, hd, mt * 128:(mt + 1) * 128], xq.t[:, hd, HS[h]], start=True, stop=True),
                    r=[xk.res, xq.res], w=[pS[mt].res])
                P.op("act", lambda e, p_=pS[mt], pt=pt: e.activation(out=pt.t[:], in_=p_.t[:], func=AF.Exp,
                                                                  scale=SC), r=[pS[mt].res], w=[pt.res])

        def x_o(it):
            hd, h = its[it]
            pts = [ptx[(it % 2) * 2 + mt] for mt in range(2)]
            pO, pD = pb[2], pb[3]

            def mm_o(e):
                for mt in range(2):
                    ins = e.matmul(pO.t[:], xv.t[:, mt, hd, :], pts[mt].t[:], start=(mt == 0), stop=(mt == 1))
                return ins
            P.op("pe", mm_o, r=[xv.res, pts[0].res, pts[1].res], w=[pO.res])

            def mm_d(e):
                for mt in range(2):
                    ins = e.matmul(pD.t[:], onesb.t[:], pts[mt].t[:], start=(mt == 0), stop=(mt == 1))
                return ins
            P.op("pe", mm_d, r=[onesb.res, pts[0].res, pts[1].res], w=[pD.res])
            P.op("dve", lambda e: e.reciprocal(out=rden.t[:], in_=pD.t[:]), r=[pD.res], w=[rden.res])
            P.op("dve", lambda e: e.tensor_tensor(out=xo.t[:, hd, HS[h]], in0=pO.t[:], in1=rden.t[:],
                                                  op=ALU.mult), r=[pO.res, rden.res], ww=[xo.res])
        x_s(0)
        for it in range(8):
            if it + 1 < 8:
                x_s(it + 1)
            x_o(it)
        for dc in range(NC_):
            for h in range(2):
                ps = pb[4 + (dc * 2 + h) % 2]

                def mm(e, ps=ps, dc=dc, h=h):
                    for k in range(4):
                        ins = e.matmul(ps.t[:], wos.t[:, k, dc * 128:(dc + 1) * 128], xo.t[:, k, HS[h]],
                                       start=(k == 0), stop=(k == 3))
                    return ins
                P.op("pe", mm, r=[wos.res, xo.res], w=[ps.res])
                P.op("dve", lambda e, ps=ps, dc=dc, h=h: e.tensor_tensor(
                    out=xT.t[:, dc, HS[h]], in0=ps.t[:], in1=xT.t[:, dc, HS[h]], op=ALU.add),
                    r=[ps.res, xT_res[dc][h]], w=[xT_res[dc][h]])
        P.barrier()

    if stop_after >= 3:
        xattn()

    outr = Res("y")
    out_state = {}

    def emit_out(tiles, ost):
        ko = out_state.get("ko", 0)
        for i in tiles:
            o_ = ost[i % 2]
            for g4 in range(4):
                ps = pb[ko % 4]
                ko += 1

                def tr(e, ps=ps, g4=g4, i=i):
                    for q in range(4):
                        c = g4 * 4 + q
                        ins = e.transpose(out=ps.t[:, q * 128:(q + 1) * 128], in_=xT.t[:, c, i * 128:(i + 1) * 128],
                                          identity=ident.t[:])
                    return ins
                P.op("pe", tr, r=[xT_res[c][i // 4] for c in range(g4 * 4, g4 * 4 + 4)] + [ident.res], w=[ps.res])
                if ko % 2:
                    P.op("dve", lambda e, o_=o_, ps=ps, g4=g4: e.tensor_copy(out=o_.t[:, g4 * 512:(g4 + 1) * 512],
                                                                            in_=ps.t[:]), r=[ps.res], w=[o_.res])
                else:
                    P.op("act", lambda e, o_=o_, ps=ps, g4=g4: e.copy(out=o_.t[:, g4 * 512:(g4 + 1) * 512],
                                                                      in_=ps.t[:]), r=[ps.res], w=[o_.res])
            P.op("sp", lambda e, o_=o_, i=i: dma(e, y_d[i * 128:(i + 1) * 128, :], o_.t[:], o_.ds),
                 r=[o_.res], ww=[outr], dsem=o_.ds, ndma=1)
        out_state["ko"] = ko

    def ffn2_tail(half, ring):
        if "ost" not in out_state:
            ost = [sb("ostA", [128, D], F32, ring[0].off), sb("ostB", [128, D], F32, ring[1].off)]
            for s_ in ost:
                s_.ds = P.dsem(s_.res.name)
            out_state["ost"] = ost
        emit_out(range(half * 4, half * 4 + 4), out_state["ost"])

    if stop_after >= 4 and not SKIP_FFN:
        ffn(*w_ffn[1], 64, "b", tail=ffn2_tail)
    else:
        RA.reset()
        ost = [RA.alloc("ost%d" % i, [128, D], F32) for i in range(2)]
        for s_ in ost:
            s_.ds = P.dsem(s_.res.name)
        emit_out(range(8), ost)
    P.barrier()
    P.op("sp", lambda e: None, r=[outr])

    with nc.Block() as block:
        P.emit(block)
    return nc


_NC_CACHE = {}
_DEBUG_MAPS = None


def _consts(rank_in_group):
    p = rank_in_group
    c = {}
    c["ident"] = np.eye(128, dtype=np.float32)
    j = np.arange(128)[:, None]
    i = np.arange(128)[None, :]
    c["triN"] = np.where(j <= i, -1.0 / 16.0, 0.0).astype(np.float32)
    c["triS"] = np.where(j > i, -1.0 / 16.0, 0.0).astype(np.float32)
    c["attmask"] = (j <= i).astype(np.float32)
    c["triT"] = (j <= i).astype(np.float32)
    past = np.zeros((8, 16), np.float32)
    for t in range(8):
        jb = t // 2
        for col in range(12):
            past[t, col] = 1.0 if col < 4 * p else 0.0
        for jj in range(4):
            past[t, 12 + jj] = 1.0 if jj < jb else 0.0
    p01 = np.broadcast_to(past[None, :, None, :], (128, 8, 8, 16)).reshape(128, 1024)
    c["past01"] = np.ascontiguousarray(p01, dtype=np.float32)
    c["pastbias"] = np.ascontiguousarray((p01 - 1.0) * 1e30, dtype=np.float32)
    mj = np.array([1.0 if jj < p else 0.0 for jj in range(3)], np.float32)
    c["mj"] = np.ascontiguousarray(np.broadcast_to(mj[None, :], (128, 3)), dtype=np.float32)
    return c


def kernel(x, mem, ffn1_norm, ffn1_w_gate, ffn1_w_up, ffn1_w_down, mix_norm, w_in,
           gla_w_gate2, gla_b_gate2, gla_out_norm, moba_q_norm, moba_k_norm, w_out,
           xattn_norm, mem_norm, xattn_w_q, xattn_w_kv, xattn_w_o, xattn_q_norm,
           xattn_k_norm, ffn2_norm, ffn2_w_gate, ffn2_w_up, ffn2_w_down):
    f = lambda a: np.ascontiguousarray(np.asarray(a, dtype=np.float32))
    x = f(x)
    mem = f(mem)

    def gl(g):
        return f(g).reshape(16, 128).T
    gains = np.ascontiguousarray(np.concatenate(
        [gl(ffn1_norm[0]), gl(mix_norm[0]), gl(xattn_norm[0]), gl(mem_norm[0]), gl(ffn2_norm[0])], axis=1))
    sg = np.ascontiguousarray(np.stack([f(moba_q_norm[0]), f(moba_k_norm[0]), f(xattn_q_norm[0]),
                                        f(xattn_k_norm[0])], axis=1))
    gout = np.ascontiguousarray(np.broadcast_to(f(gla_out_norm[0])[None, :], (128, 256)))
    shared = {
        "f1g": f(ffn1_w_gate[0]), "f1u": f(ffn1_w_up[0]), "f1d": f(ffn1_w_down[0]),
        "f2g": f(ffn2_w_gate[0]), "f2u": f(ffn2_w_up[0]), "f2d": f(ffn2_w_down[0]),
        "w_in": f(w_in[0]), "w_out": f(w_out[0]), "xw_q": f(xattn_w_q[0]), "xw_kv": f(xattn_w_kv[0]),
        "xw_o": f(xattn_w_o[0]), "gains": gains, "sgains": sg, "gout_bc": gout,
        "wg2": f(gla_w_gate2[0]), "bg2": f(gla_b_gate2[0]).reshape(1, 512),
    }
    if SKIP_FFN:
        for k in ("f1g", "f1u", "f1d", "f2g", "f2u", "f2d"):
            shared[k] = np.zeros((128, 128), np.float32)
    in_maps = []
    WNAMES = ("f1g", "f1u", "f1d", "f2g", "f2u", "f2d", "w_in", "w_out", "xw_q", "xw_kv", "xw_o")
    for c in range(8):
        b, p = c // 4, c % 4
        m = dict(shared)
        m["x"] = np.ascontiguousarray(x[b, p * T:(p + 1) * T, :])
        m["mem"] = mem[b]
        m.update(_consts(p))
        in_maps.append(m)
    if _DEBUG_MAPS is not None:
        _DEBUG_MAPS.append(in_maps)
        return None
    key = (STOP_AFTER, SKIP_FFN)
    if key not in _NC_CACHE:
        _NC_CACHE[key] = build(STOP_AFTER)
    nc = _NC_CACHE[key]
    res = run_bass_kernel_spmd(nc, in_maps, core_ids=list(range(8)))
    out = np.empty((2, 4096, D), np.float32)
    for c in range(8):
        b, p = c // 4, c % 4
        out[b, p * T:(p + 1) * T, :] = np.asarray(res.results[c]["y"], dtype=np.float32)
    return out
```

```python
import math
import numpy as np
import ml_dtypes
import concourse.bass as bass
import concourse.mybir as mybir
from concourse.bass_utils import run_bass_kernel_spmd

F32 = mybir.dt.float32
BF16 = mybir.dt.bfloat16
AF = mybir.ActivationFunctionType
ALU = mybir.AluOpType
AX = mybir.AxisListType

D = 2048
DFF = 5632
T = 1024
NC_ = 16
EPS = 1e-6
WIN = 6160
O_GQ, O_GK, O_GV, O_GR, O_GLR, O_MQ, O_MK, O_MV = 0, 512, 1024, 2048, 3072, 3088, 4112, 5136
FKV = 8192 + 8 * 8 * 130
FST = 1064
STOP_AFTER = 99
SKIP_FFN = False


class _Stop(Exception):
    pass


class Res:
    __slots__ = ("name", "w", "r", "ws")

    def __init__(self, name):
        self.name = name
        self.w = None
        self.r = []
        self.ws = []


class DSem:
    def __init__(self, sem, inc=16):
        self.sem = sem
        self.n = 0
        self.last = None
        self.inc = inc


class Op:
    pass


class Buf:
    def __init__(self, t, name):
        self.t = t
        self.res = Res(name)


class Prog:
    ENG = ("pe", "act", "dve", "pool", "sp")

    def __init__(self, nc):
        self.nc = nc
        self.ops = []
        self.dsems = []

    def dsem(self, name, inc=16):
        d = DSem(self.nc.alloc_semaphore(name), inc)
        self.dsems.append(d)
        return d

    def op(self, eng, fn, r=(), w=(), dsem=None, ndma=0, ww=()):
        o = Op()
        o.eng, o.fn, o.r, o.w, o.dsem, o.ndma = eng, fn, list(r), list(w), dsem, ndma
        o.ww = list(ww)
        o.deps = []
        o.signal = False
        o.bar = False
        self.ops.append(o)
        return o

    def barrier(self, exclude=()):
        o = Op()
        o.bar = True
        o.exclude = [id(d) for d in exclude]
        self.ops.append(o)

    def emit(self, block):
        nc = self.nc
        last = {}
        pend = {e: None for e in self.ENG}
        for o in self.ops:
            if o.bar:
                deps = [x for x in last.values()]
                deps += [d.last for d in self.dsems if d.last is not None and id(d) not in o.exclude]
                for e in self.ENG:
                    pend[e] = list(deps) if pend[e] is None else pend[e] + deps
                continue
            deps = []
            for r in o.r:
                if r.w is not None:
                    deps.append(r.w)
                deps.extend(r.ws)
            for w in o.w:
                if w.w is not None:
                    deps.append(w.w)
                deps.extend(w.r)
                deps.extend(w.ws)
            for w in o.ww:
                if w.w is not None:
                    deps.append(w.w)
                deps.extend(w.r)
            if pend[o.eng] is not None:
                deps.extend(pend[o.eng])
                pend[o.eng] = None
            seen = set()
            for d in deps:
                if d is o or id(d) in seen:
                    continue
                seen.add(id(d))
                if d.dsem is None and d.eng == "pe" and o.eng == "pe":
                    continue
                o.deps.append(d)
                d.signal = True
            for r in o.r:
                r.r.append(o)
            for w in o.w:
                w.w = o
                w.r = []
                w.ws = []
            for w in o.ww:
                w.ws.append(o)
            if o.dsem is not None:
                o.dsem.last = o
            else:
                last[o.eng] = o
        cnt = {e: 0 for e in self.ENG}
        esem = {e: nc.alloc_semaphore("es_" + e) for e in self.ENG}
        for o in self.ops:
            if o.bar:
                continue
            if o.dsem is not None:
                o.dsem.n += o.ndma
                o.ticket = o.dsem.inc * o.dsem.n
                o.sem = o.dsem.sem
            elif o.signal:
                cnt[o.eng] += 1
                o.ticket = cnt[o.eng]
                o.sem = esem[o.eng]
        by = {e: [o for o in self.ops if not o.bar and o.eng == e] for e in self.ENG}

        def run(name, e):
            waited = {}
            for o in by[name]:
                for d in o.deps:
                    k = id(d.sem)
                    if waited.get(k, 0) < d.ticket:
                        e.wait_ge(d.sem, d.ticket)
                        waited[k] = d.ticket
                ins = o.fn(e)
                if o.dsem is None and o.signal and ins is not None:
                    ins.then_inc(esem[name], 1)

        @block.tensor
        def _(e):
            run("pe", e)

        @block.scalar
        def _(e):
            run("act", e)

        @block.vector
        def _(e):
            run("dve", e)

        @block.gpsimd
        def _(e):
            run("pool", e)

        @block.sync
        def _(e):
            run("sp", e)


def build(stop_after=99):
    nc = bass.Bass("TRN2", target_bir_lowering=False)
    P = Prog(nc)

    def din(name, shape, dt=F32):
        return nc.dram_tensor(name, list(shape), dt, kind="ExternalInput").ap()

    x_d = din("x", [T, D])
    mem_d = din("mem", [256, D])
    y_d = nc.dram_tensor("y", [T, D], F32, kind="ExternalOutput").ap()
    WS = []

    class WT:
        pass

    def wshard(name, rows, cols):
        w = WT()
        if SKIP_FFN and name[0] == "f":
            rows, cols = 128, 128
        w.ap = din(name, [rows, cols])
        w.res = Res(name)
        w.sres = Res(name + "_s")
        WS.append(w)
        return w
    w_ffn = [[wshard("f1g", D, DFF), wshard("f1u", D, DFF), wshard("f1d", DFF, D)]]
    win_d = wshard("w_in", D, WIN)
    wout_d = wshard("w_out", D, D)
    wkv_d = wshard("xw_kv", D, 1024)
    wq_d = wshard("xw_q", D, 512)
    wo_d = wshard("xw_o", 512, D)
    w_ffn.append([wshard("f2g", D, DFF), wshard("f2u", D, DFF), wshard("f2d", DFF, D)])
    gains_d = din("gains", [128, 80])
    sg_d = din("sgains", [128, 4])
    gout_d = din("gout_bc", [128, 256])
    wg2_d = din("wg2", [16, 512])
    bg2_d = din("bg2", [1, 512])
    ident_d = din("ident", [128, 128])
    triN_d = din("triN", [128, 128])
    triS_d = din("triS", [128, 128])
    attm_d = din("attmask", [128, 128])
    triT_d = din("triT", [128, 128])
    pastb_d = din("pastbias", [128, 1024])
    past01_d = din("past01", [128, 1024])
    mj_d = din("mj", [128, 3])

    ksrc = [nc.dram_tensor("ksrc%d" % i, [128, 4096], BF16) for i in range(2)]
    kdst = [nc.dram_tensor("kdst%d" % i, [512, 4096], BF16) for i in range(2)]
    vsrc = [nc.dram_tensor("vsrc%d" % i, [128, 2080], BF16) for i in range(4)]
    vdst = [nc.dram_tensor("vdst%d" % i, [512, 2080], BF16) for i in range(4)]
    st_src = nc.dram_tensor("st_src", [128, FST], F32)
    st_dst = nc.dram_tensor("st_dst", [512, FST], F32)
    qn_dram = nc.dram_tensor("qn_dram", [128, 8192], BF16)

    def sb(name, shape, dt, off):
        t = nc.alloc_sbuf_tensor_at(name, list(shape), dt, offset=off)
        return Buf(t, name)

    def nbytes(shape, dt):
        n = 1
        for s in shape[1:]:
            n *= s
        return n * (2 if dt == BF16 else 4)

    class Arena:
        def __init__(self, off, size):
            self.off, self.end, self.base = off, off + size, off

        def alloc(self, name, shape, dt):
            b = sb(name, shape, dt, self.off)
            b.off = self.off
            self.off += (nbytes(shape, dt) + 31) // 32 * 32
            assert self.off <= self.end, (name, self.off, self.end)
            return b

        def reset(self):
            self.off = self.base

    BASE = 16512
    XT_OFF, HT_OFF, R_OFF, S_OFF = BASE, BASE + 65536, BASE + 98304, BASE + 196608
    xT = sb("xT", [128, NC_, T], F32, XT_OFF)
    xT_res = [[Res("xT%d_%d" % (c, h)) for h in range(2)] for c in range(NC_)]
    hT = sb("hT", [128, NC_, T], BF16, HT_OFF)
    hT_res = [[Res("hT%d_%d" % (c, h)) for h in range(2)] for c in range(NC_)]
    SA = Arena(S_OFF, 229344 - S_OFF)
    ident = SA.alloc("ident", [128, 128], F32)
    ones = SA.alloc("ones", [128, 128], F32)
    identb = SA.alloc("identb", [128, 128], BF16)
    onesb = SA.alloc("onesb", [128, 128], BF16)
    gains = SA.alloc("gains", [128, 80], F32)
    sgains = SA.alloc("sgains", [128, 4], F32)
    sqb = [SA.alloc("sq%d" % i, [128, 512], F32) for i in range(2)]
    rstd = SA.alloc("rstd", [128, 512], F32)
    rstds = [rstd, SA.alloc("rstd_b", [128, 512], F32)]
    sgb = [SA.alloc("sg%d" % i, [128, 512], F32) for i in range(2)]
    small = SA.alloc("small", [128, 64], F32)
    RA = Arena(R_OFF, S_OFF - R_OFF)

    pb = []
    for i in range(7):
        pb.append(Buf(nc.alloc_psum_tensor("pb%d" % i, [128, 512], F32), "pb%d" % i))
    pbf = Buf(nc.alloc_psum_tensor("pbf", [128, 1024], BF16), "pbf")

    cds = P.dsem("const")
    HS = [slice(0, 512), slice(512, 1024)]

    def dma(eng, out, in_, ds):
        return eng.dma_start(out=out, in_=in_).then_inc(ds.sem, 16)

    def ld_consts(e):
        dma(e, ident.t[:], ident_d, cds)
        dma(e, gains.t[:], gains_d, cds)
        return dma(e, sgains.t[:], sg_d, cds)
    P.op("sp", ld_consts, w=[ident.res, gains.res, sgains.res], dsem=cds, ndma=3)
    cdsb = P.dsem("constb")
    P.op("pool", lambda e: dma(e, identb.t[:], ident_d, cdsb), w=[identb.res], dsem=cdsb, ndma=1)
    P.op("dve", lambda e: e.memset(ones.t[:], 1.0), w=[ones.res])
    P.op("dve", lambda e: e.memset(onesb.t[:], 1.0), w=[onesb.res])

    def transpose_in(src_d, ntile, dst, dst_res_fn, stg, tcols):
        k = 0
        for i in range(ntile):
            st = stg[i % 2]
            P.op("sp", lambda e, i=i, st=st: dma(e, st.t[:], src_d[i * 128:(i + 1) * 128, :], st.ds),
                 w=[st.res], dsem=st.ds, ndma=1)
            for g4 in range(4):
                ps = pb[k % 4]
                k += 1

                def tr(e, st=st, ps=ps, g4=g4):
                    for q in range(4):
                        c = g4 * 4 + q
                        ins = e.transpose(out=ps.t[:, q * 128:(q + 1) * 128], in_=st.t[:, c * 128:(c + 1) * 128],
                                          identity=ident.t[:])
                    return ins
                P.op("pe", tr, r=[st.res, ident.res], w=[ps.res])
                dst_ap = dst.t[:, g4 * 4:(g4 + 1) * 4, i * 128:(i + 1) * 128]
                src_ap = ps.t[:].rearrange("p (q t) -> p q t", q=4)
                wres = dst_res_fn(g4, i)
                if k % 2 == 0:
                    P.op("dve", lambda e, a=dst_ap, b=src_ap: e.tensor_copy(out=a, in_=b), r=[ps.res], w=wres)
                else:
                    P.op("act", lambda e, a=dst_ap, b=src_ap: e.copy(out=a, in_=b), r=[ps.res], w=wres)

    def stat_rstd(ps, dim):
        n = 512
        P.op("act", lambda e: e.activation(out=rstd.t[:, :n], in_=ps.t[:, :n], func=AF.Ln, bias=EPS,
                                           scale=1.0 / dim), r=[ps.res], w=[rstd.res])
        P.op("act", lambda e: e.activation(out=rstd.t[:, :n], in_=rstd.t[:, :n], func=AF.Exp, scale=-0.5),
             r=[rstd.res], w=[rstd.res])

    sqh = [sb("sqh%d" % i, [128, 512], BF16, sqb[i // 2].off + (i % 2) * 1024) for i in range(4)]

    def rms_norm_T(gcol, ncols=T):
        for h in range(2):
            ps = pb[6 - h]
            for c in range(NC_):
                sq = sqh[c % 4]
                if c % 2 == 0:
                    P.op("dve", lambda e, sq=sq, c=c, h=h: e.tensor_tensor(out=sq.t[:], in0=xT.t[:, c, HS[h]],
                                                                          in1=xT.t[:, c, HS[h]], op=ALU.mult),
                         r=[xT_res[c][h]], w=[sq.res])
                else:
                    P.op("act", lambda e, sq=sq, c=c, h=h: e.activation(out=sq.t[:], in_=xT.t[:, c, HS[h]],
                                                                       func=AF.Square),
                         r=[xT_res[c][h]], w=[sq.res])
                P.op("pe", lambda e, sq=sq, c=c, ps=ps: e.matmul(ps.t[:], onesb.t[:], sq.t[:], start=(c == 0),
                                                                stop=(c == NC_ - 1)),
                     r=[sq.res, onesb.res], w=[ps.res])
        for h in range(2):
            ps, rs = pb[6 - h], rstds[h]
            P.op("act", lambda e, ps=ps, rs=rs: e.activation(out=rs.t[:], in_=ps.t[:], func=AF.Ln, bias=EPS,
                                                             scale=1.0 / D), r=[ps.res], w=[rs.res])
            P.op("act", lambda e, rs=rs: e.activation(out=rs.t[:], in_=rs.t[:], func=AF.Exp, scale=-0.5),
                 r=[rs.res], w=[rs.res])
        for h in range(2):
            rs = rstds[h]
            for c in range(NC_):
                P.op("dve", lambda e, c=c, h=h, rs=rs: e.scalar_tensor_tensor(
                    out=hT.t[:, c, HS[h]], in0=xT.t[:, c, HS[h]], scalar=gains.t[:, gcol + c:gcol + c + 1],
                    in1=rs.t[:], op0=ALU.mult, op1=ALU.mult),
                    r=[xT_res[c][h], rs.res, gains.res], w=[hT_res[c][h]])

    hn_k = [0]

    def head_norm(src_ap, src_res, gidx, out_ap, out_res, n=512, extra_r=()):
        hn_k[0] += 1
        sq = sqb[hn_k[0] % 2]
        ps = pb[5 + hn_k[0] % 2]
        rstd = rstds[hn_k[0] % 2]
        P.op("dve", lambda e: e.tensor_tensor(out=sq.t[:, :n], in0=src_ap, in1=src_ap, op=ALU.mult),
             r=[src_res], w=[sq.res])
        P.op("pe", lambda e: e.matmul(ps.t[:, :n], ones.t[:], sq.t[:, :n], start=True, stop=True),
             r=[sq.res, ones.res], w=[ps.res])
        P.op("act", lambda e: e.activation(out=rstd.t[:, :n], in_=ps.t[:, :n], func=AF.Ln, bias=EPS,
                                           scale=1.0 / 128), r=[ps.res], w=[rstd.res])
        P.op("act", lambda e: e.activation(out=rstd.t[:, :n], in_=rstd.t[:, :n], func=AF.Exp, scale=-0.5),
             r=[rstd.res], w=[rstd.res])
        P.op("dve", lambda e: e.scalar_tensor_tensor(out=out_ap, in0=src_ap, scalar=sgains.t[:, gidx:gidx + 1],
                                                     in1=rstd.t[:, :n], op0=ALU.mult, op1=ALU.mult),
             r=[src_res, rstd.res, sgains.res] + list(extra_r), ww=[out_res])

    def fill_cols(slot, W, c0, n, dst0=0, parts=4, nchunk=NC_):
        step = nchunk // parts

        def f(e):
            for k in range(parts):
                ins = dma(e, slot.t[:, k * step:(k + 1) * step, dst0:dst0 + n],
                          W.ap[k * step * 128:(k + 1) * step * 128, c0:c0 + n].rearrange("(c p) n -> p c n", p=128),
                          slot.ds)
            return ins
        P.op("pool", f, r=[W.res], w=[slot.res], dsem=slot.ds, ndma=parts)

    stg = [RA.alloc("stg%d" % i, [128, D], F32) for i in range(2)]
    for s in stg:
        s.ds = P.dsem(s.res.name)
    transpose_in(x_d, 8, xT, lambda g4, i: [xT_res[c][i // 4] for c in range(g4 * 4, g4 * 4 + 4)], stg, T)
    P.barrier()
    RA.reset()

    def ffn(wg, wu, wd, gcol, tag, tail=None):
        RA.reset()
        ring = [RA.alloc("ring%s%d" % (tag, i), [128, 8192], BF16) for i in range(5)]
        for s in ring:
            s.ds = P.dsem(s.res.name)
        act = [RA.alloc("act%s%d" % (tag, i), [128, 4, T], BF16) for i in range(2)]
        act_res = [[[Res("act%d_%d_%d" % (b, j, h)) for h in range(2)] for j in range(4)] for b in range(2)]
        rms_norm_T(gcol)
        NG = DFF // 512
        fills = []
        for g in range(NG):
            fills.append(("g", g))
            fills.append(("u", g))
            if g >= 1:
                fills.append(("d", g - 1))
        fills.append(("d", NG - 1))
        slot_of = {}
        issued = [0]

        def issue_upto(n):
            while issued[0] < min(n, len(fills)):
                kind, g = fills[issued[0]]
                slot = ring[issued[0] % 5]
                slot_of[(kind, g)] = slot
                if kind == "d":
                    def f(e, slot=slot, g=g):
                        for k in range(4):
                            ins = dma(e, slot.t[:, k * 2048:(k + 1) * 2048],
                                      wd.ap[(g * 4 + k) * 128:(g * 4 + k + 1) * 128, :], slot.ds)
                        return ins
                    P.op("pool", f, r=[wd.res], w=[slot.res], dsem=slot.ds, ndma=4)
                else:
                    W = wg if kind == "g" else wu

                    def f(e, slot=slot, g=g, W=W):
                        for k in range(4):
                            ins = dma(e, slot.t[:, k * 2048:(k + 1) * 2048].rearrange("p (c n) -> p c n", c=4),
                                      W.ap[k * 512:(k + 1) * 512, g * 512:(g + 1) * 512].rearrange(
                                          "(c p) n -> p c n", p=128), slot.ds)
                        return ins
                    P.op("pool", f, r=[W.res], w=[slot.res], dsem=slot.ds, ndma=4)
                issued[0] += 1
        gu_bank = [pb[0], pb[1], pb[2], pb[3]]
        dn_bank = [pb[4], pb[5]]
        kk = [0, 0]

        def gate_up(g):
            G = slot_of[("g", g)]
            U = slot_of[("u", g)]
            ab = g % 2
            for j in range(4):
                for h in range(2):
                    pg = gu_bank[(kk[0] % 2) * 2]
                    pu = gu_bank[(kk[0] % 2) * 2 + 1]
                    kk[0] += 1
                    sg = sgb[kk[0] % 2]

                    def mm(e, S, ps, j=j, h=h):
                        for c in range(NC_):
                            ins = e.matmul(ps.t[:], S.t[:, c * 512 + j * 128:c * 512 + (j + 1) * 128],
                                           hT.t[:, c, HS[h]], start=(c == 0), stop=(c == NC_ - 1))
                        return ins
                    hr = [hT_res[c][h] for c in range(NC_)]
                    P.op("pe", lambda e, G=G, pg=pg, mm=mm: mm(e, G, pg), r=hr + [G.res], w=[pg.res])
                    P.op("pe", lambda e, U=U, pu=pu, mm=mm: mm(e, U, pu), r=hr + [U.res], w=[pu.res])
                    P.op("act", lambda e, sg=sg, pg=pg: e.activation(out=sg.t[:], in_=pg.t[:], func=AF.Silu),
                         r=[pg.res], w=[sg.res])
                    P.op("dve", lambda e, sg=sg, pu=pu, ab=ab, j=j, h=h: e.tensor_tensor(
                        out=act[ab].t[:, j, HS[h]], in0=sg.t[:], in1=pu.t[:], op=ALU.mult),
                        r=[sg.res, pu.res], w=[act_res[ab][j][h]])

        def down(g, half_major=False):
            Dn = slot_of[("d", g)]
            ab = g % 2
            order = ([(dc, h) for h in range(2) for dc in range(NC_)] if half_major
                     else [(dc, h) for dc in range(NC_) for h in range(2)])
            for n_, (dc, h) in enumerate(order):
                if half_major and n_ == NC_:
                    tail(0, ring)
                if True:
                    ps = dn_bank[kk[1] % 2]
                    kk[1] += 1

                    def mm(e, ps=ps, dc=dc, h=h, Dn=Dn, ab=ab):
                        for j in range(4):
                            ins = e.matmul(ps.t[:], Dn.t[:, j * 2048 + dc * 128:j * 2048 + (dc + 1) * 128],
                                           act[ab].t[:, j, HS[h]], start=(j == 0), stop=(j == 3))
                        return ins
                    P.op("pe", mm, r=[Dn.res] + [act_res[ab][j][h] for j in range(4)], w=[ps.res])
                    P.op("dve", lambda e, ps=ps, dc=dc, h=h: e.scalar_tensor_tensor(
                        out=xT.t[:, dc, HS[h]], in0=ps.t[:], scalar=0.5, in1=xT.t[:, dc, HS[h]],
                        op0=ALU.mult, op1=ALU.add), r=[ps.res, xT_res[dc][h]], w=[xT_res[dc][h]])
        nf = 0
        issue_upto(5)
        for g in range(NG):
            gate_up(g)
            nf += 2
            issue_upto(nf + 5)
            if g >= 1:
                down(g - 1)
                nf += 1
                issue_upto(nf + 5)
        if tail is None:
            down(NG - 1)
            P.barrier()
        else:
            down(NG - 1, half_major=True)
            tail(1, ring)

    if stop_after >= 1 and not SKIP_FFN:
        ffn(*w_ffn[0], 0, "a")

    def chk(v):
        if stop_after < v:
            raise _Stop()

    def mixer():
        RA.reset()
        wsl = [RA.alloc("wsl%d" % i, [128, NC_, 256], BF16) for i in range(3)]
        for s in wsl:
            s.ds = P.dsem(s.res.name)
        stash = []
        for h in range(4):
            stash.append(dict(
                qd=RA.alloc("qd%d" % h, [128, T], BF16), ki=RA.alloc("ki%d" % h, [128, T], BF16),
                kt=RA.alloc("kt%d" % h, [128, 8, 128], BF16), v=RA.alloc("v%d" % h, [128, 8, 256], BF16),
                sr=RA.alloc("sr%d" % h, [128, 8, 256], BF16)))
        TA = Arena(RA.off, RA.end - RA.off)
        rms_norm_T(16)
        wi = [0]

        def next_slot():
            s = wsl[wi[0] % 3]
            wi[0] += 1
            return s
        hr = [[hT_res[c][h] for c in range(NC_)] for h in range(2)]
        hr_all = hr[0] + hr[1]

        def projT(slot, col, ps, h, n=128):
            def mm(e):
                for c in range(NC_):
                    ins = e.matmul(ps.t[:n, :], slot.t[:, c, col:col + n], hT.t[:, c, HS[h]],
                                   start=(c == 0), stop=(c == NC_ - 1))
                return ins
            P.op("pe", mm, r=hr[h] + [slot.res], w=[ps.res])

        def projTok(slot, col, n, ps, pcol, i):
            def mm(e):
                for c in range(NC_):
                    ins = e.matmul(ps.t[:, pcol:pcol + n], hT.t[:, c, i * 128:(i + 1) * 128],
                                   slot.t[:, c, col:col + n], start=(c == 0), stop=(c == NC_ - 1))
                return ins
            P.op("pe", mm, r=hr[i // 4] + [slot.res], w=[ps.res])

        ststage = TA.alloc("ststage", [128, FST], F32)
        TA_mark = TA.off
        raws = [TA.alloc("raw%d" % i, [128, 512], F32) for i in range(2)]
        knfs = [TA.alloc("knf%d" % i, [128, 512], F32) for i in range(2)]
        knb = [TA.alloc("knb%d" % i, [128, 512], BF16) for i in range(2)]
        vst = [TA.alloc("vst%d" % i, [128, 2, 130], BF16) for i in range(2)]
        kmown = SA.alloc("kmown", [128, 8, 4], F32)
        for b in knb + vst:
            b.ds = P.dsem(b.res.name)
        for b in vst:
            P.op("dve", lambda e, b=b: e.memset(b.t[:, :, 128:129], 1.0), w=[b.res])
            P.op("dve", lambda e, b=b: e.memset(b.t[:, :, 129:130], 0.0), w=[b.res])
        kvs = Res("kv_src")
        sts = Res("st_src")
        qnr = Res("qn_dram")
        kqc = [0]

        def m1_qk(which, ocol, gidx):
            its_ = [(hp, hh, h) for hp in range(4) for hh in range(2) for h in range(2)]
            slots, stt = {}, {}

            def stage_a(n):
                hp, hh, h = its_[n]
                if hp not in slots:
                    slots[hp] = next_slot()
                    fill_cols(slots[hp], win_d, ocol + hp * 256, 256)
                kq = kqc[0]
                kqc[0] += 1
                ps, raw = pb[kq % 2], raws[kq % 2]
                projT(slots[hp], hh * 128, ps, h)
                P.op("act", lambda e: e.copy(out=raw.t[:], in_=ps.t[:]), r=[ps.res], w=[raw.res])
                stt[n] = (raw, knfs[kq % 2], knb[kq % 2])

            def stage_b(n):
                hp, hh, h = its_[n]
                hd = hp * 2 + hh
                raw, knf, ob = stt[n]
                if which == "q":
                    head_norm(raw.t[:], raw.res, gidx, ob.t[:], ob.res)
                    P.op("sp", lambda e: dma(
                        e, qn_dram.ap()[:, hd * 1024 + h * 512:hd * 1024 + (h + 1) * 512], ob.t[:], ob.ds),
                        r=[ob.res], ww=[qnr], dsem=ob.ds, ndma=1)
                else:
                    head_norm(raw.t[:], raw.res, gidx, knf.t[:], knf.res)
                    P.op("dve", lambda e: e.tensor_reduce(
                        out=kmown.t[:, hd, h * 2:h * 2 + 2], in_=knf.t[:].rearrange("p (b k) -> p b k", b=2),
                        axis=AX.X, op=ALU.add), r=[knf.res], ww=[kmown.res])
                    P.op("act", lambda e: e.copy(out=ob.t[:], in_=knf.t[:]), r=[knf.res], w=[ob.res])
                    P.op("sp", lambda e: dma(
                        e, ksrc[hd // 4].ap()[:, (hd % 4) * 1024 + h * 512:(hd % 4) * 1024 + (h + 1) * 512],
                        ob.t[:], ob.ds), r=[ob.res], ww=[kvs], dsem=ob.ds, ndma=1)
            stage_a(0)
            for n in range(len(its_)):
                if n + 1 < len(its_):
                    stage_a(n + 1)
                stage_b(n)
        m1_qk("k", O_MK, 1)
        kq = 0
        for hp in range(4):
            slot = next_slot()
            fill_cols(slot, win_d, O_MV + hp * 256, 256)
            for i in range(8):
                ps = pb[kq % 2]
                kq += 1
                projTok(slot, 0, 256, ps, 0, i)
                vb = vst[kq % 2]
                P.op("act", lambda e, ps=ps, vb=vb: e.copy(out=vb.t[:, :, 0:128],
                                                            in_=ps.t[:, 0:256].rearrange("p (a c) -> p a c", a=2)),
                     r=[ps.res], w=[vb.res])
                P.op("sp", lambda e, vb=vb, hp=hp, i=i: dma(
                    e, vsrc[hp].ap().rearrange("p (h i c) -> p h i c", h=2, i=8)[:, :, i, :],
                    vb.t[:], vb.ds), r=[vb.res], ww=[kvs], dsem=vb.ds, ndma=1)
        P.op("dve", lambda e: e.memset(ststage.t[:, 1024:FST], 0.0), w=[ststage.res])
        P.op("dve", lambda e: e.tensor_scalar(out=kmown.t[:], in0=kmown.t[:], scalar1=1.0 / 256, scalar2=None,
                                              op0=ALU.mult), r=[kmown.res], w=[kmown.res])
        P.op("dve", lambda e: e.tensor_copy(out=ststage.t[:, 1028:1060].rearrange("p (h b) -> p h b", h=8),
                                            in_=kmown.t[:]), r=[kmown.res], w=[ststage.res])
        pre_glr = next_slot()
        fill_cols(pre_glr, win_d, O_GLR, 16)
        pre_sr0 = next_slot()
        fill_cols(pre_sr0, win_d, O_GR, 256)
        P.barrier()
        kvd = Res("kv_dst")
        RG = [[0, 1, 2, 3], [4, 5, 6, 7]]
        for src_, dst_ in list(zip(ksrc, kdst)) + list(zip(vsrc, vdst)):
            cs_ = P.dsem("cc_" + src_.name, inc=1)
            P.op("pool", lambda e, src_=src_, dst_=dst_, cs_=cs_: e.collective_compute(
                "AllGather", ALU.bypass, replica_groups=RG, ins=[src_.ap().opt()],
                outs=[dst_.ap().opt()]).then_inc(cs_.sem), r=[kvs], ww=[kvd], dsem=cs_, ndma=1)
        if stop_after < 1.3:
            return
        TA.off = TA_mark
        glrT = TA.alloc("glrT", [16, T], F32)
        spb = TA.alloc("sp", [128, 512], F32)
        e12 = TA.alloc("e12", [128, 512], F32)
        e2b = TA.alloc("e2b", [128, 512], F32)
        e3 = sb("e3", [128, 512], F32, sqb[1].off)
        wg2 = sb("wg2", [16, 512], F32, sgb[0].off)
        bg2 = sb("bg2", [1, 512], F32, sgb[1].off)
        dcy = SA.alloc("dcy", [128, 4, 8], F32)
        triN = TA.alloc("triN", [128, 128], F32)
        triS = TA.alloc("triS", [128, 128], F32)

        def ld2(e):
            dma(e, wg2.t[:], wg2_d, cds2)
            dma(e, bg2.t[:], bg2_d, cds2)
            dma(e, triN.t[:], triN_d, cds2)
            return dma(e, triS.t[:], triS_d, cds2)
        cds2 = P.dsem("const2")
        P.op("sp", ld2, w=[wg2.res, bg2.res, triN.res, triS.res], dsem=cds2, ndma=4)
        slot = pre_glr
        for h in range(2):
            ps = pb[h]
            projT(slot, 0, ps, h, n=16)
            P.op("act", lambda e, ps=ps, h=h: e.copy(out=glrT.t[:, HS[h]], in_=ps.t[:16, :]), r=[ps.res],
                 w=[glrT.res])
        LNS = math.log(128 ** -0.5)
        chk(1.31)
        for hd in range(4):
            st = stash[hd]
            if hd == 0:
                sr_ = pre_sr0
            else:
                sr_ = next_slot()
                fill_cols(sr_, win_d, O_GR + hd * 256, 256)
            sqk = next_slot()
            fill_cols(sqk, win_d, O_GQ + hd * 128, 128, dst0=0)
            fill_cols(sqk, win_d, O_GK + hd * 128, 128, dst0=128)
            sv = next_slot()
            fill_cols(sv, win_d, O_GV + hd * 256, 256)
            for t2 in range(4):
                pr = pb[5] if t2 % 2 == 0 else pb[6]
                for a in range(2):
                    projTok(sr_, 0, 256, pr, a * 256, t2 * 2 + a)
                P.op("act", lambda e, st=st, pr=pr, t2=t2: e.activation(
                    out=st["sr"].t[:, t2 * 2:t2 * 2 + 2, :],
                    in_=pr.t[:].rearrange("p (a c) -> p a c", a=2), func=AF.Silu),
                    r=[pr.res], ww=[st["sr"].res])
            for h in range(2):
                pla, pbc, ptl, pq, pk = pb[0], pb[1], pb[2], pb[3], pb[4]

                def mm_la(e, h=h, hd=hd):
                    for q in range(4):
                        i = h * 4 + q
                        e.matmul(pla.t[:, q * 128:(q + 1) * 128], glrT.t[:, i * 128:(i + 1) * 128],
                                 wg2.t[:, hd * 128:(hd + 1) * 128], start=True, stop=False)
                        ins = e.matmul(pla.t[:, q * 128:(q + 1) * 128], ones.t[0:1, :],
                                       bg2.t[:, hd * 128:(hd + 1) * 128], start=False, stop=True)
                    return ins
                P.op("pe", mm_la, r=[glrT.res, wg2.res, bg2.res, ones.res], w=[pla.res])
                P.op("act", lambda e: e.activation(out=spb.t[:], in_=pla.t[:], func=AF.Exp, scale=-1.0),
                     r=[pla.res], w=[spb.res])
                P.op("act", lambda e: e.activation(out=spb.t[:], in_=spb.t[:], func=AF.Ln, bias=1.0),
                     r=[spb.res], w=[spb.res])

                chk(1.32)

                def mm_bc(e):
                    for q in range(4):
                        ins = e.matmul(pbc.t[:, q * 128:(q + 1) * 128], spb.t[:, q * 128:(q + 1) * 128], triN.t[:],
                                       start=True, stop=True)
                    return ins
                P.op("pe", mm_bc, r=[spb.res, triN.res], w=[pbc.res])

                def mm_tl(e):
                    for q in range(4):
                        ins = e.matmul(ptl.t[:, q * 128:(q + 1) * 128], triS.t[:], spb.t[:, q * 128:(q + 1) * 128],
                                       start=True, stop=True)
                    return ins
                P.op("pe", mm_tl, r=[spb.res, triS.res], w=[ptl.res])
                chk(1.321)
                projT(sqk, 0, pq, h)
                projT(sqk, 128, pk, h)
                chk(1.322)
                P.op("act", lambda e, hd=hd, h=h: e.activation(
                    out=dcy.t[:, hd, h * 4:(h + 1) * 4],
                    in_=pbc.t[:].rearrange("p (q t) -> p q t", q=4)[:, :, 127], func=AF.Exp),
                    r=[pbc.res], ww=[dcy.res])
                chk(1.323)
                P.op("act", lambda e: e.activation(out=e12.t[:], in_=pbc.t[:], func=AF.Exp, bias=LNS),
                     r=[pbc.res], w=[e12.res])
                P.op("dve", lambda e, st=st, h=h: e.tensor_tensor(out=st["qd"].t[:, HS[h]], in0=pq.t[:],
                                                                  in1=e12.t[:], op=ALU.mult),
                     r=[pq.res, e12.res], ww=[st["qd"].res])
                chk(1.324)
                P.op("act", lambda e: e.activation(out=e2b.t[:], in_=pbc.t[:], func=AF.Exp, scale=-1.0),
                     r=[pbc.res], w=[e2b.res])
                P.op("dve", lambda e, st=st, h=h: e.tensor_tensor(out=st["ki"].t[:, HS[h]], in0=pk.t[:],
                                                                  in1=e2b.t[:], op=ALU.mult),
                     r=[pk.res, e2b.res], ww=[st["ki"].res])
                P.op("act", lambda e: e.activation(out=e3.t[:], in_=ptl.t[:], func=AF.Exp), r=[ptl.res],
                     w=[e3.res])
                chk(1.33)
                for q in range(4):
                    i = h * 4 + q
                    pkk, pvv = pb[5], pb[6]
                    projTok(sqk, 128, 128, pkk, 0, i)
                    projTok(sv, 0, 256, pvv, 0, i)
                    P.op("dve", lambda e, st=st, i=i, q=q, pkk=pkk: e.tensor_tensor(
                        out=st["kt"].t[:, i, :], in0=pkk.t[:, 0:128], in1=e3.t[:, q * 128:(q + 1) * 128],
                        op=ALU.mult), r=[pkk.res, e3.res], ww=[st["kt"].res])
                    P.op("act", lambda e, st=st, i=i, pvv=pvv: e.copy(out=st["v"].t[:, i, :], in_=pvv.t[:, 0:256]),
                         r=[pvv.res], ww=[st["v"].res])
                chk(1.332)
            chk(1.34)
            S_ap = ststage.t[:, hd * 256:(hd + 1) * 256]
            P.op("dve", lambda e, S_ap=S_ap: e.memset(S_ap, 0.0), w=[ststage.res])
            P.op("dve", lambda e, hd=hd: e.memset(ststage.t[:, 1024 + hd:1025 + hd], 1.0), w=[ststage.res])
            for i in range(8):
                pu = pb[i % 2]
                P.op("pe", lambda e, st=st, i=i, pu=pu: e.matmul(pu.t[:, 0:256], st["kt"].t[:, i, :],
                                                                st["v"].t[:, i, :], start=True, stop=True),
                     r=[st["kt"].res, st["v"].res], w=[pu.res])
                P.op("dve", lambda e, S_ap=S_ap, hd=hd, i=i, pu=pu: e.scalar_tensor_tensor(
                    out=S_ap, in0=S_ap, scalar=dcy.t[:, hd, i:i + 1], in1=pu.t[:, 0:256], op0=ALU.mult,
                    op1=ALU.add), r=[pu.res, dcy.res, ststage.res], w=[ststage.res])
                P.op("dve", lambda e, hd=hd, i=i: e.tensor_tensor(
                    out=ststage.t[:, 1024 + hd:1025 + hd], in0=ststage.t[:, 1024 + hd:1025 + hd],
                    in1=dcy.t[:, hd, i:i + 1], op=ALU.mult), r=[dcy.res, ststage.res], w=[ststage.res])
        if stop_after < 1.5:
            P.barrier()
            return
        stds = P.dsem("stds")
        P.op("sp", lambda e: dma(e, st_src.ap(), ststage.t[:], stds), r=[ststage.res], w=[sts], dsem=stds, ndma=1)
        ccs = P.dsem("cc", inc=1)
        ccs2 = P.dsem("cc2", inc=1)
        std = Res("st_dst")

        P.op("pool", lambda e: e.collective_compute(
            "AllGather", ALU.bypass, replica_groups=RG, ins=[st_src.ap().opt()],
            outs=[st_dst.ap().opt()]).then_inc(ccs2.sem), r=[sts], w=[std], dsem=ccs2, ndma=1)
        P.barrier(exclude=[ccs2])
        m1_qk("q", O_MQ, 0)
        P.barrier()
        if stop_after < 1.7:
            return
        TB = Arena(TA_mark, TA.end - TA_mark)
        ldbs = [TB.alloc("ldb%d" % i, [128, FST], F32) for i in range(2)]
        for b_ in ldbs:
            b_.ds = P.dsem(b_.res.name)
        mjt = SA.alloc("mjt", [128, 3], F32)
        gout = TB.alloc("gout", [128, 256], F32)
        attmask = TB.alloc("attmask", [128, 128], F32)
        cds3 = P.dsem("const3")
        P.op("sp", lambda e: [dma(e, mjt.t[:], mj_d, cds3), dma(e, gout.t[:], gout_d, cds3),
                              dma(e, attmask.t[:], attm_d, cds3)][-1],
             w=[mjt.res, gout.res, attmask.res], dsem=cds3, ndma=3)
        dpr = SA.alloc("dpr", [128, 4], F32)
        ltmp = TB.alloc("ltmp", [128, 256], F32)
        P.op("dve", lambda e: e.memset(ststage.t[:, 0:1024], 0.0), w=[ststage.res])
        for j in range(3):
            ldb = ldbs[j % 2]
            P.op("sp", lambda e, j=j, ldb=ldb: dma(e, ldb.t[:], st_dst.ap()[j * 128:(j + 1) * 128, :], ldb.ds),
                 r=[std], w=[ldb.res], dsem=ldb.ds, ndma=1)
            P.op("dve", lambda e, j=j, ldb=ldb: e.tensor_scalar(out=dpr.t[:], in0=ldb.t[:, 1024:1028], scalar1=-1.0,
                                                       scalar2=mjt.t[:, j:j + 1], op0=ALU.add, op1=ALU.mult),
                 r=[ldb.res, mjt.res], w=[dpr.res])
            P.op("dve", lambda e: e.tensor_scalar(out=dpr.t[:], in0=dpr.t[:], scalar1=1.0, scalar2=None,
                                                  op0=ALU.add), r=[dpr.res], w=[dpr.res])
            for hd in range(4):
                P.op("dve", lambda e, j=j, hd=hd, ldb=ldb: e.tensor_scalar(
                    out=ltmp.t[:], in0=ldb.t[:, hd * 256:(hd + 1) * 256], scalar1=mjt.t[:, j:j + 1], scalar2=None,
                    op0=ALU.mult), r=[ldb.res, mjt.res], w=[ltmp.res])
                P.op("dve", lambda e, hd=hd: e.scalar_tensor_tensor(
                    out=ststage.t[:, hd * 256:(hd + 1) * 256], in0=ststage.t[:, hd * 256:(hd + 1) * 256],
                    scalar=dpr.t[:, hd:hd + 1], in1=ltmp.t[:], op0=ALU.mult, op1=ALU.add),
                    r=[ltmp.res, dpr.res, ststage.res], w=[ststage.res])
        mixv = nc.alloc_sbuf_tensor_at("mix", [128, 8, D], BF16, offset=HT_OFF)
        mix_res = Res("mix")
        HA = Arena(wsl[0].off, 3 * 8192)
        Sbf = [HA.alloc("Sbf%d" % h, [128, 256], BF16) for h in range(4)]
        attb = [HA.alloc("attb%d" % h, [128, 128], BF16) for h in range(4)]
        osq = [HA.alloc("osq%d" % h, [128, 256], F32) for h in range(4)]
        otmp = [HA.alloc("otmp%d" % h, [128, 256], F32) for h in range(4)]
        ocp = [HA.alloc("ocp%d" % h, [128, 256], F32) for h in range(4)]
        smh = [HA.alloc("smh%d" % h, [128, 8], F32) for h in range(4)]
        Sres = [Res("S%d" % h) for h in range(4)]
        pa_res = [pb[h].res for h in range(4)]
        po_res = [pb[h].res for h in range(4)]
        pu_res = [pb[4 + h % 3].res for h in range(4)]
        def hv(hd):
            return dict(st=stash[hd], S_ap=ststage.t[:, hd * 256:(hd + 1) * 256], pa=pb[hd].t[:, 0:128],
                        po=pb[hd].t[:, 256:512],
                        pu=pb[4 + hd % 3].t[:, (hd // 3) * 256:(hd // 3) * 256 + 256], sm=smh[hd])
        HV = [hv(hd) for hd in range(4)]
        for i in range(8):
            first_r = [ststage.res] if i == 0 else []
            for hd in range(4):
                v_ = HV[hd]
                st, S_ap, pa, pu = v_["st"], v_["S_ap"], v_["pa"], v_["pu"]
                P.op("act", lambda e, S_ap=S_ap, hd=hd: e.copy(out=Sbf[hd].t[:], in_=S_ap),
                     r=[Sres[hd]] + first_r, w=[Sbf[hd].res])
                P.op("pe", lambda e, st=st, i=i, pa=pa: e.matmul(
                    pa, st["ki"].t[:, i * 128:(i + 1) * 128], st["qd"].t[:, i * 128:(i + 1) * 128],
                    start=True, stop=True), r=[st["ki"].res, st["qd"].res], w=[pa_res[hd]])
            for hd in range(4):
                v_ = HV[hd]
                st, pu = v_["st"], v_["pu"]
                P.op("pe", lambda e, st=st, i=i, pu=pu: e.matmul(pu, st["kt"].t[:, i, :], st["v"].t[:, i, :],
                                                                start=True, stop=True),
                     r=[st["kt"].res, st["v"].res], w=[pu_res[hd]])
            for hd in range(4):
                pa = HV[hd]["pa"]
                P.op("dve", lambda e, pa=pa, hd=hd: e.tensor_tensor(out=attb[hd].t[:], in0=pa, in1=attmask.t[:],
                                                                   op=ALU.mult),
                     r=[pa_res[hd], attmask.res], w=[attb[hd].res])
            for hd in range(4):
                v_ = HV[hd]
                st, po, pu, S_ap = v_["st"], v_["po"], v_["pu"], v_["S_ap"]

                def mm_o(e, st=st, i=i, po=po, hd=hd):
                    e.matmul(po, st["qd"].t[:, i * 128:(i + 1) * 128], Sbf[hd].t[:], start=True, stop=False)
                    return e.matmul(po, attb[hd].t[:], st["v"].t[:, i, :], start=False, stop=True)
                P.op("pe", mm_o, r=[st["qd"].res, Sbf[hd].res, attb[hd].res, st["v"].res], w=[po_res[hd]])
                P.op("dve", lambda e, S_ap=S_ap, hd=hd, i=i, pu=pu: e.scalar_tensor_tensor(
                    out=S_ap, in0=S_ap, scalar=dcy.t[:, hd, i:i + 1], in1=pu, op0=ALU.mult,
                    op1=ALU.add), r=[pu_res[hd], dcy.res, Sres[hd], Sbf[hd].res] + first_r, w=[Sres[hd]])
            for hd in range(4):
                po = HV[hd]["po"]
                P.op("act", lambda e, po=po, hd=hd: e.copy(out=ocp[hd].t[:], in_=po), r=[po_res[hd]],
                     w=[ocp[hd].res])
            for hd in range(4):
                sm = HV[hd]["sm"]
                P.op("dve", lambda e, hd=hd: e.tensor_tensor(out=osq[hd].t[:], in0=ocp[hd].t[:], in1=ocp[hd].t[:],
                                                             op=ALU.mult), r=[ocp[hd].res], w=[osq[hd].res])
                P.op("dve", lambda e, hd=hd, sm=sm: e.tensor_reduce(out=sm.t[:, 0:1], in_=osq[hd].t[:], axis=AX.X,
                                                                   op=ALU.add), r=[osq[hd].res], w=[sm.res])
            for hd in range(4):
                sm = HV[hd]["sm"]
                P.op("act", lambda e, sm=sm: e.activation(out=sm.t[:, 1:2], in_=sm.t[:, 0:1], func=AF.Ln, bias=EPS,
                                                          scale=1.0 / 256), r=[sm.res], w=[sm.res])
                P.op("act", lambda e, sm=sm: e.activation(out=sm.t[:, 2:3], in_=sm.t[:, 1:2], func=AF.Exp,
                                                          scale=-0.5), r=[sm.res], w=[sm.res])
            for hd in range(4):
                v_ = HV[hd]
                st, sm = v_["st"], v_["sm"]
                P.op("dve", lambda e, hd=hd, sm=sm: e.scalar_tensor_tensor(
                    out=otmp[hd].t[:], in0=ocp[hd].t[:], scalar=sm.t[:, 2:3], in1=gout.t[:], op0=ALU.mult,
                    op1=ALU.mult), r=[ocp[hd].res, sm.res, gout.res], w=[otmp[hd].res])
                P.op("dve", lambda e, st=st, i=i, hd=hd: e.tensor_tensor(
                    out=mixv[:, i, hd * 256:(hd + 1) * 256], in0=otmp[hd].t[:], in1=st["sr"].t[:, i, :],
                    op=ALU.mult), r=[otmp[hd].res, st["sr"].res], ww=[mix_res])
        P.barrier()
        if stop_after < 1.85:
            return
        RA.reset()
        M = RA
        pastb = M.alloc("pastb", [128, 8, 8, 16], F32)
        past01 = M.alloc("past01", [128, 8, 8, 16], F32)
        sel = M.alloc("sel", [128, 8, 8, 16], F32)
        gm = M.alloc("gm", [128, 128], F32)
        kmT = M.alloc("kmT", [128, 8, 16], F32)
        kmTb = M.alloc("kmTb", [128, 8, 16], BF16)
        triT = M.alloc("triT", [128, 128], BF16)
        oacc = [M.alloc("oacc%d" % i, [128, 2, 130], F32) for i in range(2)]
        ptb = [M.alloc("ptb%d" % i, [128, 2, 256], BF16) for i in range(2)]
        qload = [M.alloc("qall%d" % i, [128, T], BF16) for i in range(2)]
        ld = []
        for i in range(2):
            d_ = dict(q=qload[i], ktp=M.alloc("ktp%d" % i, [128, 3, T], BF16), kto=M.alloc("kto%d" % i, [128, T], BF16),
                      vp=M.alloc("vp%d" % i, [128, 3, 8, 130], BF16), vo=M.alloc("vo%d" % i, [128, 8, 130], BF16),
                      res=Res("ld%d" % i), ds=P.dsem("ld%d" % i))
            ld.append(d_)

        def ld4(e):
            dma(e, pastb.t[:], pastb_d.rearrange("p (a b c) -> p a b c", a=8, b=8), cds4)
            dma(e, past01.t[:], past01_d.rearrange("p (a b c) -> p a b c", a=8, b=8), cds4)
            for r in range(3):
                dma(e, kmT.t[:, :, r * 4:(r + 1) * 4],
                    st_dst.ap()[r * 128:(r + 1) * 128, 1028:1060].rearrange("p (h b) -> p h b", h=8), cds4)
            return dma(e, kmT.t[:, :, 12:16], kmown.t[:], cds4)
        cds4 = P.dsem("const4")
        cds5 = P.dsem("const5")
        P.op("sp", ld4, r=[std, kmown.res], w=[pastb.res, past01.res, kmT.res], dsem=cds4, ndma=6)
        P.op("pool", lambda e: dma(e, triT.t[:], triT_d, cds5), w=[triT.res], dsem=cds5, ndma=1)
        P.op("dve", lambda e: e.tensor_copy(out=kmTb.t[:], in_=kmT.t[:]), r=[kmT.res], w=[kmTb.res])

        def load_head(hd):
            L = ld[hd % 2]

            def f(e):
                dma(e, L["q"].t[:], qn_dram.ap()[:, hd * 1024:(hd + 1) * 1024], L["ds"])
                kc = slice((hd % 4) * 1024, (hd % 4 + 1) * 1024)
                vc = slice((hd % 2) * 1040, (hd % 2 + 1) * 1040)
                dma(e, L["kto"].t[:], ksrc[hd // 4].ap()[:, kc], L["ds"])
                dma(e, L["vo"].t[:], vsrc[hd // 2].ap()[:, vc].rearrange(
                    "p (i c) -> p i c", i=8), L["ds"])
                for r in range(3):
                    dma(e, L["ktp"].t[:, r, :], kdst[hd // 4].ap()[r * 128:(r + 1) * 128, kc], L["ds"])
                    ins = dma(e, L["vp"].t[:, r, :, :],
                              vdst[hd // 2].ap()[r * 128:(r + 1) * 128, vc].rearrange(
                                  "p (i c) -> p i c", i=8), L["ds"])
                return ins
            P.op("sp", f, r=[kvd, kvs, qnr], w=[L["res"]], dsem=L["ds"], ndma=9)
        SC = 128 ** -0.5
        ptb = ptb + [M.alloc("ptb%d" % i, [128, 2, 256], BF16) for i in range(2, 5)]
        mx8 = M.alloc("mx8b", [128, 8, 8], F32)
        gm3 = gm.t[:].rearrange("p (i n) -> p i n", i=8)

        def gate_ops(hd):
            L = ld[hd % 2]
            pg = pb[6]

            def mmg(e):
                for i in range(8):
                    ins = e.matmul(pg.t[:, i * 16:(i + 1) * 16], L["q"].t[:, i * 128:(i + 1) * 128],
                                   kmTb.t[:, hd, :], start=True, stop=True)
                return ins
            P.op("pe", mmg, r=[L["res"], kmTb.res], w=[pg.res])
            P.op("dve", lambda e: e.tensor_tensor(
                out=gm3, in0=pg.t[:, 0:128].rearrange("p (i n) -> p i n", i=8), in1=pastb.t[:, :, hd, :],
                op=ALU.add), r=[pg.res, pastb.res], w=[gm.res])
            for i in range(8):
                P.op("dve", lambda e, i=i: e.max(out=mx8.t[:, i, :], in_=gm.t[:, i * 16:(i + 1) * 16]),
                     r=[gm.res], ww=[mx8.res])
            for i in range(8):
                P.op("dve", lambda e, i=i: e.scalar_tensor_tensor(
                    out=sel.t[:, i, hd, :], in0=gm.t[:, i * 16:(i + 1) * 16], scalar=mx8.t[:, i, 2:3],
                    in1=past01.t[:, i, hd, :], op0=ALU.is_ge, op1=ALU.mult),
                    r=[gm.res, mx8.res, past01.res], ww=[sel.res])

        tasks = []
        for hd in range(8):
            for jb in range(4):
                blocks = [("p", r, j, r * 4 + j) for r in range(3) for j in range(4)]
                blocks += [("o", 0, j, 12 + j) for j in range(jb)]
                blocks += [("d", 0, jb, -1)]
                for bi, (kind, r, j, col) in enumerate(blocks):
                    tasks.append(dict(hd=hd, jb=jb, kind=kind, r=r, j=j, col=col, first=(bi == 0),
                                      last=(bi == len(blocks) - 1)))
        SK, NS, NT, NU = 3, 3, 5, 3

        def emit_s(t):
            tk = tasks[t]
            L = ld[tk["hd"] % 2]
            psS, pt = pb[t % NS], ptb[t % NT]
            jb, kind, r, j = tk["jb"], tk["kind"], tk["r"], tk["j"]
            Q = L["q"].t[:, jb * 256:(jb + 1) * 256]
            KT = L["ktp"].t[:, r, j * 256:(j + 1) * 256] if kind == "p" else L["kto"].t[:, j * 256:(j + 1) * 256]

            def mm_s(e):
                for k in range(2):
                    ins = e.matmul(psS.t[:, k * 256:(k + 1) * 256], KT[:, k * 128:(k + 1) * 128], Q,
                                   start=True, stop=True)
                return ins
            P.op("pe", mm_s, r=[L["res"]], w=[psS.res])
            P.op("act", lambda e: e.activation(out=pt.t[:].rearrange("p a b -> p (a b)"), in_=psS.t[:],
                                               func=AF.Exp, scale=SC), r=[psS.res], w=[pt.res])
            if kind == "d":
                P.op("dve", lambda e: e.tensor_tensor(out=pt.t[:, 0, 0:128], in0=pt.t[:, 0, 0:128],
                                                      in1=triT.t[:], op=ALU.mult),
                     r=[pt.res, triT.res], w=[pt.res])
                P.op("dve", lambda e: e.tensor_tensor(out=pt.t[:, 1, 128:256], in0=pt.t[:, 1, 128:256],
                                                      in1=triT.t[:], op=ALU.mult),
                     r=[pt.res, triT.res], w=[pt.res])

        def emit_u(t):
            tk = tasks[t]
            hd, jb, kind, r, j, col = tk["hd"], tk["jb"], tk["kind"], tk["r"], tk["j"], tk["col"]
            L = ld[hd % 2]
            pt, psU = ptb[t % NT], pb[3 + t % NU]
            oa = oacc[jb % 2]
            if kind == "p":
                V = [L["vp"].t[:, r, j * 2 + k, :] for k in range(2)]
            else:
                V = [L["vo"].t[:, j * 2 + k, :] for k in range(2)]

            def mm_u(e):
                for a in range(2):
                    ks = [0] if (kind == "d" and a == 0) else [0, 1]
                    for n_, k in enumerate(ks):
                        ins = e.matmul(psU.t[:, a * 256:a * 256 + 130], pt.t[:, k, a * 128:(a + 1) * 128],
                                       V[k], start=(n_ == 0), stop=(n_ == len(ks) - 1))
                return ins
            P.op("pe", mm_u, r=[pt.res, L["res"]], w=[psU.res])
            for a in range(2):
                i = jb * 2 + a
                sc = 1.0 if kind == "d" else sel.t[:, i, hd, col:col + 1]
                if tk["first"]:
                    P.op("dve", lambda e, a=a, sc=sc: e.tensor_scalar(
                        out=oa.t[:, a, :], in0=psU.t[:, a * 256:a * 256 + 130], scalar1=sc, scalar2=None,
                        op0=ALU.mult), r=[psU.res, sel.res], w=[oa.res])
                else:
                    P.op("dve", lambda e, a=a, sc=sc: e.scalar_tensor_tensor(
                        out=oa.t[:, a, :], in0=psU.t[:, a * 256:a * 256 + 130], scalar=sc, in1=oa.t[:, a, :],
                        op0=ALU.mult, op1=ALU.add), r=[psU.res, sel.res, oa.res], w=[oa.res])
            if tk["last"]:
                for a in range(2):
                    i = jb * 2 + a
                    P.op("dve", lambda e, a=a: e.reciprocal(out=small.t[:, 8 + a:9 + a], in_=oa.t[:, a, 128:129]),
                         r=[oa.res], w=[small.res])
                    P.op("dve", lambda e, a=a, i=i: e.tensor_scalar(
                        out=mixv[:, i, 1024 + hd * 128:1024 + (hd + 1) * 128], in0=oa.t[:, a, 0:128],
                        scalar1=small.t[:, 8 + a:9 + a], scalar2=None, op0=ALU.mult),
                        r=[oa.res, small.res], ww=[mix_res])

        load_head(0)
        gate_ops(0)
        load_head(1)
        for t in range(len(tasks) + SK):
            if t < len(tasks):
                tk = tasks[t]
                if tk["jb"] == 2 and tk["first"] and tk["hd"] + 1 < 8:
                    gate_ops(tk["hd"] + 1)
                emit_s(t)
            u = t - SK
            if u >= 0:
                emit_u(u)
                tk = tasks[u]
                if tk["jb"] == 3 and tk["last"] and tk["hd"] + 2 < 8:
                    load_head(tk["hd"] + 2)
        P.barrier()
        if stop_after < 1.95:
            return
        RA.reset()
        mixT = RA.alloc("mixT", [128, NC_, T], BF16)
        mixT_res = Res("mixT")
        wsl2 = [RA.alloc("wso%d" % i, [128, NC_, 256], BF16) for i in range(3)]
        for s in wsl2:
            s.ds = P.dsem(s.res.name)
        k5 = 0
        for i in range(8):
            for g4 in range(4):
                def tr(e, i=i, g4=g4):
                    for q in range(4):
                        c = g4 * 4 + q
                        ins = e.transpose(out=pbf.t[:, q * 128:(q + 1) * 128], in_=mixv[:, i, c * 128:(c + 1) * 128],
                                          identity=identb.t[:])
                    return ins
                P.op("pe", tr, r=[mix_res, identb.res], w=[pbf.res])
                dst_ap = mixT.t[:, g4 * 4:(g4 + 1) * 4, i * 128:(i + 1) * 128]
                src_ap = pbf.t[:, 0:512].rearrange("p (q t) -> p q t", q=4)
                k5 += 1
                if k5 % 2:
                    P.op("dve", lambda e, a=dst_ap, b=src_ap: e.tensor_copy(out=a, in_=b), r=[pbf.res], ww=[mixT_res])
                else:
                    P.op("act", lambda e, a=dst_ap, b=src_ap: e.copy(out=a, in_=b), r=[pbf.res], ww=[mixT_res])
        for g in range(8):
            slot = wsl2[g % 3]
            fill_cols(slot, wout_d, g * 256, 256)
            for dd in range(2):
                dc = g * 2 + dd
                for h in range(2):
                    ps = pb[k5 % 2]
                    k5 += 1

                    def mm(e, slot=slot, dd=dd, h=h, ps=ps):
                        for c in range(NC_):
                            ins = e.matmul(ps.t[:], slot.t[:, c, dd * 128:(dd + 1) * 128], mixT.t[:, c, HS[h]],
                                           start=(c == 0), stop=(c == NC_ - 1))
                        return ins
                    P.op("pe", mm, r=[slot.res, mixT_res], w=[ps.res])
                    P.op("dve", lambda e, ps=ps, dc=dc, h=h: e.tensor_tensor(
                        out=xT.t[:, dc, HS[h]], in0=ps.t[:], in1=xT.t[:, dc, HS[h]], op=ALU.add),
                        r=[ps.res, xT_res[dc][h]], w=[xT_res[dc][h]])
        P.barrier()

    if stop_after >= 1.1:
        try:
            mixer()
        except _Stop:
            P.barrier()

    def xattn():
        RA.reset()
        A = RA
        memT = A.alloc("memT", [128, NC_, 256], F32)
        memn = A.alloc("memn", [128, NC_, 256], BF16)
        xk = A.alloc("xk", [128, 4, 256], BF16)
        xv = A.alloc("xv", [128, 2, 4, 128], BF16)
        xq = A.alloc("xq", [128, 4, T], BF16)
        xo = A.alloc("xo", [128, 4, T], BF16)
        wsl = [A.alloc("wsx%d" % i, [128, NC_, 256], BF16) for i in range(3)]
        wos = A.alloc("wos", [128, 4, D], BF16)
        rawsx = [A.alloc("rawx%d" % i, [128, 512], F32) for i in range(2)]
        rden = A.alloc("rden", [128, 512], F32)
        ptx = [A.alloc("ptx%d" % i, [128, 512], BF16) for i in range(4)]
        mstg = [sb("mstg0", [128, D], F32, xq.off), sb("mstg1", [128, D], F32, xo.off)]
        for s in wsl + [wos] + mstg:
            s.ds = P.dsem(s.res.name)
        transpose_in(mem_d, 2, memT, lambda g4, i: [memT.res], mstg, 256)
        ps = pb[6]
        for c in range(NC_):
            sq = sqb[c % 2]
            P.op("dve", lambda e, sq=sq, c=c: e.tensor_tensor(out=sq.t[:, :256], in0=memT.t[:, c, :],
                                                              in1=memT.t[:, c, :], op=ALU.mult),
                 r=[memT.res], w=[sq.res])
            P.op("pe", lambda e, sq=sq, c=c: e.matmul(ps.t[:, :256], ones.t[:], sq.t[:, :256], start=(c == 0),
                                                      stop=(c == NC_ - 1)), r=[sq.res, ones.res], w=[ps.res])
        P.op("act", lambda e: e.activation(out=rstd.t[:, :256], in_=ps.t[:, :256], func=AF.Ln, bias=EPS,
                                           scale=1.0 / D), r=[ps.res], w=[rstd.res])
        P.op("act", lambda e: e.activation(out=rstd.t[:, :256], in_=rstd.t[:, :256], func=AF.Exp, scale=-0.5),
             r=[rstd.res], w=[rstd.res])
        for c in range(NC_):
            P.op("dve", lambda e, c=c: e.scalar_tensor_tensor(
                out=memn.t[:, c, :], in0=memT.t[:, c, :], scalar=gains.t[:, 48 + c:49 + c], in1=rstd.t[:, :256],
                op0=ALU.mult, op1=ALU.mult), r=[memT.res, rstd.res, gains.res], ww=[memn.res])
        kx = 0
        for hp in range(2):
            slot = wsl[kx % 3]
            kx += 1
            fill_cols(slot, wkv_d, hp * 256, 256)
            for hh in range(2):
                hd = hp * 2 + hh
                pk = pb[hh]

                def mm(e, slot=slot, hh=hh, pk=pk):
                    for c in range(NC_):
                        ins = e.matmul(pk.t[:, :256], slot.t[:, c, hh * 128:(hh + 1) * 128], memn.t[:, c, :],
                                       start=(c == 0), stop=(c == NC_ - 1))
                    return ins
                P.op("pe", mm, r=[slot.res, memn.res], w=[pk.res])
                raw = rawsx[hh]
                P.op("act", lambda e, pk=pk, raw=raw: e.copy(out=raw.t[:, :256], in_=pk.t[:, :256]), r=[pk.res],
                     w=[raw.res])
                head_norm(raw.t[:, :256], raw.res, 3, xk.t[:, hd, :], xk.res, n=256)
        for hp in range(2):
            slot = wsl[kx % 3]
            kx += 1
            fill_cols(slot, wkv_d, 512 + hp * 256, 256)
            for mt in range(2):
                pv = pb[2 + mt]

                def mm(e, slot=slot, mt=mt, pv=pv):
                    for c in range(NC_):
                        ins = e.matmul(pv.t[:, :256], memn.t[:, c, mt * 128:(mt + 1) * 128], slot.t[:, c, :],
                                       start=(c == 0), stop=(c == NC_ - 1))
                    return ins
                P.op("pe", mm, r=[slot.res, memn.res], w=[pv.res])
                P.op("act", lambda e, pv=pv, mt=mt, hp=hp: e.copy(
                    out=xv.t[:, mt, hp * 2:hp * 2 + 2, :], in_=pv.t[:, :256].rearrange("p (a c) -> p a c", a=2)),
                    r=[pv.res], ww=[xv.res])
        P.barrier()
        rms_norm_T(32)
        hr = [[hT_res[c][h] for c in range(NC_)] for h in range(2)]
        qits = [(hp, hh, h) for hp in range(2) for hh in range(2) for h in range(2)]
        qslots, qst = {}, {}

        def q_a(n):
            hp, hh, h = qits[n]
            if hp not in qslots:
                qslots[hp] = wsl[(kx + hp) % 3]
                fill_cols(qslots[hp], wq_d, hp * 256, 256)
            slot = qslots[hp]
            pq, raw = pb[n % 2], rawsx[n % 2]

            def mm(e):
                for c in range(NC_):
                    ins = e.matmul(pq.t[:], slot.t[:, c, hh * 128:(hh + 1) * 128], hT.t[:, c, HS[h]],
                                   start=(c == 0), stop=(c == NC_ - 1))
                return ins
            P.op("pe", mm, r=[slot.res] + hr[h], w=[pq.res])
            P.op("act", lambda e: e.copy(out=raw.t[:], in_=pq.t[:]), r=[pq.res], w=[raw.res])
            qst[n] = raw

        def q_b(n):
            hp, hh, h = qits[n]
            raw = qst[n]
            head_norm(raw.t[:], raw.res, 2, xq.t[:, hp * 2 + hh, HS[h]], xq.res)
        q_a(0)
        for n in range(len(qits)):
            if n + 1 < len(qits):
                q_a(n + 1)
            q_b(n)
        def fwo(e):
            for k in range(4):
                ins = dma(e, wos.t[:, k, :], wo_d.ap[k * 128:(k + 1) * 128, :], wos.ds)
            return ins
        P.op("pool", fwo, r=[wo_d.res], w=[wos.res], dsem=wos.ds, ndma=4)
        SC = 128 ** -0.5
        its = [(hd, h) for hd in range(4) for h in range(2)]

        def x_s(it):
            hd, h = its[it]
            pS = [pb[0], pb[1]] if it % 2 == 0 else [pb[4], pb[5]]
            for mt in range(2):
                pt = ptx[(it % 2) * 2 + mt]
                P.op("pe", lambda e, mt=mt, p_=pS[mt]: e.matmul(
                    p_.t[:], xk.t[:, hd, mt * 128:(mt + 1) * 128], xq.t[:, hd, HS[h]], start=True, stop=True),
                    r=[xk.res, xq.res], w=[pS[mt].res])
                P.op("act", lambda e, p_=pS[mt], pt=pt: e.activation(out=pt.t[:], in_=p_.t[:], func=AF.Exp,
                                                                  scale=SC), r=[pS[mt].res], w=[pt.res])

        def x_o(it):
            hd, h = its[it]
            pts = [ptx[(it % 2) * 2 + mt] for mt in range(2)]
            pO, pD = pb[2], pb[3]

            def mm_o(e):
                for mt in range(2):
                    ins = e.matmul(pO.t[:], xv.t[:, mt, hd, :], pts[mt].t[:], start=(mt == 0), stop=(mt == 1))
                return ins
            P.op("pe", mm_o, r=[xv.res, pts[0].res, pts[1].res], w=[pO.res])

            def mm_d(e):
                for mt in range(2):
                    ins = e.matmul(pD.t[:], onesb.t[:], pts[mt].t[:], start=(mt == 0), stop=(mt == 1))
                return ins
            P.op("pe", mm_d, r=[onesb.res, pts[0].res, pts[1].res], w=[pD.res])
            P.op("dve", lambda e: e.reciprocal(out=rden.t[:], in_=pD.t[:]), r=[pD.res], w=[rden.res])
            P.op("dve", lambda e: e.tensor_tensor(out=xo.t[:, hd, HS[h]], in0=pO.t[:], in1=rden.t[:],
                                                  op=ALU.mult), r=[pO.res, rden.res], ww=[xo.res])
        x_s(0)
        for it in range(8):
            if it + 1 < 8:
                x_s(it + 1)
            x_o(it)
        for dc in range(NC_):
            for h in range(2):
                ps = pb[4 + (dc * 2 + h) % 2]

                def mm(e, ps=ps, dc=dc, h=h):
                    for k in range(4):
                        ins = e.matmul(ps.t[:], wos.t[:, k, dc * 128:(dc + 1) * 128], xo.t[:, k, HS[h]],
                                       start=(k == 0), stop=(k == 3))
                    return ins
                P.op("pe", mm, r=[wos.res, xo.res], w=[ps.res])
                P.op("dve", lambda e, ps=ps, dc=dc, h=h: e.tensor_tensor(
                    out=xT.t[:, dc, HS[h]], in0=ps.t[:], in1=xT.t[:, dc, HS[h]], op=ALU.add),
                    r=[ps.res, xT_res[dc][h]], w=[xT_res[dc][h]])
        P.barrier()

    if stop_after >= 3:
        xattn()

    outr = Res("y")
    out_state = {}

    def emit_out(tiles, ost):
        ko = out_state.get("ko", 0)
        for i in tiles:
            o_ = ost[i % 2]
            for g4 in range(4):
                ps = pb[ko % 4]
                ko += 1

                def tr(e, ps=ps, g4=g4, i=i):
                    for q in range(4):
                        c = g4 * 4 + q
                        ins = e.transpose(out=ps.t[:, q * 128:(q + 1) * 128], in_=xT.t[:, c, i * 128:(i + 1) * 128],
                                          identity=ident.t[:])
                    return ins
                P.op("pe", tr, r=[xT_res[c][i // 4] for c in range(g4 * 4, g4 * 4 + 4)] + [ident.res], w=[ps.res])
                if ko % 2:
                    P.op("dve", lambda e, o_=o_, ps=ps, g4=g4: e.tensor_copy(out=o_.t[:, g4 * 512:(g4 + 1) * 512],
                                                                            in_=ps.t[:]), r=[ps.res], w=[o_.res])
                else:
                    P.op("act", lambda e, o_=o_, ps=ps, g4=g4: e.copy(out=o_.t[:, g4 * 512:(g4 + 1) * 512],
                                                                      in_=ps.t[:]), r=[ps.res], w=[o_.res])
            P.op("sp", lambda e, o_=o_, i=i: dma(e, y_d[i * 128:(i + 1) * 128, :], o_.t[:], o_.ds),
                 r=[o_.res], ww=[outr], dsem=o_.ds, ndma=1)
        out_state["ko"] = ko

    def ffn2_tail(half, ring):
        if "ost" not in out_state:
            ost = [sb("ostA", [128, D], F32, ring[0].off), sb("ostB", [128, D], F32, ring[1].off)]
            for s_ in ost:
                s_.ds = P.dsem(s_.res.name)
            out_state["ost"] = ost
        emit_out(range(half * 4, half * 4 + 4), out_state["ost"])

    if stop_after >= 4 and not SKIP_FFN:
        ffn(*w_ffn[1], 64, "b", tail=ffn2_tail)
    else:
        RA.reset()
        ost = [RA.alloc("ost%d" % i, [128, D], F32) for i in range(2)]
        for s_ in ost:
            s_.ds = P.dsem(s_.res.name)
        emit_out(range(8), ost)
    P.barrier()
    P.op("sp", lambda e: None, r=[outr])

    with nc.Block() as block:
        P.emit(block)
    return nc


_NC_CACHE = {}
_DEBUG_MAPS = None


def _consts(rank_in_group):
    p = rank_in_group
    c = {}
    c["ident"] = np.eye(128, dtype=np.float32)
    j = np.arange(128)[:, None]
    i = np.arange(128)[None, :]
    c["triN"] = np.where(j <= i, -1.0 / 16.0, 0.0).astype(np.float32)
    c["triS"] = np.where(j > i, -1.0 / 16.0, 0.0).astype(np.float32)
    c["attmask"] = (j <= i).astype(np.float32)
    c["triT"] = (j <= i).astype(np.float32)
    past = np.zeros((8, 16), np.float32)
    for t in range(8):
        jb = t // 2
        for col in range(12):
            past[t, col] = 1.0 if col < 4 * p else 0.0
        for jj in range(4):
            past[t, 12 + jj] = 1.0 if jj < jb else 0.0
    p01 = np.broadcast_to(past[None, :, None, :], (128, 8, 8, 16)).reshape(128, 1024)
    c["past01"] = np.ascontiguousarray(p01, dtype=np.float32)
    c["pastbias"] = np.ascontiguousarray((p01 - 1.0) * 1e30, dtype=np.float32)
    mj = np.array([1.0 if jj < p else 0.0 for jj in range(3)], np.float32)
    c["mj"] = np.ascontiguousarray(np.broadcast_to(mj[None, :], (128, 3)), dtype=np.float32)
    return c


def kernel(x, mem, ffn1_norm, ffn1_w_gate, ffn1_w_up, ffn1_w_down, mix_norm, w_in,
           gla_w_gate2, gla_b_gate2, gla_out_norm, moba_q_norm, moba_k_norm, w_out,
           xattn_norm, mem_norm, xattn_w_q, xattn_w_kv, xattn_w_o, xattn_q_norm,
           xattn_k_norm, ffn2_norm, ffn2_w_gate, ffn2_w_up, ffn2_w_down):
    f = lambda a: np.ascontiguousarray(np.asarray(a, dtype=np.float32))
    x = f(x)
    mem = f(mem)

    def gl(g):
        return f(g).reshape(16, 128).T
    gains = np.ascontiguousarray(np.concatenate(
        [gl(ffn1_norm[0]), gl(mix_norm[0]), gl(xattn_norm[0]), gl(mem_norm[0]), gl(ffn2_norm[0])], axis=1))
    sg = np.ascontiguousarray(np.stack([f(moba_q_norm[0]), f(moba_k_norm[0]), f(xattn_q_norm[0]),
                                        f(xattn_k_norm[0])], axis=1))
    gout = np.ascontiguousarray(np.broadcast_to(f(gla_out_norm[0])[None, :], (128, 256)))
    shared = {
        "f1g": f(ffn1_w_gate[0]), "f1u": f(ffn1_w_up[0]), "f1d": f(ffn1_w_down[0]),
        "f2g": f(ffn2_w_gate[0]), "f2u": f(ffn2_w_up[0]), "f2d": f(ffn2_w_down[0]),
        "w_in": f(w_in[0]), "w_out": f(w_out[0]), "xw_q": f(xattn_w_q[0]), "xw_kv": f(xattn_w_kv[0]),
        "xw_o": f(xattn_w_o[0]), "gains": gains, "sgains": sg, "gout_bc": gout,
        "wg2": f(gla_w_gate2[0]), "bg2": f(gla_b_gate2[0]).reshape(1, 512),
    }
    if SKIP_FFN:
        for k in ("f1g", "f1u", "f1d", "f2g", "f2u", "f2d"):
            shared[k] = np.zeros((128, 128), np.float32)
    in_maps = []
    WNAMES = ("f1g", "f1u", "f1d", "f2g", "f2u", "f2d", "w_in", "w_out", "xw_q", "xw_kv", "xw_o")
    for c in range(8):
        b, p = c // 4, c % 4
        m = dict(shared)
        m["x"] = np.ascontiguousarray(x[b, p * T:(p + 1) * T, :])
        m["mem"] = mem[b]
        m.update(_consts(p))
        in_maps.append(m)
    if _DEBUG_MAPS is not None:
        _DEBUG_MAPS.append(in_maps)
        return None
    key = (STOP_AFTER, SKIP_FFN)
    if key not in _NC_CACHE:
        _NC_CACHE[key] = build(STOP_AFTER)
    nc = _NC_CACHE[key]
    res = run_bass_kernel_spmd(nc, in_maps, core_ids=list(range(8)))
    out = np.empty((2, 4096, D), np.float32)
    for c in range(8):
        b, p = c // 4, c % 4
        out[b, p * T:(p + 1) * T, :] = np.asarray(res.results[c]["y"], dtype=np.float32)
    return out
```
